# Optimizing a Trainium2 kernel written in Bass

```python
import math
import jax
import jax.numpy as jnp
from jax import lax
import numpy as np

D_MODEL = 1024
BATCH = 32
SEQ = 2048
DEPTH = 4

CONV_WIDTH = 256
CONV_GROUPS = 4
DW_CONV_LEN = 31
GDN_HEADS = 4
GDN_HEAD_DIM = 64
GDN_WIDTH = GDN_HEADS * GDN_HEAD_DIM
GDN_SHORT_CONV = 4
GDN_CHUNK = 64
DIFF_HEADS = 4
DIFF_QK_DIM = 64
DIFF_V_DIM = 2 * DIFF_QK_DIM
DIFF_WIDTH = DIFF_HEADS * DIFF_V_DIM
MIX_WIDTH = CONV_WIDTH + GDN_WIDTH + DIFF_WIDTH
ATTN_BLOCK = 128
D_FF = -(-(8 * D_MODEL) // (3 * 256)) * 256
RMS_EPS = 1e-6
LN_EPS = 1e-5

IN_CONV = 2 * CONV_WIDTH
IN_GDN_QKV = 3 * GDN_WIDTH
IN_GDN_Z = GDN_WIDTH
IN_GDN_AB = 2 * GDN_HEADS
IN_DIFF_QK = DIFF_HEADS * 2 * DIFF_QK_DIM
IN_DIFF_V = DIFF_WIDTH
SPLIT_1 = IN_CONV
SPLIT_2 = SPLIT_1 + IN_GDN_QKV
SPLIT_3 = SPLIT_2 + IN_GDN_Z
SPLIT_4 = SPLIT_3 + IN_GDN_AB
SPLIT_5 = SPLIT_4 + IN_DIFF_QK
SPLIT_6 = SPLIT_5 + IN_DIFF_QK
IN_WIDTH = SPLIT_6 + IN_DIFF_V

kernel_name = 'hymba_style_conv_gdn_diffattn_block'


def rms_norm(x, g):
    xf = x.astype(jnp.float32)
    y = xf * lax.rsqrt(jnp.mean(xf * xf, axis=-1, keepdims=True) + RMS_EPS)
    return (y * g.astype(jnp.float32)).astype(x.dtype)


def layer_norm(x, g, b):
    xf = x.astype(jnp.float32)
    mu = jnp.mean(xf, axis=-1, keepdims=True)
    xc = xf - mu
    var = jnp.mean(xc * xc, axis=-1, keepdims=True)
    y = xc * lax.rsqrt(var + LN_EPS) * g.astype(jnp.float32) + b.astype(jnp.float32)
    return y.astype(x.dtype)


def l2_normalize(x):
    return x * lax.rsqrt(jnp.sum(x * x, axis=-1, keepdims=True) + 1e-6)


def causal_depthwise_conv(x, w):
    k_len, ch = w.shape
    xp = jnp.pad(x, ((0, 0), (k_len - 1, 0), (0, 0)))
    return lax.conv_general_dilated(
        xp, w[:, None, :].astype(x.dtype), window_strides=(1,), padding='VALID',
        dimension_numbers=('NWC', 'WIO', 'NWC'), feature_group_count=ch)


def alibi_slopes(n):
    start = 2.0 ** (-8.0 / n)
    return jnp.asarray(np.array([start ** (i + 1) for i in range(n)], dtype=np.float32))


def conformer_conv_mixer(u, w_dw, b_dw, ln_g, ln_b):
    val, gate = jnp.split(u, 2, axis=-1)
    h = val * jax.nn.sigmoid(gate)
    h = causal_depthwise_conv(h, w_dw) + b_dw.astype(h.dtype)
    h = layer_norm(h, ln_g, ln_b)
    return jax.nn.silu(h)


def gated_delta_chunked(q, k, v, g, beta):
    bsz, seq, nh, dk = q.shape
    dv = v.shape[-1]
    c = GDN_CHUNK
    n = seq // c
    q = q * (dk ** -0.5)

    def chunks(t):
        return t.reshape(bsz, n, c, nh, -1).transpose(0, 3, 1, 2, 4)

    q, k, v = chunks(q), chunks(k), chunks(v)
    g = jnp.cumsum(g.reshape(bsz, n, c, nh).transpose(0, 3, 1, 2), axis=-1)
    beta = beta.reshape(bsz, n, c, nh).transpose(0, 3, 1, 2)
    idx = jnp.arange(c)
    causal = idx[:, None] >= idx[None, :]
    strict = idx[:, None] > idx[None, :]
    gdiff = g[..., :, None] - g[..., None, :]
    decay = jnp.where(causal, jnp.exp(jnp.where(causal, gdiff, 0.0)), 0.0)
    kk = jnp.einsum('bhncd,bhnsd->bhncs', k, k)
    lower = jnp.where(strict, beta[..., None] * kk * decay, 0.0)
    a_mat = lower + jnp.eye(c, dtype=jnp.float32)
    rhs = jnp.concatenate([v * beta[..., None], k * (beta * jnp.exp(g))[..., None]], axis=-1)
    sol = lax.linalg.triangular_solve(a_mat, rhs, left_side=True, lower=True, unit_diagonal=True)
    u_c, w_c = sol[..., :dv], sol[..., dv:]
    qk = jnp.einsum('bhncd,bhnsd->bhncs', q, k) * decay

    def step(state, inp):
        q_i, k_i, u_i, w_i, qk_i, g_i = inp
        v_new = u_i - jnp.einsum('bhcd,bhde->bhce', w_i, state)
        o_i = (jnp.einsum('bhcd,bhde->bhce', q_i * jnp.exp(g_i)[..., None], state)
               + jnp.einsum('bhcs,bhse->bhce', qk_i, v_new))
        g_last = g_i[..., -1:]
        state = (state * jnp.exp(g_last)[..., None]
                 + jnp.einsum('bhcd,bhce->bhde', k_i * jnp.exp(g_last - g_i)[..., None], v_new))
        return state, o_i

    xs = tuple(jnp.moveaxis(t, 2, 0) for t in (q, k, u_c, w_c, qk, g))
    state0 = jnp.zeros((bsz, nh, dk, dv), jnp.float32)
    _, o = lax.scan(step, state0, xs)
    return o.transpose(1, 0, 3, 2, 4).reshape(bsz, seq, nh, dv)


def gated_deltanet_mixer(qkv, z, ab, w_conv, a_log, dt_bias, norm_w):
    bsz, seq, _ = qkv.shape
    out_dtype = qkv.dtype
    qkv = jax.nn.silu(causal_depthwise_conv(qkv, w_conv)).astype(jnp.float32)
    q, k, v = jnp.split(qkv, 3, axis=-1)
    q = l2_normalize(q.reshape(bsz, seq, GDN_HEADS, GDN_HEAD_DIM))
    k = l2_normalize(k.reshape(bsz, seq, GDN_HEADS, GDN_HEAD_DIM))
    v = v.reshape(bsz, seq, GDN_HEADS, GDN_HEAD_DIM)
    a_in, b_in = jnp.split(ab.astype(jnp.float32), 2, axis=-1)
    beta = jax.nn.sigmoid(b_in)
    g = -jnp.exp(a_log.astype(jnp.float32)) * jax.nn.softplus(a_in + dt_bias.astype(jnp.float32))
    o = gated_delta_chunked(q, k, v, g, beta)
    o = rms_norm(o, norm_w) * jax.nn.silu(
        z.astype(jnp.float32).reshape(bsz, seq, GDN_HEADS, GDN_HEAD_DIM))
    return o.reshape(bsz, seq, GDN_WIDTH).astype(out_dtype)


def differential_attention_mixer(q, k, v, lam_vecs, lam_init, subln_w):
    bsz, seq, _ = q.shape
    nh = DIFF_HEADS
    q = q.reshape(bsz, seq, nh, 2, DIFF_QK_DIM)
    k = k.reshape(bsz, seq, nh, 2, DIFF_QK_DIM)
    v = v.reshape(bsz, seq, nh, DIFF_V_DIM)
    lv = lam_vecs.astype(jnp.float32)
    lam = jnp.exp(jnp.sum(lv[0] * lv[1])) - jnp.exp(jnp.sum(lv[2] * lv[3])) + lam_init
    slopes = alibi_slopes(nh)
    scale = DIFF_QK_DIM ** -0.5
    nblk = seq // ATTN_BLOCK
    q_blocks = q.reshape(bsz, nblk, ATTN_BLOCK, nh, 2, DIFF_QK_DIM).transpose(1, 0, 2, 3, 4, 5)
    kpos = jnp.arange(seq)

    def block(args):
        q_blk, i = args
        qpos = i * ATTN_BLOCK + jnp.arange(ATTN_BLOCK)
        s = jnp.einsum('bqhcd,bkhcd->bhcqk', q_blk, k).astype(jnp.float32) * scale
        dist = (qpos[:, None] - kpos[None, :]).astype(jnp.float32)
        s = s - slopes[None, :, None, None, None] * dist
        s = jnp.where(dist >= 0.0, s, -jnp.inf)
        p = jax.nn.softmax(s, axis=-1)
        p = p[:, :, 0] - lam * p[:, :, 1]
        return jnp.einsum('bhqk,bkhd->bqhd', p.astype(v.dtype), v)

    o = lax.map(block, (q_blocks, jnp.arange(nblk)))
    o = o.transpose(1, 0, 2, 3, 4).reshape(bsz, seq, nh, DIFF_V_DIM)
    o = rms_norm(o, subln_w) * (1.0 - lam_init)
    return o.reshape(bsz, seq, DIFF_WIDTH)


def setup_inputs(seed: int = 0) -> dict:
    key = jax.random.key(seed)
    ks = jax.random.split(key, 20)
    f32 = jnp.float32
    nl = DEPTH

    def nrm(k, shape, scale):
        return jax.random.normal(k, shape, f32) * scale

    out_scale = (2.0 * DEPTH) ** -0.5
    dt = jnp.exp(jax.random.uniform(ks[9], (nl, GDN_HEADS), f32, math.log(1e-3), math.log(1e-1)))
    return {
        'x': nrm(ks[0], (BATCH, SEQ, D_MODEL), 1.0),
        'norm1_g': 1.0 + nrm(ks[1], (nl, D_MODEL), 0.02),
        'w_in': nrm(ks[2], (nl, D_MODEL, IN_WIDTH), D_MODEL ** -0.5),
        'conv_dw_w': nrm(ks[3], (nl, DW_CONV_LEN, CONV_WIDTH), DW_CONV_LEN ** -0.5),
        'conv_dw_b': nrm(ks[4], (nl, CONV_WIDTH), 0.01),
        'conv_ln_g': 1.0 + nrm(ks[5], (nl, CONV_WIDTH), 0.02),
        'conv_ln_b': nrm(ks[6], (nl, CONV_WIDTH), 0.01),
        'gdn_conv_w': nrm(ks[7], (nl, GDN_SHORT_CONV, 3 * GDN_WIDTH), GDN_SHORT_CONV ** -0.5),
        'gdn_a_log': jnp.log(jax.random.uniform(ks[8], (nl, GDN_HEADS), f32, 1.0, 16.0)),
        'gdn_dt_bias': dt + jnp.log(-jnp.expm1(-dt)),
        'gdn_norm_w': 1.0 + nrm(ks[10], (nl, GDN_HEAD_DIM), 0.02),
        'diff_lambda': nrm(ks[11], (nl, 4, DIFF_QK_DIM), 0.1),
        'diff_subln_w': 1.0 + nrm(ks[12], (nl, DIFF_V_DIM), 0.02),
        'w_out': nrm(ks[13], (nl, MIX_WIDTH, D_MODEL), MIX_WIDTH ** -0.5 * out_scale),
        'norm2_g': 1.0 + nrm(ks[14], (nl, D_MODEL), 0.02),
        'w_ffn_in': nrm(ks[15], (nl, D_MODEL, 2 * D_FF), D_MODEL ** -0.5),
        'w_ffn_out': nrm(ks[16], (nl, D_FF, D_MODEL), D_FF ** -0.5 * out_scale),
        'final_norm_g': 1.0 + nrm(ks[17], (D_MODEL,), 0.02),
    }


def reference(x, norm1_g, w_in, conv_dw_w, conv_dw_b, conv_ln_g, conv_ln_b, gdn_conv_w,
              gdn_a_log, gdn_dt_bias, gdn_norm_w, diff_lambda, diff_subln_w, w_out,
              norm2_g, w_ffn_in, w_ffn_out, final_norm_g):
    for l in range(DEPTH):
        lam_init = 0.8 - 0.6 * math.exp(-0.3 * l)
        h = rms_norm(x, norm1_g[l])
        u = jnp.einsum('btd,de->bte', h, w_in[l])
        u_conv, u_qkv, u_z, u_ab, u_dq, u_dk, u_dv = jnp.split(
            u, [SPLIT_1, SPLIT_2, SPLIT_3, SPLIT_4, SPLIT_5, SPLIT_6], axis=-1)
        y_conv = conformer_conv_mixer(u_conv, conv_dw_w[l], conv_dw_b[l], conv_ln_g[l], conv_ln_b[l])
        y_gdn = gated_deltanet_mixer(u_qkv, u_z, u_ab, gdn_conv_w[l], gdn_a_log[l],
                                     gdn_dt_bias[l], gdn_norm_w[l])
        y_diff = differential_attention_mixer(u_dq, u_dk, u_dv, diff_lambda[l], lam_init,
                                              diff_subln_w[l])
        y = jnp.concatenate([y_conv, y_gdn, y_diff], axis=-1)
        x = x + jnp.einsum('bte,ed->btd', y, w_out[l])
        h = rms_norm(x, norm2_g[l])
        gate, up = jnp.split(jnp.einsum('btd,df->btf', h, w_ffn_in[l]), 2, axis=-1)
        x = x + jnp.einsum('btf,fd->btd', jax.nn.silu(gate) * up, w_ffn_out[l])
    return rms_norm(x, final_norm_g)
```

```python
import math
import os
GSTOP = int(os.environ.get('GSTOP', '99'))
GCUT = int(os.environ.get('GCUT', '0'))
from contextlib import ExitStack, contextmanager

import numpy as np
import concourse.bass as bass
import concourse.mybir as mybir
from concourse.bass_utils import run_bass_kernel_spmd

F32 = mybir.dt.float32
BF16 = mybir.dt.bfloat16
AF = mybir.ActivationFunctionType
ALU = mybir.AluOpType
AX = mybir.AxisListType

D = 1024
T = 2048
NT = 16
DEPTH = 4
DFF = 2816
NFC = 22
INW = 3080
RMS_EPS = 1e-6
LN_EPS = 1e-5
NEG = -30000.0
SLOPES = [(2.0 ** (-8.0 / 4)) ** (i + 1) for i in range(4)]
NR = 19

PC_G1, PC_G2, PC_CB, PC_LG, PC_LB, PC_DW, PC_GW = 0, 8, 16, 18, 20, 22, 84
NPC = 84 + 24
PR_GNW, PR_SUB, PR_LAM, PR_ALOG, PR_DTB = 0, 64, 192, 448, 452
NPR = 456


class Buf:
    __slots__ = ("t", "lw", "rs", "dsem", "dkey", "dcnt", "name", "psum")

    def __init__(self, t, name, psum=False):
        self.t = t
        self.name = name
        self.psum = psum
        self.lw = None
        self.rs = {}
        self.dsem = None
        self.dkey = None
        self.dcnt = 0

    def view(self, name=None):
        return Buf(self.t, name or self.name)


class K:
    def __init__(self, nc, es):
        self.nc = nc
        self.es = es
        self.engs = {"pe": nc.tensor, "act": nc.scalar, "dve": nc.vector, "pool": nc.gpsimd, "sp": nc.sync}
        self.semh = {}
        self.cnt = {}
        self.pend = {}
        self.seen = {}
        for e in self.engs:
            self.semh[e] = es.enter_context(nc.semaphore("s_" + e))
            self.cnt[e] = 0
            self.pend[e] = False
            self.seen[e] = {}
        self.ndsem = 0
        self.nalloc = 0
        self.ninst = 0

    def sb(self, shape, dt, name=None, es=None):
        self.nalloc += 1
        name = "sb%d_%s" % (self.nalloc, name or "t")
        t = (es or self.es).enter_context(self.nc.sbuf_tensor(name, list(shape), dt))
        return Buf(t, name)

    def ps(self, name):
        t = self.es.enter_context(self.nc.psum_tensor(name, [128, 512], F32))
        return Buf(t, name, psum=True)

    @contextmanager
    def scope(self):
        es = ExitStack()
        k = self

        class S:
            def sb(self, shape, dt, name=None):
                return k.sb(shape, dt, name, es=es)

        try:
            yield S()
            self.barrier()
        finally:
            es.close()

    def _waits(self, e, deps, skipkey=None):
        eng = self.engs[e]
        for key, v in deps.items():
            if key == skipkey:
                continue
            if key == e and e in ("pe", "sp", "pool"):
                continue
            if self.seen[e].get(key, 0) >= v:
                continue
            eng.wait_ge(self.semh[key], v)
            self.seen[e][key] = v
            self.ninst += 1

    @staticmethod
    def _deps(reads, writes, e=None):
        deps = {}
        for b in reads:
            if b.lw is not None:
                deps[b.lw[0]] = max(deps.get(b.lw[0], 0), b.lw[1])
            if b.psum:
                for key, v in b.rs.items():
                    if key != e:
                        deps[key] = max(deps.get(key, 0), v)
        for b in writes:
            if b.lw is not None:
                deps[b.lw[0]] = max(deps.get(b.lw[0], 0), b.lw[1])
            for key, v in b.rs.items():
                deps[key] = max(deps.get(key, 0), v)
        return deps

    def op(self, e, name, reads=(), writes=(), inc=True, **kw):
        self.nops = getattr(self, "nops", 0) + 1
        if GCUT and self.nops > GCUT:
            return None
        deps = self._deps(reads, writes, e)
        self._waits(e, deps)
        ins = getattr(self.engs[e], name)(**kw)
        self.ninst += 1
        idx = self.cnt[e] + 1
        if inc:
            ins.then_inc(self.semh[e], 1)
            self.cnt[e] = idx
            self.pend[e] = False
        else:
            self.pend[e] = True
        for b in reads:
            b.rs[e] = idx
        for b in writes:
            b.lw = (e, idx)
            b.rs = {}
        return ins

    def _dsem(self, b):
        if b.dsem is None:
            self.ndsem += 1
            b.dkey = "d%d_%s" % (self.ndsem, b.name)
            b.dsem = self.es.enter_context(self.nc.semaphore(b.dkey))
            self.semh[b.dkey] = b.dsem
        return b.dsem

    def load(self, q, buf, out, in_):
        sem = self._dsem(buf)
        deps = self._deps((), (buf,))
        self._waits(q, deps, skipkey=buf.dkey)
        self.engs[q].dma_start(out=out, in_=in_).then_inc(sem, 16)
        self.ninst += 1
        buf.dcnt += 16
        buf.lw = (buf.dkey, buf.dcnt)
        buf.rs = {}

    def store(self, q, buf, out, in_):
        sem = self._dsem(buf)
        deps = self._deps((buf,), ())
        self._waits(q, deps, skipkey=None)
        self.engs[q].dma_start(out=out, in_=in_).then_inc(sem, 16)
        self.ninst += 1
        buf.dcnt += 16
        buf.rs[buf.dkey] = buf.dcnt

    def barrier(self):
        assert GCUT or not any(self.pend.values()), self.pend
        for e in ("pe", "act", "dve"):
            deps = {d: self.cnt[d] for d in ("pe", "act", "dve") if d != e and self.cnt[d] > 0}
            self._waits(e, deps)

    def wait_all_stores(self, e, bufs):
        deps = {}
        for b in bufs:
            if b.dkey is not None:
                deps[b.dkey] = b.dcnt
        self._waits(e, deps)


def build_program(nseq=4, depth=DEPTH, dbg=False, mixers=("conv", "gdn", "attn"), do_ffn=True):
    nc = bass.Bass("TRN2", target_bir_lowering=False)
    dr = {}

    def din(name, shape, dt=F32):
        dr[name] = nc.dram_tensor(name, list(shape), dt, kind="ExternalInput").ap()
        return dr[name]

    xT_d = din("xT", [nseq, D, T])
    w_in_d = din("w_in", [DEPTH, D, INW])
    w_out_d = din("w_out", [DEPTH, D, D])
    w_f1_d = din("w_ffn_in", [DEPTH, D, 2 * DFF])
    w_f2_d = din("w_ffn_out", [DEPTH, DFF, D])
    pcol_d = din("pcol", [DEPTH + 1, 128, NPC])
    prow_d = din("prow", [DEPTH, 128, NPR])
    c32_d = din("c32", [128, 4 * 128 + 4 * 128 + 4 * NR])
    cb_d = din("cb", [128, 5 * 128])
    out_d = nc.dram_tensor("outT", [nseq, D, T], F32, kind="ExternalOutput").ap()
    if dbg:
        dbg_d = nc.dram_tensor("dbg", [8, 128, T], F32, kind="ExternalOutput").ap()

    with ExitStack() as es:
        k = K(nc, es)
        xT_t = k.sb([128, 8, T], F32, "xT")
        hT_t = k.sb([128, 8, T], BF16, "hT")
        xT = [xT_t.view("xT%d" % g) for g in range(4)]
        hT = [hT_t.view("hT%d" % g) for g in range(4)]
        NS = 6
        slots = [k.sb([128, 2048], BF16, "slot%d" % i) for i in range(NS)]
        c32 = k.sb([128, 4 * 128 + 4 * 128 + 4 * NR], F32, "c32")
        cb = k.sb([128, 5 * 128], BF16, "cb")
        pcol = k.sb([128, NPC], F32, "pcol")
        prow = k.sb([128, NPR], F32, "prow")
        wab = k.sb([128, 8, 8], BF16, "wab")
        P = [k.ps("P%d" % i) for i in range(8)]
        epsb = k.sb([128, 4], F32, "epsb")
        k.op("dve", "memset", writes=[epsb], ap=epsb.t[:, 0:1], constant=RMS_EPS)
        k.op("dve", "memset", writes=[epsb], ap=epsb.t[:, 1:2], constant=LN_EPS)
        k.op("dve", "memset", writes=[epsb], ap=epsb.t[:, 2:3], constant=1.0)
        k.op("dve", "memset", writes=[epsb], ap=epsb.t[:, 3:4], constant=0.0)

        k.load("sp", c32, c32.t[:], c32_d)
        k.load("pool", cb, cb.t[:], cb_d)
        maskC = c32.t[:, 0:128]
        maskU = c32.t[:, 128:256]
        triT = c32.t[:, 256:384]
        ones32 = c32.t[:, 384:512]
        ident32 = c32.t[:, 512:640]
        abias = lambda h, r: c32.t[:, 1024 + h * NR + (r + 3):1024 + h * NR + (r + 3) + 1]
        ident = cb.t[:, 0:128]
        onesb = cb.t[:, 128:256]
        blk64 = cb.t[:, 256:384]
        strict = cb.t[:, 384:512]
        triU = cb.t[:, 512:640]

        slot_i = [0]

        def next_slot():
            s = slots[slot_i[0] % NS]
            slot_i[0] += 1
            return s

        def wload(slot, dst, src):
            k.load("pool", slot, dst, src)

        def w_in_cols(l, c0, n):
            return w_in_d[l, :, c0:c0 + n].rearrange("(c p) e -> p c e", p=128)

        def proj_fm(pb, wslot, wv, m0, m, tg, act=hT, kc=8, act_t=None):
            at = act_t if act_t is not None else hT_t.t
            for c in range(kc):
                k.op("pe", "matmul", reads=[wslot, act[tg]], writes=[pb], inc=(c == kc - 1),
                     out=pb.t[0:m, :], lhsT=wv[:, c, m0:m0 + m], rhs=at[:, c, tg * 512:(tg + 1) * 512],
                     start=(c == 0), stop=(c == kc - 1))

        def phase_norm(gcol, final=False, s=0):
            with k.scope() as sc:
                sq = [sc.sb([128, 8, 512], BF16) for _ in range(2)]
                rs = [sc.sb([128, 512], F32) for _ in range(2)]
                ob = [sc.sb([128, 8, 512], F32) for _ in range(2)] if final else None
                for tg in range(4):
                    ts = slice(tg * 512, (tg + 1) * 512)
                    s_, r_, pb = sq[tg % 2], rs[tg % 2], P[tg % 2]
                    k.op("act", "activation", reads=[xT[tg]], writes=[s_],
                         out=s_.t[:], in_=xT_t.t[:, :, ts], func=AF.Square)
                    for c in range(8):
                        k.op("pe", "matmul", reads=[s_, cb], writes=[pb], inc=(c == 7),
                             out=pb.t[:], lhsT=onesb, rhs=s_.t[:, c, :], start=(c == 0), stop=(c == 7))
                    k.op("act", "activation", reads=[pb, epsb], writes=[r_],
                         out=r_.t[:], in_=pb.t[:], func=AF.Sqrt, scale=1.0 / D, bias=epsb.t[:, 0:1])
                    k.op("dve", "reciprocal", reads=[r_], writes=[r_], out=r_.t[:], in_=r_.t[:])
                    if not final:
                        for c in range(8):
                            k.op("dve", "scalar_tensor_tensor", reads=[xT[tg], r_, pcol], writes=[hT[tg]],
                                 out=hT_t.t[:, c, ts], in0=xT_t.t[:, c, ts], scalar=gcol[:, c:c + 1],
                                 in1=r_.t[:], op0=ALU.mult, op1=ALU.mult)
                    else:
                        o_ = ob[tg % 2]
                        for c in range(8):
                            k.op("dve", "scalar_tensor_tensor", reads=[xT[tg], r_, pcol], writes=[o_],
                                 out=o_.t[:, c, :], in0=xT_t.t[:, c, ts], scalar=gcol[:, c:c + 1],
                                 in1=r_.t[:], op0=ALU.mult, op1=ALU.mult)
                        k.store("sp", o_, out_d[s, :, ts].rearrange("(c p) t -> p c t", p=128), o_.t[:])
                if final:
                    k.wait_all_stores("sp", ob)

        def out_proj(l, yb, yt, r0, kc):
            sl = next_slot()
            wv = sl.t[:, 0:kc * 1024].rearrange("p (c e) -> p c e", c=kc)
            wload(sl, wv, w_out_d[l, r0 * 128:(r0 + kc) * 128, :].rearrange("(c p) e -> p c e", p=128))
            i = 0
            for dc in range(8):
                for tg in range(4):
                    pb = P[i % 2]
                    i += 1
                    ts = slice(tg * 512, (tg + 1) * 512)
                    for c in range(kc):
                        k.op("pe", "matmul", reads=[sl, yb], writes=[pb], inc=(c == kc - 1),
                             out=pb.t[:], lhsT=wv[:, c, dc * 128:(dc + 1) * 128], rhs=yt[:, c, ts],
                             start=(c == 0), stop=(c == kc - 1))
                    k.op("dve", "tensor_tensor", reads=[pb, xT[tg]], writes=[xT[tg]],
                         out=xT_t.t[:, dc, ts], in0=xT_t.t[:, dc, ts], in1=pb.t[:], op=ALU.add)

        def dbg_dump(yb, yt, c0, kc, sc):
            if not dbg:
                return
            tmp = sc.sb([128, 512], F32)
            for c in range(kc):
                for tg in range(4):
                    k.op("act", "activation", reads=[yb], writes=[tmp], out=tmp.t[:], in_=yt[:, c, tg * 512:(tg + 1) * 512], func=AF.Copy)
                    k.store("sp", tmp, dbg_d[c0 + c, :, tg * 512:(tg + 1) * 512], tmp.t[:])
            k.wait_all_stores("act", [tmp])

        def phase_conv(l):
            with k.scope() as sc:
                sv, sg = next_slot(), next_slot()
                wvv = sv.t[:].rearrange("p (c e) -> p c e", c=8)
                wgv = sg.t[:].rearrange("p (c e) -> p c e", c=8)
                wload(sv, wvv, w_in_cols(l, 0, 256))
                wload(sg, wgv, w_in_cols(l, 256, 256))
                glu = sc.sb([128, 2, 32 + T], BF16, "glu")
                yc = sc.sb([128, 2, T], BF16, "yconv")
                sig = [sc.sb([128, 512], F32) for _ in range(2)]
                k.op("dve", "memset", writes=[glu], ap=glu.t[:, :, 0:32], constant=0.0)
                n = 0
                for tg in range(4):
                    for j in range(2):
                        pv, pg = P[(n * 2) % 4], P[(n * 2 + 1) % 4]
                        sg_ = sig[n % 2]
                        n += 1
                        proj_fm(pg, sg, wgv, j * 128, 128, tg)
                        proj_fm(pv, sv, wvv, j * 128, 128, tg)
                        k.op("act", "activation", reads=[pg], writes=[sg_], out=sg_.t[:], in_=pg.t[:], func=AF.Sigmoid)
                        k.op("dve", "tensor_tensor", reads=[pv, sg_], writes=[glu],
                             out=glu.t[:, j, 32 + tg * 512:32 + (tg + 1) * 512], in0=pv.t[:], in1=sg_.t[:], op=ALU.mult)
                dg = [sc.sb([128, 128], BF16) for _ in range(4)]
                cv = [sc.sb([128, 2, 512], F32, "cv%d" % i) for i in range(4)]
                cq = [sc.sb([128, 2, 512], F32, "cq%d" % i) for i in range(4)]
                n = 0
                for j in range(2):
                    for tap in range(31):
                        d_ = dg[n % 4]
                        n += 1
                        k.op("dve", "tensor_scalar", reads=[cb, pcol], writes=[d_],
                             out=d_.t[:], in0=ident, scalar1=pcol.t[:, PC_DW + j * 31 + tap:PC_DW + j * 31 + tap + 1],
                             scalar2=None, op0=ALU.mult)
                        for tg in range(4):
                            pb = P[4 + tg]
                            k.op("pe", "matmul", reads=[d_, glu], writes=[pb], inc=(tap == 30 or tg == 3),
                                 out=pb.t[:], lhsT=d_.t[:], rhs=glu.t[:, j, 2 + tap + tg * 512:2 + tap + (tg + 1) * 512],
                                 start=(tap == 0), stop=(tap == 30))
                    for tg in range(4):
                        pb = P[4 + tg]
                        k.op("act", "activation", reads=[pb, pcol], writes=[cv[tg]],
                             out=cv[tg].t[:, j, :], in_=pb.t[:], func=AF.Identity, bias=pcol.t[:, PC_CB + j:PC_CB + j + 1])
                        k.op("act", "activation", reads=[cv[tg]], writes=[cq[tg]],
                             out=cq[tg].t[:, j, :], in_=cv[tg].t[:, j, :], func=AF.Square)
                tmp = [sc.sb([128, 512], F32) for _ in range(4)]
                for tg in range(4):
                    pm, pq = P[(tg * 2) % 4], P[(tg * 2 + 1) % 4]
                    for j in range(2):
                        k.op("pe", "matmul", reads=[cv[tg], c32], writes=[pm], inc=(j == 1),
                             out=pm.t[:], lhsT=ones32, rhs=cv[tg].t[:, j, :], start=(j == 0), stop=(j == 1))
                    for j in range(2):
                        k.op("pe", "matmul", reads=[cq[tg], c32], writes=[pq], inc=(j == 1),
                             out=pq.t[:], lhsT=ones32, rhs=cq[tg].t[:, j, :], start=(j == 0), stop=(j == 1))
                    mean, var = tmp[0], tmp[1]
                    k.op("act", "activation", reads=[pm], writes=[mean], out=mean.t[:], in_=pm.t[:], func=AF.Copy, scale=1.0 / 256)
                    k.op("act", "activation", reads=[mean], writes=[var], out=var.t[:], in_=mean.t[:], func=AF.Square)
                    k.op("dve", "scalar_tensor_tensor", reads=[pq, var], writes=[var],
                         out=var.t[:], in0=pq.t[:], scalar=1.0 / 256, in1=var.t[:], op0=ALU.mult, op1=ALU.subtract)
                    k.op("act", "activation", reads=[var, epsb], writes=[var],
                         out=var.t[:], in_=var.t[:], func=AF.Sqrt, bias=epsb.t[:, 1:2])
                    k.op("dve", "reciprocal", reads=[var], writes=[var], out=var.t[:], in_=var.t[:])
                    for j in range(2):
                        t_ = tmp[2 + j]
                        k.op("dve", "tensor_tensor", reads=[cv[tg], mean], writes=[t_],
                             out=t_.t[:], in0=cv[tg].t[:, j, :], in1=mean.t[:], op=ALU.subtract)
                        k.op("dve", "tensor_tensor", reads=[t_, var], writes=[t_],
                             out=t_.t[:], in0=t_.t[:], in1=var.t[:], op=ALU.mult)
                        k.op("act", "activation", reads=[t_, pcol], writes=[yc],
                             out=yc.t[:, j, tg * 512:(tg + 1) * 512], in_=t_.t[:], func=AF.Silu,
                             scale=pcol.t[:, PC_LG + j:PC_LG + j + 1], bias=pcol.t[:, PC_LB + j:PC_LB + j + 1])
                dbg_dump(yc, yc.t, 0, 2, sc)
                out_proj(l, yc, yc.t, 0, 2)

        def phase_attn(l, lam_init):
            scale = 64 ** -0.5
            with k.scope() as sc:
                lt = sc.sb([128, 2, 64], F32)
                ls = sc.sb([128, 2], F32)
                nlam = sc.sb([128, 1], F32)
                k.op("dve", "tensor_tensor", reads=[prow], writes=[lt], out=lt.t[:, 0, :],
                     in0=prow.t[:, PR_LAM:PR_LAM + 64], in1=prow.t[:, PR_LAM + 64:PR_LAM + 128], op=ALU.mult)
                k.op("dve", "tensor_tensor", reads=[prow], writes=[lt], out=lt.t[:, 1, :],
                     in0=prow.t[:, PR_LAM + 128:PR_LAM + 192], in1=prow.t[:, PR_LAM + 192:PR_LAM + 256], op=ALU.mult)
                k.op("dve", "tensor_reduce", reads=[lt], writes=[ls], out=ls.t[:], in_=lt.t[:], axis=AX.X, op=ALU.add)
                k.op("act", "activation", reads=[ls], writes=[ls], out=ls.t[:], in_=ls.t[:], func=AF.Exp)
                k.op("dve", "scalar_tensor_tensor", reads=[ls], writes=[nlam], out=nlam.t[:], in0=ls.t[:, 1:2],
                     scalar=-lam_init, in1=ls.t[:, 0:1], op0=ALU.add, op1=ALU.subtract)
                wrow = sc.sb([128, 128], F32)
                k.op("dve", "tensor_scalar", reads=[prow], writes=[wrow], out=wrow.t[:], in0=prow.t[:, PR_SUB:PR_SUB + 128],
                     scalar1=1.0 - lam_init, scalar2=None, op0=ALU.mult)

                qT = sc.sb([128, 2, T], BF16, "qT")
                kT = sc.sb([128, 2, T], BF16, "kT")
                V1 = sc.sb([128, NT, 2, 132], BF16, "V1")
                yd = sc.sb([128, 2, T], BF16, "ydiff")
                pt = [sc.sb([128, 512], BF16, "pt%d" % i) for i in range(3)]
                oh = [sc.sb([128, 4, 132], F32, "oh%d" % i) for i in range(2)]
                rc = sc.sb([128, 8], F32)
                o1 = sc.sb([128, 128], F32)
                o2 = sc.sb([128, 128], F32)
                ssq = sc.sb([128, 2], F32)
                yb_ = sc.sb([128, 128], BF16)
                for hp in range(2):
                    sq_, sk_, sv_ = next_slot(), next_slot(), next_slot()
                    views = []
                    for s_, c0 in ((sq_, 1544), (sk_, 2056), (sv_, 2568)):
                        v_ = s_.t[:].rearrange("p (c e) -> p c e", c=8)
                        wload(s_, v_, w_in_cols(l, c0 + hp * 256, 256))
                        views.append(v_)
                    wq, wk, wv = views
                    k.op("dve", "memset", writes=[V1], ap=V1.t[:, :, :, 128:129], constant=1.0)
                    n = 0
                    for hh in range(2):
                        for tg in range(4):
                            pb = P[n % 2]
                            n += 1
                            proj_fm(pb, sq_, wq, hh * 128, 128, tg)
                            k.op("act", "activation", reads=[pb], writes=[qT], out=qT.t[:, hh, tg * 512:(tg + 1) * 512],
                                 in_=pb.t[:], func=AF.Copy, scale=scale)
                            pb = P[n % 2]
                            n += 1
                            proj_fm(pb, sk_, wk, hh * 128, 128, tg)
                            k.op("dve", "tensor_copy", reads=[pb], writes=[kT], out=kT.t[:, hh, tg * 512:(tg + 1) * 512],
                                 in_=pb.t[:])
                    for tt in range(NT):
                        pb = P[tt % 2]
                        for c in range(8):
                            k.op("pe", "matmul", reads=[sv_, hT[tt // 4]], writes=[pb], inc=(c == 7),
                                 out=pb.t[:, 0:256], lhsT=hT_t.t[:, c, tt * 128:(tt + 1) * 128], rhs=wv[:, c, :],
                                 start=(c == 0), stop=(c == 7))
                        k.op("act", "activation", reads=[pb], writes=[V1], out=V1.t[:, tt, :, 0:128],
                             in_=pb.t[:, 0:256].rearrange("p (h e) -> p h e", h=2), func=AF.Copy)
                    for hh in range(2):
                        h = hp * 2 + hh
                        W = 128 if SLOPES[h] * 511 > 40 else 512
                        for g in range(4):
                            for c in range(2):
                                acc = [P[2], P[3], P[4], P[5]]
                                pr = slice(c * 64, (c + 1) * 64)
                                for j in range(4 * g + 4):
                                    qb0 = max(j, 4 * g)
                                    nq = 4 * g + 4 - qb0
                                    pS = P[6 + (j % 2)]
                                    p_ = pt[j % 3]
                                    k.op("pe", "matmul", reads=[kT, qT], writes=[pS],
                                         out=pS.t[:, 0:nq * 128], lhsT=kT.t[pr, hh, j * 128:(j + 1) * 128],
                                         rhs=qT.t[pr, hh, qb0 * 128:(qb0 + nq) * 128], start=True, stop=True)
                                    if W == 512:
                                        k.op("act", "activation", reads=[pS, c32], writes=[p_], out=p_.t[:, 0:nq * 128],
                                             in_=pS.t[:, 0:nq * 128], func=AF.Exp, bias=abias(h, 4 * g - j))
                                    else:
                                        for qi in range(nq):
                                            k.op("act", "activation", reads=[pS, c32], writes=[p_],
                                                 out=p_.t[:, qi * 128:(qi + 1) * 128], in_=pS.t[:, qi * 128:(qi + 1) * 128],
                                                 func=AF.Exp, bias=abias(h, qb0 + qi - j))
                                    if j >= 4 * g:
                                        k.op("dve", "tensor_tensor", reads=[p_, cb], writes=[p_], out=p_.t[:, 0:128],
                                             in0=p_.t[:, 0:128], in1=triU, op=ALU.mult)
                                    for qi in range(nq):
                                        qb = qb0 + qi
                                        a_ = acc[qb % 4]
                                        col = 0
                                        k.op("pe", "matmul", reads=[p_, V1], writes=[a_], inc=True,
                                             out=a_.t[:, col:col + 129], lhsT=p_.t[:, qi * 128:(qi + 1) * 128],
                                             rhs=V1.t[:, j, hh, 0:129], start=(j == 0), stop=(j == qb))
                                o_ = oh[c]
                                for a in range(4):
                                    k.op("act" if a % 2 == 0 else "dve", "activation" if a % 2 == 0 else "tensor_copy",
                                         reads=[acc[a]], writes=[o_], out=o_.t[:, a, 0:129], in_=acc[a].t[:, 0:129],
                                         **({"func": AF.Copy} if a % 2 == 0 else {}))
                            for c in range(2):
                                k.op("dve", "reciprocal", reads=[oh[c]], writes=[rc], out=rc.t[:, c * 4:(c + 1) * 4],
                                     in_=oh[c].t[:, :, 128])
                            for qi in range(4):
                                qb = 4 * g + qi
                                k.op("dve", "tensor_scalar", reads=[oh[0], rc], writes=[o1], out=o1.t[:], in0=oh[0].t[:, qi, 0:128],
                                     scalar1=rc.t[:, qi:qi + 1], scalar2=None, op0=ALU.mult)
                                k.op("dve", "tensor_scalar", reads=[oh[1], rc, nlam], writes=[o2], out=o2.t[:], in0=oh[1].t[:, qi, 0:128],
                                     scalar1=rc.t[:, 4 + qi:5 + qi], scalar2=nlam.t[:, 0:1], op0=ALU.mult, op1=ALU.mult)
                                k.op("dve", "tensor_tensor", reads=[o1, o2], writes=[o1], out=o1.t[:], in0=o1.t[:], in1=o2.t[:], op=ALU.add)
                                k.op("act", "activation", reads=[o1], writes=[o2, ssq], out=o2.t[:], in_=o1.t[:], func=AF.Square,
                                     accum_out=ssq.t[:, 0:1])
                                k.op("act", "activation", reads=[ssq, epsb], writes=[ssq], out=ssq.t[:, 1:2], in_=ssq.t[:, 0:1],
                                     func=AF.Sqrt, scale=1.0 / 128, bias=epsb.t[:, 0:1])
                                k.op("dve", "reciprocal", reads=[ssq], writes=[ssq], out=ssq.t[:, 1:2], in_=ssq.t[:, 1:2])
                                k.op("dve", "scalar_tensor_tensor", reads=[o1, ssq, wrow], writes=[yb_], out=yb_.t[:], in0=o1.t[:],
                                     scalar=ssq.t[:, 1:2], in1=wrow.t[:], op0=ALU.mult, op1=ALU.mult)
                                pT = P[qi % 2]
                                k.op("pe", "matmul", reads=[yb_, cb], writes=[pT], out=pT.t[:, 0:128], lhsT=yb_.t[:], rhs=ident,
                                     start=True, stop=True)
                                k.op("act", "activation", reads=[pT], writes=[yd], out=yd.t[:, hh, qb * 128:(qb + 1) * 128],
                                     in_=pT.t[:, 0:128], func=AF.Copy)
                    dbg_dump(yd, yd.t, 4 + hp * 2, 2, sc)
                    out_proj(l, yd, yd.t, 4 + hp * 2, 2)

        def phase_ffn(l):
            with k.scope() as sc:
                actT = sc.sb([128, NFC, 1024], BF16, "actT")
                sg = [sc.sb([128, 512], BF16) for _ in range(2)]
                for half in range(2):
                    n = 0
                    for fb in range(NFC):
                        sl = next_slot()
                        wv = sl.t[:].rearrange("p (c e) -> p c e", c=8)
                        wload(sl, wv[:, :, 0:128], w_f1_d[l, :, fb * 128:(fb + 1) * 128].rearrange("(c p) e -> p c e", p=128))
                        wload(sl, wv[:, :, 128:256], w_f1_d[l, :, DFF + fb * 128:DFF + (fb + 1) * 128].rearrange("(c p) e -> p c e", p=128))
                        for t2 in range(2):
                            tg = half * 2 + t2
                            pg, pu = P[(n * 2) % 4], P[(n * 2 + 1) % 4]
                            s_ = sg[n % 2]
                            n += 1
                            proj_fm(pg, sl, wv, 0, 128, tg)
                            proj_fm(pu, sl, wv, 128, 128, tg)
                            k.op("act", "activation", reads=[pg], writes=[s_], out=s_.t[:], in_=pg.t[:], func=AF.Silu)
                            k.op("dve", "tensor_tensor", reads=[pu, s_], writes=[actT], out=actT.t[:, fb, t2 * 512:(t2 + 1) * 512],
                                 in0=pu.t[:], in1=s_.t[:], op=ALU.mult)
                    for dg_ in range(4):
                        banks = [P[4 + i] for i in range(4)]
                        for kb in range(3):
                            nk = min(8, NFC - kb * 8)
                            sl = next_slot()
                            wv = sl.t[:].rearrange("p (c e) -> p c e", c=8)
                            wload(sl, wv[:, 0:nk, :], w_f2_d[l, kb * 1024:kb * 1024 + nk * 128, dg_ * 256:(dg_ + 1) * 256]
                                  .rearrange("(c p) e -> p c e", p=128))
                            for ci in range(nk):
                                fc = kb * 8 + ci
                                for dc2 in range(2):
                                    for t2 in range(2):
                                        pb = banks[dc2 * 2 + t2]
                                        k.op("pe", "matmul", reads=[sl, actT], writes=[pb], inc=(fc == NFC - 1 or (ci == nk - 1 and dc2 == 1 and t2 == 1)),
                                             out=pb.t[:], lhsT=wv[:, ci, dc2 * 128:(dc2 + 1) * 128],
                                             rhs=actT.t[:, fc, t2 * 512:(t2 + 1) * 512], start=(fc == 0), stop=(fc == NFC - 1))
                        for dc2 in range(2):
                            for t2 in range(2):
                                pb = banks[dc2 * 2 + t2]
                                tg = half * 2 + t2
                                dc = dg_ * 2 + dc2
                                ts = slice(tg * 512, (tg + 1) * 512)
                                k.op("dve", "tensor_tensor", reads=[pb, xT[tg]], writes=[xT[tg]],
                                     out=xT_t.t[:, dc, ts], in0=xT_t.t[:, dc, ts], in1=pb.t[:], op=ALU.add)

        def phase_gdn(l):
            with k.scope() as sc:
                s_q, s_k, s_v, s_z = next_slot(), next_slot(), next_slot(), next_slot()
                wviews = []
                for s_, c0 in ((s_q, 512), (s_k, 768), (s_v, 1024), (s_z, 1280)):
                    v_ = s_.t[:].rearrange("p (c e) -> p c e", c=8)
                    wload(s_, v_, w_in_cols(l, c0, 256))
                    wviews.append(v_)
                wq, wk, wv, wz = wviews
                k.load("pool", wab, wab.t[:], w_in_cols(l, 1536, 8))
                abp = P[0]
                for tt in range(NT):
                    for c in range(8):
                        k.op("pe", "matmul", reads=[wab, hT[tt // 4]], writes=[abp], inc=(c == 7),
                             out=abp.t[:, tt * 8:(tt + 1) * 8], lhsT=hT_t.t[:, c, tt * 128:(tt + 1) * 128], rhs=wab.t[:, c, :],
                             start=(c == 0), stop=(c == 7))
                abv = abp.t[:, 0:128].rearrange("p (t e) -> p t e", e=8)
                gt = sc.sb([128, NT, 4], F32, "gt")
                bt = sc.sb([128, NT, 4], F32, "bt")
                nbt = sc.sb([128, NT, 4], F32, "nbt")
                gc = sc.sb([128, NT, 4], F32, "gc")
                ed = sc.sb([128, NT, 4], F32, "ed")
                bew = sc.sb([128, NT, 4], F32, "bew")
                na = sc.sb([128, 4], F32, "na")
                for h in range(4):
                    k.op("dve", "tensor_scalar", reads=[abp, prow], writes=[gt], out=gt.t[:, :, h], in0=abv[:, :, h],
                         scalar1=prow.t[:, PR_DTB + h:PR_DTB + h + 1], scalar2=None, op0=ALU.add)
                k.op("act", "activation", reads=[gt], writes=[gt], out=gt.t[:], in_=gt.t[:], func=AF.Exp)
                k.op("act", "activation", reads=[gt, epsb], writes=[gt], out=gt.t[:], in_=gt.t[:], func=AF.Ln, bias=epsb.t[:, 2:3])
                k.op("act", "activation", reads=[prow], writes=[na], out=na.t[:], in_=prow.t[:, PR_ALOG:PR_ALOG + 4], func=AF.Exp)
                for h in range(4):
                    k.op("dve", "tensor_scalar", reads=[gt, na], writes=[gt], out=gt.t[:, :, h], in0=gt.t[:, :, h],
                         scalar1=na.t[:, h:h + 1], scalar2=-1.0, op0=ALU.mult, op1=ALU.mult)
                k.op("act", "activation", reads=[abp], writes=[bt], out=bt.t[:], in_=abv[:, :, 4:8], func=AF.Sigmoid)
                k.op("dve", "tensor_scalar", reads=[bt], writes=[nbt], out=nbt.t[:], in0=bt.t[:], scalar1=-1.0, scalar2=None, op0=ALU.mult)
                gflat = gt.t[:].rearrange("p t h -> p (t h)")
                pcs = P[1]
                k.op("pe", "matmul", reads=[gt, c32], writes=[pcs], out=pcs.t[:, 0:64], lhsT=triT, rhs=gflat, start=True, stop=True)
                k.op("pe", "matmul", reads=[gt, c32], writes=[pcs], out=pcs.t[:, 64:128], lhsT=ones32, rhs=gflat, start=True, stop=True)
                gcf = gc.t[:].rearrange("p t h -> p (t h)")
                edf = ed.t[:].rearrange("p t h -> p (t h)")
                bewf = bew.t[:].rearrange("p t h -> p (t h)")
                k.op("act", "activation", reads=[pcs], writes=[gc], out=gcf, in_=pcs.t[:, 0:64], func=AF.Copy)
                k.op("dve", "tensor_tensor", reads=[pcs, gc], writes=[ed], out=edf, in0=pcs.t[:, 64:128], in1=gcf, op=ALU.subtract)
                k.op("act", "activation", reads=[ed], writes=[ed], out=edf, in_=edf, func=AF.Exp)
                k.op("act", "activation", reads=[gc], writes=[bew], out=bewf, in_=gcf, func=AF.Exp)
                k.op("dve", "tensor_tensor", reads=[bew, bt], writes=[bew], out=bewf, in0=bewf, in1=bt.t[:].rearrange("p t h -> p (t h)"), op=ALU.mult)

                print("ops at stage1:", k.nops)
                if GSTOP <= 1:
                    return
                qT = sc.sb([128, 2, T], BF16, "gq")
                kT = sc.sb([128, 2, T], BF16, "gk")
                vT = sc.sb([128, 2, T], BF16, "gv")
                yg = sc.sb([128, 2, T], BF16, "yg")
                with k.scope() as sc2:
                    ut = [sc2.sb([128, 4 + T], BF16, "ut%d" % i) for i in range(2)]
                    dgs = [sc2.sb([128, 128], BF16) for _ in range(4)]
                    sl32 = [sc2.sb([128, 512], F32) for _ in range(2)]
                    sqb = [sc2.sb([128, 512], BF16) for _ in range(2)]
                    rn = [sc2.sb([128, 512], F32) for _ in range(2)]
                    for u_ in ut:
                        k.op("dve", "memset", writes=[u_], ap=u_.t[:, 0:4], constant=0.0)
                    n = 0
                    for kind, (slot_, wv_, dst) in enumerate(((s_q, wq, qT), (s_k, wk, kT), (s_v, wv, vT))):
                        for hp in range(2):
                            ch = kind * 2 + hp
                            u_ = ut[(kind * 2 + hp) % 2]
                            for tg in range(4):
                                pb = P[tg % 2]
                                proj_fm(pb, slot_, wv_, hp * 128, 128, tg)
                                k.op("act", "activation", reads=[pb], writes=[u_], out=u_.t[:, 4 + tg * 512:4 + (tg + 1) * 512],
                                     in_=pb.t[:], func=AF.Copy)
                            for tap in range(4):
                                k.op("dve", "tensor_scalar", reads=[cb, pcol], writes=[dgs[tap]], out=dgs[tap].t[:], in0=ident,
                                     scalar1=pcol.t[:, PC_GW + ch * 4 + tap:PC_GW + ch * 4 + tap + 1], scalar2=None, op0=ALU.mult)
                            for tg in range(4):
                                pb = P[4 + tg]
                                ts = slice(tg * 512, (tg + 1) * 512)
                                for tap in range(4):
                                    k.op("pe", "matmul", reads=[dgs[tap], u_], writes=[pb], inc=(tap == 3), out=pb.t[:], lhsT=dgs[tap].t[:],
                                         rhs=u_.t[:, 1 + tap + tg * 512:1 + tap + (tg + 1) * 512], start=(tap == 0), stop=(tap == 3))
                                if kind == 2:
                                    k.op("act", "activation", reads=[pb], writes=[dst], out=dst.t[:, hp, ts], in_=pb.t[:], func=AF.Silu)
                                else:
                                    s32, sq_, rn_ = sl32[n % 2], sqb[n % 2], rn[n % 2]
                                    pn = P[2 + n % 2]
                                    n += 1
                                    k.op("act", "activation", reads=[pb], writes=[s32], out=s32.t[:], in_=pb.t[:], func=AF.Silu)
                                    k.op("act", "activation", reads=[s32], writes=[sq_], out=sq_.t[:], in_=s32.t[:], func=AF.Square)
                                    k.op("pe", "matmul", reads=[sq_, cb], writes=[pn], out=pn.t[:], lhsT=blk64, rhs=sq_.t[:], start=True, stop=True)
                                    k.op("act", "activation", reads=[pn, epsb], writes=[rn_], out=rn_.t[:], in_=pn.t[:], func=AF.Sqrt, bias=epsb.t[:, 0:1])
                                    k.op("dve", "reciprocal", reads=[rn_], writes=[rn_], out=rn_.t[:], in_=rn_.t[:])
                                    k.op("dve", "scalar_tensor_tensor", reads=[s32, rn_], writes=[dst], out=dst.t[:, hp, ts], in0=s32.t[:],
                                         scalar=(0.125 if kind == 0 else 1.0), in1=rn_.t[:], op0=ALU.mult, op1=ALU.mult)

                print("ops at stage2:", k.nops)
                if GSTOP <= 2:
                    return
                S = sc.sb([128, 2, 128], F32, "S")
                k.op("dve", "memset", writes=[S], ap=S.t[:], constant=0.0)
                Gb = sc.sb([128, 4, 128], F32, "Gb")
                Gm = sc.sb([128, 4, 128], F32, "Gm")
                Gm2 = sc.sb([128, 4, 128], F32, "Gm2")
                Dc = sc.sb([128, 4, 128], F32, "Dc")
                DT = sc.sb([128, 4, 128], BF16, "DT")
                Eg = sc.sb([128, 4, 128], F32, "Eg")
                Nb = [sc.sb([128, 4, 128], F32, "Nb%d" % i) for i in range(2)]
                Mb = [sc.sb([128, 4, 128], F32, "Mb%d" % i) for i in range(2)]
                Y = sc.sb([128, 4, 128], F32, "Y")
                QKT = sc.sb([128, 4, 128], BF16, "QKT")
                U = sc.sb([128, 256], F32, "U")
                WT = sc.sb([128, 2, 128], F32, "WT")
                qe = sc.sb([128, 2, 128], F32, "qe")
                kd = sc.sb([128, 256], BF16, "kd")
                rhu = sc.sb([128, 256], F32, "rhu")
                rhw = sc.sb([128, 256], F32, "rhw")
                vnew = sc.sb([128, 256], BF16, "vnew")
                ob = sc.sb([128, 4, 64], F32, "ob")
                osq = sc.sb([128, 4, 64], F32, "osq")
                ors = sc.sb([128, 8], F32, "ors")
                zs = sc.sb([128, 256], F32, "zs")
                ytok = sc.sb([128, 256], BF16, "ytok")
                kc2 = sc.sb([128, 4, 128], BF16, "kc2")
                kc2v = kc2.t[:].rearrange("p (a b) e -> p a b e", a=2)
                k.op("dve", "memset", writes=[kc2], ap=kc2.t[:], constant=0.0)
                gnw = prow.t[:, PR_GNW:PR_GNW + 64]
                for n in range(NT):
                    cs = slice(n * 128, (n + 1) * 128)
                    tgi = n // 4
                    PA, PB, PT_, PN, PM, PY, PK, PX = P
                    for hp in range(2):
                        k.op("pe", "matmul", reads=[kT, cb], writes=[PK], out=PK.t[:, hp * 128:(hp + 1) * 128], lhsT=kT.t[:, hp, cs], rhs=ident,
                             start=True, stop=True)
                        k.op("pe", "matmul", reads=[vT, cb], writes=[PK], out=PK.t[:, 256 + hp * 128:256 + (hp + 1) * 128], lhsT=vT.t[:, hp, cs],
                             rhs=ident, start=True, stop=True)
                    for h in range(4):
                        hs = slice(h * 64, (h + 1) * 64)
                        k.op("dve", "tensor_scalar", reads=[PK, bew], writes=[rhw], out=rhw.t[:, hs], in0=PK.t[:, hs],
                             scalar1=bew.t[:, n, h:h + 1], scalar2=None, op0=ALU.mult)
                        k.op("dve", "tensor_scalar", reads=[PK, ed], writes=[kd], out=kd.t[:, hs], in0=PK.t[:, hs],
                             scalar1=ed.t[:, n, h:h + 1], scalar2=None, op0=ALU.mult)
                        k.op("dve", "tensor_scalar", reads=[PK, bt], writes=[rhu], out=rhu.t[:, hs], in0=PK.t[:, 256 + h * 64:256 + (h + 1) * 64],
                             scalar1=bt.t[:, n, h:h + 1], scalar2=None, op0=ALU.mult)
                    if n == 0: print('ops at stage3:', k.nops)
                    if GSTOP <= 3:
                        return
                    for h in range(4):
                        k.op("dve", "tensor_scalar", reads=[c32, gt], writes=[Gb], out=Gb.t[:, h, :], in0=ones32,
                             scalar1=gt.t[:, n, h:h + 1], scalar2=-1.0, op0=ALU.mult, op1=ALU.mult)
                        k.op("pe", "matmul", reads=[Gb, c32], writes=[PA], out=PA.t[:, h * 128:(h + 1) * 128], lhsT=Gb.t[:, h, :], rhs=triT,
                             start=True, stop=True)
                    PA3 = PA.t[:].rearrange("p (h e) -> p h e", h=4)
                    for h in range(4):
                        k.op("dve", "scalar_tensor_tensor", reads=[PA, gc, c32], writes=[Gm], out=Gm.t[:, h, :], in0=PA3[:, h, :],
                             scalar=gc.t[:, n, h:h + 1], in1=maskC, op0=ALU.add, op1=ALU.add)
                        k.op("dve", "scalar_tensor_tensor", reads=[PA, gc, c32], writes=[Gm2], out=Gm2.t[:, h, :], in0=PA3[:, h, :],
                             scalar=gc.t[:, n, h:h + 1], in1=maskU, op0=ALU.add, op1=ALU.subtract)
                    k.op("act", "activation", reads=[Gm], writes=[Dc], out=Dc.t[:], in_=Gm.t[:], func=AF.Exp)
                    k.op("act", "activation", reads=[Gm2], writes=[DT], out=DT.t[:], in_=Gm2.t[:], func=AF.Exp, scale=-1.0)
                    k.op("act", "activation", reads=[PA], writes=[Eg], out=Eg.t[:], in_=PA3, func=AF.Exp, scale=-1.0)
                    if n == 0: print('ops at stage4:', k.nops)
                    if GSTOP <= 4:
                        return
                    N_, M_ = Nb[0], Mb[0]
                    PB3 = PB.t[:].rearrange("p (h e) -> p h e", h=4)
                    for h in range(4):
                        pr = slice((h % 2) * 64, (h % 2 + 1) * 64)
                        if h == 0:
                            k.op("act", "activation", reads=[kT], writes=[kc2], out=kc2v[0:64, :, 0, :], in_=kT.t[0:64, :, cs], func=AF.Copy)
                            k.op("act", "activation", reads=[kT], writes=[kc2], out=kc2v[64:128, :, 1, :], in_=kT.t[64:128, :, cs], func=AF.Copy)
                        k.op("pe", "matmul", reads=[kT, kc2], writes=[PB], out=PB.t[:, h * 128:(h + 1) * 128], lhsT=kT.t[:, h // 2, cs], rhs=kc2.t[:, h, :],
                             start=True, stop=True)
                    for h in range(4):
                        k.op("dve", "scalar_tensor_tensor", reads=[PB, nbt, Dc], writes=[N_], out=N_.t[:, h, :], in0=PB3[:, h, :],
                             scalar=nbt.t[:, n, h:h + 1], in1=Dc.t[:, h, :], op0=ALU.mult, op1=ALU.mult)
                    PT3 = PT_.t[:].rearrange("p (h e) -> p h e", h=4)
                    for h in range(4):
                        k.op("pe", "matmul", reads=[N_, c32], writes=[PT_], out=PT_.t[:, h * 128:(h + 1) * 128], lhsT=N_.t[:, h, :], rhs=ident32, start=True, stop=True)
                    k.op("act", "activation", reads=[PT_], writes=[M_], out=M_.t[:], in_=PT3, func=AF.Copy)
                    for h in range(4):
                        k.op("dve", "tensor_tensor", reads=[PT_, c32], writes=[Y], out=Y.t[:, h, :], in0=PT3[:, h, :], in1=ident32, op=ALU.add)
                    if n == 0: print('ops at stage5:', k.nops)
                    if GSTOP <= 5:
                        return
                    PN3 = PN.t[:].rearrange("p (h e) -> p h e", h=4)
                    PM3 = PM.t[:].rearrange("p (h e) -> p h e", h=4)
                    PY3 = PY.t[:].rearrange("p (h e) -> p h e", h=4)
                    cur = 0
                    for lev in range(6):
                        Nc, Mc = Nb[cur], Mb[cur]
                        Nn, Mn = Nb[1 - cur], Mb[1 - cur]
                        for h in range(4):
                            k.op("pe", "matmul", reads=[Mc, Nc], writes=[PN], out=PN.t[:, h * 128:(h + 1) * 128], lhsT=Mc.t[:, h, :], rhs=Nc.t[:, h, :],
                                 start=True, stop=True)
                        k.op("act", "activation", reads=[PN], writes=[Nn], out=Nn.t[:], in_=PN3, func=AF.Copy)
                        if lev < 5:
                            for h in range(4):
                                k.op("pe", "matmul", reads=[Mc, Nc], writes=[PM], out=PM.t[:, h * 128:(h + 1) * 128], lhsT=Nc.t[:, h, :], rhs=Mc.t[:, h, :],
                                     start=True, stop=True)
                            k.op("dve", "tensor_copy", reads=[PM], writes=[Mn], out=Mn.t[:], in_=PM3)
                        for h in range(4):
                            k.op("pe", "matmul", reads=[Nn, Y], writes=[PY], out=PY.t[:, h * 128:(h + 1) * 128], lhsT=Nn.t[:, h, :], rhs=Y.t[:, h, :],
                                 start=True, stop=True)
                        k.op("dve", "tensor_tensor", reads=[PY, Y], writes=[Y], out=Y.t[:], in0=Y.t[:], in1=PY3, op=ALU.add)
                        cur = 1 - cur
                    if n == 0: print('ops at stage6:', k.nops)
                    if GSTOP <= 6:
                        return
                    for h in range(4):
                        hs = slice(h * 64, (h + 1) * 64)
                        k.op("pe", "matmul", reads=[Y, rhu], writes=[PX], out=PX.t[:, hs], lhsT=Y.t[:, h, :], rhs=rhu.t[:, hs], start=True, stop=True)
                    k.op("act", "activation", reads=[PX], writes=[U], out=U.t[:], in_=PX.t[:, 0:256], func=AF.Copy)
                    for h in range(4):
                        hp = h // 2
                        k.op("pe", "matmul", reads=[Y, rhw], writes=[PB], out=PB.t[:, h * 128:(h + 1) * 128], lhsT=rhw.t[:, hp * 128:(hp + 1) * 128], rhs=Y.t[:, h, :],
                             start=True, stop=True)
                    for h in range(4):
                        pr = slice((h % 2) * 64, (h % 2 + 1) * 64)
                        if h % 2 == 0:
                            k.op("act", "activation", reads=[PB], writes=[WT], out=WT.t[pr, h // 2, :], in_=PB3[pr, h, :], func=AF.Copy)
                        else:
                            k.op("dve", "tensor_copy", reads=[PB], writes=[WT], out=WT.t[pr, h // 2, :], in_=PB3[pr, h, :])
                    if n == 0: print('ops at stage7:', k.nops)
                    if GSTOP <= 7:
                        return
                    for h in range(4):
                        pr = slice((h % 2) * 64, (h % 2 + 1) * 64)
                        k.op("pe", "matmul", reads=[kc2, qT], writes=[PT_], out=PT_.t[:, h * 128:(h + 1) * 128], lhsT=kc2.t[:, h, :], rhs=qT.t[:, h // 2, cs],
                             start=True, stop=True)
                    k.op("dve", "tensor_tensor", reads=[PT_, DT], writes=[QKT], out=QKT.t[:], in0=PT3, in1=DT.t[:], op=ALU.mult)
                    for h in range(4):
                        pr = slice((h % 2) * 64, (h % 2 + 1) * 64)
                        k.op("dve", "tensor_tensor", reads=[qT, Eg], writes=[qe], out=qe.t[pr, h // 2, :], in0=qT.t[pr, h // 2, cs],
                             in1=Eg.t[pr, h, :], op=ALU.mult)
                    if n == 0: print('ops at stage8:', k.nops)
                    if GSTOP <= 8:
                        return
                    for hp in range(2):
                        k.op("pe", "matmul", reads=[WT, S], writes=[PX], out=PX.t[:, 256 + hp * 128:256 + (hp + 1) * 128], lhsT=WT.t[:, hp, :],
                             rhs=S.t[:, hp, :], start=True, stop=True)
                    k.op("dve", "tensor_tensor", reads=[U, PX], writes=[vnew], out=vnew.t[:], in0=U.t[:], in1=PX.t[:, 256:512], op=ALU.subtract)
                    for h in range(4):
                        hs = slice(h * 64, (h + 1) * 64)
                        k.op("pe", "matmul", reads=[qe, S], writes=[PN], out=PN.t[:, hs], lhsT=qe.t[:, h // 2, :],
                             rhs=S.t[:, h // 2, (h % 2) * 64:(h % 2 + 1) * 64], start=True, stop=False)
                        k.op("pe", "matmul", reads=[QKT, vnew], writes=[PN], out=PN.t[:, hs], lhsT=QKT.t[:, h, :], rhs=vnew.t[:, hs],
                             start=False, stop=True)
                    for hp in range(2):
                        k.op("pe", "matmul", reads=[kd, vnew], writes=[PM], out=PM.t[:, hp * 128:(hp + 1) * 128], lhsT=kd.t[:, hp * 128:(hp + 1) * 128],
                             rhs=vnew.t[:, hp * 128:(hp + 1) * 128], start=True, stop=True)
                    for h in range(4):
                        pr = slice((h % 2) * 64, (h % 2 + 1) * 64)
                        cs2 = slice((h % 2) * 64, (h % 2 + 1) * 64)
                        k.op("dve", "scalar_tensor_tensor", reads=[S, Eg, PM], writes=[S], out=S.t[pr, h // 2, cs2], in0=S.t[pr, h // 2, cs2],
                             scalar=Eg.t[pr, h, 127:128], in1=PM.t[pr, (h // 2) * 128 + (h % 2) * 64:(h // 2) * 128 + (h % 2 + 1) * 64],
                             op0=ALU.mult, op1=ALU.add)
                    if n == 0: print('ops at stage9:', k.nops)
                    if GSTOP <= 9:
                        return
                    k.op("act", "activation", reads=[PN], writes=[ob], out=ob.t[:].rearrange("p h e -> p (h e)"), in_=PN.t[:, 0:256], func=AF.Copy)
                    k.op("act", "activation", reads=[ob], writes=[osq], out=osq.t[:], in_=ob.t[:], func=AF.Square)
                    k.op("dve", "tensor_reduce", reads=[osq], writes=[ors], out=ors.t[:, 0:4], in_=osq.t[:], axis=AX.X, op=ALU.add)
                    k.op("act", "activation", reads=[ors, epsb], writes=[ors], out=ors.t[:, 4:8], in_=ors.t[:, 0:4], func=AF.Sqrt, scale=1.0 / 64,
                         bias=epsb.t[:, 0:1])
                    k.op("dve", "reciprocal", reads=[ors], writes=[ors], out=ors.t[:, 4:8], in_=ors.t[:, 4:8])
                    for c in range(8):
                        k.op("pe", "matmul", reads=[s_z, hT[tgi]], writes=[PY], inc=(c == 7), out=PY.t[:, 0:256], lhsT=hT_t.t[:, c, cs], rhs=wz[:, c, :],
                             start=(c == 0), stop=(c == 7))
                    k.op("act", "activation", reads=[PY], writes=[zs], out=zs.t[:], in_=PY.t[:, 0:256], func=AF.Silu)
                    for h in range(4):
                        k.op("dve", "scalar_tensor_tensor", reads=[ob, ors, prow], writes=[osq], out=osq.t[:, h, :], in0=ob.t[:, h, :],
                             scalar=ors.t[:, 4 + h:5 + h], in1=gnw, op0=ALU.mult, op1=ALU.mult)
                    k.op("dve", "tensor_tensor", reads=[osq, zs], writes=[ytok], out=ytok.t[:], in0=osq.t[:].rearrange("p h e -> p (h e)"),
                         in1=zs.t[:], op=ALU.mult)
                    for hp in range(2):
                        k.op("pe", "matmul", reads=[ytok, cb], writes=[PK], out=PK.t[:, hp * 128:(hp + 1) * 128], lhsT=ytok.t[:, hp * 128:(hp + 1) * 128],
                             rhs=ident, start=True, stop=True)
                    k.op("act", "activation", reads=[PK], writes=[yg], out=yg.t[:, :, cs], in_=PK.t[:, 0:256].rearrange("p (a e) -> p a e", a=2),
                         func=AF.Copy)
                dbg_dump(yg, yg.t, 2, 2, sc)
                out_proj(l, yg, yg.t, 2, 2)

        for s in range(nseq):
            for g in range(4):
                k.load("sp", xT[g], xT_t.t[:, :, g * 512:(g + 1) * 512],
                       xT_d[s, :, g * 512:(g + 1) * 512].rearrange("(c p) t -> p c t", p=128))
            for l in range(depth):
                lam_init = 0.8 - 0.6 * math.exp(-0.3 * l)
                k.load("sp", pcol, pcol.t[:], pcol_d[l])
                k.load("sp", prow, prow.t[:], prow_d[l])
                phase_norm(pcol.t[:, PC_G1:PC_G1 + 8])
                if "conv" in mixers:
                    phase_conv(l)
                if "gdn" in mixers:
                    phase_gdn(l)
                if "attn" in mixers:
                    phase_attn(l, lam_init)
                if do_ffn:
                    phase_norm(pcol.t[:, PC_G2:PC_G2 + 8])
                    phase_ffn(l)
            k.load("sp", pcol, pcol.t[:], pcol_d[DEPTH])
            phase_norm(pcol.t[:, PC_G1:PC_G1 + 8], final=True, s=s)
        print("bass instructions:", k.ninst, "dma sems:", k.ndsem)
    return nc


def make_consts():
    i = np.arange(128)
    c32 = np.zeros((128, 4 * 128 + 4 * 128 + 4 * NR), np.float32)
    c32[:, 0:128] = np.where(i[:, None] > i[None, :], 0.0, NEG)
    c32[:, 128:256] = np.where(i[None, :] >= i[:, None], 0.0, NEG)
    c32[:, 256:384] = (i[:, None] <= i[None, :]).astype(np.float32)
    c32[:, 384:512] = 1.0
    c32[:, 512:640] = np.eye(128)
    for h in range(4):
        for r in range(-3, 16):
            c32[:, 1024 + h * NR + (r + 3)] = SLOPES[h] * (i - 128.0 * r)
    cb = np.zeros((128, 5 * 128), np.float32)
    cb[:, 0:128] = np.eye(128)
    cb[:, 128:256] = 1.0
    cb[0:64, 256:320] = 1.0
    cb[64:128, 320:384] = 1.0
    cb[:, 384:512] = (i[:, None] > i[None, :]).astype(np.float32)
    cb[:, 512:640] = (i[None, :] >= i[:, None]).astype(np.float32)
    return c32, cb


def make_params(inp):
    f = np.float32
    pcol = np.zeros((DEPTH + 1, 128, NPC), f)
    prow = np.zeros((DEPTH, 128, NPR), f)
    for l in range(DEPTH):
        pcol[l, :, PC_G1:PC_G1 + 8] = inp["norm1_g"][l].reshape(8, 128).T
        pcol[l, :, PC_G2:PC_G2 + 8] = inp["norm2_g"][l].reshape(8, 128).T
        pcol[l, :, PC_CB:PC_CB + 2] = inp["conv_dw_b"][l].reshape(2, 128).T
        pcol[l, :, PC_LG:PC_LG + 2] = inp["conv_ln_g"][l].reshape(2, 128).T
        pcol[l, :, PC_LB:PC_LB + 2] = inp["conv_ln_b"][l].reshape(2, 128).T
        pcol[l, :, PC_DW:PC_DW + 62] = inp["conv_dw_w"][l].reshape(31, 2, 128).transpose(2, 1, 0).reshape(128, 62)
        pcol[l, :, PC_GW:PC_GW + 24] = inp["gdn_conv_w"][l].reshape(4, 6, 128).transpose(2, 1, 0).reshape(128, 24)
        prow[l, :, PR_GNW:PR_GNW + 64] = inp["gdn_norm_w"][l][None, :]
        prow[l, :, PR_SUB:PR_SUB + 128] = inp["diff_subln_w"][l][None, :]
        prow[l, :, PR_LAM:PR_LAM + 256] = inp["diff_lambda"][l].reshape(1, 256)
        prow[l, :, PR_ALOG:PR_ALOG + 4] = inp["gdn_a_log"][l][None, :]
        prow[l, :, PR_DTB:PR_DTB + 4] = inp["gdn_dt_bias"][l][None, :]
    pcol[DEPTH, :, PC_G1:PC_G1 + 8] = inp["final_norm_g"].reshape(8, 128).T
    return pcol, prow


def kernel(**inp):
    ncores = 8
    x = np.asarray(inp["x"], np.float32)
    B = x.shape[0]
    nseq = B // ncores
    c32, cb = make_consts()
    pcol, prow = make_params({k_: np.asarray(v, np.float32) for k_, v in inp.items()})
    xT = np.ascontiguousarray(x.transpose(0, 2, 1))
    nc = build_program(nseq=nseq)
    shared = {
        "w_in": np.ascontiguousarray(inp["w_in"], dtype=np.float32),
        "w_out": np.ascontiguousarray(inp["w_out"], dtype=np.float32),
        "w_ffn_in": np.ascontiguousarray(inp["w_ffn_in"], dtype=np.float32),
        "w_ffn_out": np.ascontiguousarray(inp["w_ffn_out"], dtype=np.float32),
        "pcol": pcol, "prow": prow, "c32": c32, "cb": cb,
    }
    in_maps = [dict(shared, xT=xT[c * nseq:(c + 1) * nseq]) for c in range(ncores)]
    res = run_bass_kernel_spmd(nc, in_maps, core_ids=list(range(ncores)))
    outT = np.concatenate([r["outT"] for r in res.results], axis=0)
    return np.ascontiguousarray(outT.transpose(0, 2, 1)).astype(np.float32)
```

```python
import math
import os
GSTOP = int(os.environ.get('GSTOP', '99'))
GCUT = int(os.environ.get('GCUT', '0'))
from contextlib import ExitStack, contextmanager

import numpy as np
import concourse.bass as bass
import concourse.mybir as mybir
from concourse.bass_utils import run_bass_kernel_spmd

F32 = mybir.dt.float32
BF16 = mybir.dt.bfloat16
AF = mybir.ActivationFunctionType
ALU = mybir.AluOpType
AX = mybir.AxisListType

D = 1024
T = 2048
NT = 16
DEPTH = 4
DFF = 2816
NFC = 22
INW = 3080
RMS_EPS = 1e-6
LN_EPS = 1e-5
NEG = -30000.0
SLOPES = [(2.0 ** (-8.0 / 4)) ** (i + 1) for i in range(4)]
NR = 19

PC_G1, PC_G2, PC_CB, PC_LG, PC_LB, PC_DW, PC_GW = 0, 8, 16, 18, 20, 22, 84
NPC = 84 + 24
PR_GNW, PR_SUB, PR_LAM, PR_ALOG, PR_DTB = 0, 64, 192, 448, 452
NPR = 456


class Buf:
    __slots__ = ("t", "lw", "rs", "dsem", "dkey", "dcnt", "name", "psum")

    def __init__(self, t, name, psum=False):
        self.t = t
        self.name = name
        self.psum = psum
        self.lw = None
        self.rs = {}
        self.dsem = None
        self.dkey = None
        self.dcnt = 0

    def view(self, name=None):
        return Buf(self.t, name or self.name)


class K:
    def __init__(self, nc, es):
        self.nc = nc
        self.es = es
        self.engs = {"pe": nc.tensor, "act": nc.scalar, "dve": nc.vector, "pool": nc.gpsimd, "sp": nc.sync}
        self.semh = {}
        self.cnt = {}
        self.pend = {}
        self.seen = {}
        for e in self.engs:
            self.semh[e] = es.enter_context(nc.semaphore("s_" + e))
            self.cnt[e] = 0
            self.pend[e] = False
            self.seen[e] = {}
        self.ndsem = 0
        self.nalloc = 0
        self.ninst = 0

    def sb(self, shape, dt, name=None, es=None):
        self.nalloc += 1
        name = "sb%d_%s" % (self.nalloc, name or "t")
        t = (es or self.es).enter_context(self.nc.sbuf_tensor(name, list(shape), dt))
        return Buf(t, name)

    def ps(self, name):
        t = self.es.enter_context(self.nc.psum_tensor(name, [128, 512], F32))
        return Buf(t, name, psum=True)

    @contextmanager
    def scope(self):
        es = ExitStack()
        k = self

        class S:
            def sb(self, shape, dt, name=None):
                return k.sb(shape, dt, name, es=es)

        try:
            yield S()
            self.barrier()
        finally:
            es.close()

    def _waits(self, e, deps, skipkey=None):
        eng = self.engs[e]
        for key, v in deps.items():
            if key == skipkey:
                continue
            if key == e and e in ("pe", "sp", "pool"):
                continue
            if self.seen[e].get(key, 0) >= v:
                continue
            eng.wait_ge(self.semh[key], v)
            self.seen[e][key] = v
            self.ninst += 1

    @staticmethod
    def _deps(reads, writes, e=None):
        deps = {}
        for b in reads:
            if b.lw is not None:
                deps[b.lw[0]] = max(deps.get(b.lw[0], 0), b.lw[1])
            if b.psum:
                for key, v in b.rs.items():
                    if key != e:
                        deps[key] = max(deps.get(key, 0), v)
        for b in writes:
            if b.lw is not None:
                deps[b.lw[0]] = max(deps.get(b.lw[0], 0), b.lw[1])
            for key, v in b.rs.items():
                deps[key] = max(deps.get(key, 0), v)
        return deps

    def op(self, e, name, reads=(), writes=(), inc=True, **kw):
        self.nops = getattr(self, "nops", 0) + 1
        if GCUT and self.nops > GCUT:
            return None
        deps = self._deps(reads, writes, e)
        self._waits(e, deps)
        ins = getattr(self.engs[e], name)(**kw)
        self.ninst += 1
        idx = self.cnt[e] + 1
        if inc:
            ins.then_inc(self.semh[e], 1)
            self.cnt[e] = idx
            self.pend[e] = False
        else:
            self.pend[e] = True
        for b in reads:
            b.rs[e] = idx
        for b in writes:
            b.lw = (e, idx)
            b.rs = {}
        return ins

    def _dsem(self, b):
        if b.dsem is None:
            self.ndsem += 1
            b.dkey = "d%d_%s" % (self.ndsem, b.name)
            b.dsem = self.es.enter_context(self.nc.semaphore(b.dkey))
            self.semh[b.dkey] = b.dsem
        return b.dsem

    def load(self, q, buf, out, in_):
        sem = self._dsem(buf)
        deps = self._deps((), (buf,))
        self._waits(q, deps, skipkey=buf.dkey)
        self.engs[q].dma_start(out=out, in_=in_).then_inc(sem, 16)
        self.ninst += 1
        buf.dcnt += 16
        buf.lw = (buf.dkey, buf.dcnt)
        buf.rs = {}

    def store(self, q, buf, out, in_):
        sem = self._dsem(buf)
        deps = self._deps((buf,), ())
        self._waits(q, deps, skipkey=None)
        self.engs[q].dma_start(out=out, in_=in_).then_inc(sem, 16)
        self.ninst += 1
        buf.dcnt += 16
        buf.rs[buf.dkey] = buf.dcnt

    def barrier(self):
        assert GCUT or not any(self.pend.values()), self.pend
        for e in ("pe", "act", "dve"):
            deps = {d: self.cnt[d] for d in ("pe", "act", "dve") if d != e and self.cnt[d] > 0}
            self._waits(e, deps)

    def wait_all_stores(self, e, bufs):
        deps = {}
        for b in bufs:
            if b.dkey is not None:
                deps[b.dkey] = b.dcnt
        self._waits(e, deps)


def build_program(nseq=4, depth=DEPTH, dbg=False, mixers=("conv", "gdn", "attn"), do_ffn=True):
    nc = bass.Bass("TRN2", target_bir_lowering=False)
    dr = {}

    def din(name, shape, dt=F32):
        dr[name] = nc.dram_tensor(name, list(shape), dt, kind="ExternalInput").ap()
        return dr[name]

    xT_d = din("xT", [nseq, D, T])
    w_in_d = din("w_in", [DEPTH, D, INW])
    w_out_d = din("w_out", [DEPTH, D, D])
    w_f1_d = din("w_ffn_in", [DEPTH, D, 2 * DFF])
    w_f2_d = din("w_ffn_out", [DEPTH, DFF, D])
    pcol_d = din("pcol", [DEPTH + 1, 128, NPC])
    prow_d = din("prow", [DEPTH, 128, NPR])
    c32_d = din("c32", [128, 4 * 128 + 4 * 128 + 4 * NR])
    cb_d = din("cb", [128, 5 * 128])
    out_d = nc.dram_tensor("outT", [nseq, D, T], F32, kind="ExternalOutput").ap()
    if dbg:
        dbg_d = nc.dram_tensor("dbg", [8, 128, T], F32, kind="ExternalOutput").ap()

    with ExitStack() as es:
        k = K(nc, es)
        xT_t = k.sb([128, 8, T], F32, "xT")
        hT_t = k.sb([128, 8, T], BF16, "hT")
        xT = [xT_t.view("xT%d" % g) for g in range(4)]
        hT = [hT_t.view("hT%d" % g) for g in range(4)]
        NS = 6
        slots = [k.sb([128, 2048], BF16, "slot%d" % i) for i in range(NS)]
        c32 = k.sb([128, 4 * 128 + 4 * 128 + 4 * NR], F32, "c32")
        cb = k.sb([128, 5 * 128], BF16, "cb")
        pcol = k.sb([128, NPC], F32, "pcol")
        prow = k.sb([128, NPR], F32, "prow")
        wab = k.sb([128, 8, 8], BF16, "wab")
        P = [k.ps("P%d" % i) for i in range(8)]
        epsb = k.sb([128, 4], F32, "epsb")
        k.op("dve", "memset", writes=[epsb], ap=epsb.t[:, 0:1], constant=RMS_EPS)
        k.op("dve", "memset", writes=[epsb], ap=epsb.t[:, 1:2], constant=LN_EPS)
        k.op("dve", "memset", writes=[epsb], ap=epsb.t[:, 2:3], constant=1.0)
        k.op("dve", "memset", writes=[epsb], ap=epsb.t[:, 3:4], constant=0.0)

        k.load("sp", c32, c32.t[:], c32_d)
        k.load("pool", cb, cb.t[:], cb_d)
        maskC = c32.t[:, 0:128]
        maskU = c32.t[:, 128:256]
        triT = c32.t[:, 256:384]
        ones32 = c32.t[:, 384:512]
        ident32 = c32.t[:, 512:640]
        abias = lambda h, r: c32.t[:, 1024 + h * NR + (r + 3):1024 + h * NR + (r + 3) + 1]
        ident = cb.t[:, 0:128]
        onesb = cb.t[:, 128:256]
        blk64 = cb.t[:, 256:384]
        strict = cb.t[:, 384:512]
        triU = cb.t[:, 512:640]

        slot_i = [0]

        def next_slot():
            s = slots[slot_i[0] % NS]
            slot_i[0] += 1
            return s

        def wload(slot, dst, src):
            k.load("pool", slot, dst, src)

        def w_in_cols(l, c0, n):
            return w_in_d[l, :, c0:c0 + n].rearrange("(c p) e -> p c e", p=128)

        def proj_fm(pb, wslot, wv, m0, m, tg, act=hT, kc=8, act_t=None):
            at = act_t if act_t is not None else hT_t.t
            for c in range(kc):
                k.op("pe", "matmul", reads=[wslot, act[tg]], writes=[pb], inc=(c == kc - 1),
                     out=pb.t[0:m, :], lhsT=wv[:, c, m0:m0 + m], rhs=at[:, c, tg * 512:(tg + 1) * 512],
                     start=(c == 0), stop=(c == kc - 1))

        def phase_norm(gcol, final=False, s=0):
            with k.scope() as sc:
                sq = [sc.sb([128, 8, 512], BF16) for _ in range(2)]
                rs = [sc.sb([128, 512], F32) for _ in range(2)]
                ob = [sc.sb([128, 8, 512], F32) for _ in range(2)] if final else None
                for tg in range(4):
                    ts = slice(tg * 512, (tg + 1) * 512)
                    s_, r_, pb = sq[tg % 2], rs[tg % 2], P[tg % 2]
                    k.op("act", "activation", reads=[xT[tg]], writes=[s_],
                         out=s_.t[:], in_=xT_t.t[:, :, ts], func=AF.Square)
                    for c in range(8):
                        k.op("pe", "matmul", reads=[s_, cb], writes=[pb], inc=(c == 7),
                             out=pb.t[:], lhsT=onesb, rhs=s_.t[:, c, :], start=(c == 0), stop=(c == 7))
                    k.op("act", "activation", reads=[pb, epsb], writes=[r_],
                         out=r_.t[:], in_=pb.t[:], func=AF.Sqrt, scale=1.0 / D, bias=epsb.t[:, 0:1])
                    k.op("dve", "reciprocal", reads=[r_], writes=[r_], out=r_.t[:], in_=r_.t[:])
                    if not final:
                        for c in range(8):
                            k.op("dve", "scalar_tensor_tensor", reads=[xT[tg], r_, pcol], writes=[hT[tg]],
                                 out=hT_t.t[:, c, ts], in0=xT_t.t[:, c, ts], scalar=gcol[:, c:c + 1],
                                 in1=r_.t[:], op0=ALU.mult, op1=ALU.mult)
                    else:
                        o_ = ob[tg % 2]
                        for c in range(8):
                            k.op("dve", "scalar_tensor_tensor", reads=[xT[tg], r_, pcol], writes=[o_],
                                 out=o_.t[:, c, :], in0=xT_t.t[:, c, ts], scalar=gcol[:, c:c + 1],
                                 in1=r_.t[:], op0=ALU.mult, op1=ALU.mult)
                        k.store("sp", o_, out_d[s, :, ts].rearrange("(c p) t -> p c t", p=128), o_.t[:])
                if final:
                    k.wait_all_stores("sp", ob)

        def out_proj(l, yb, yt, r0, kc):
            sl = next_slot()
            wv = sl.t[:, 0:kc * 1024].rearrange("p (c e) -> p c e", c=kc)
            wload(sl, wv, w_out_d[l, r0 * 128:(r0 + kc) * 128, :].rearrange("(c p) e -> p c e", p=128))
            i = 0
            for dc in range(8):
                for tg in range(4):
                    pb = P[i % 2]
                    i += 1
                    ts = slice(tg * 512, (tg + 1) * 512)
                    for c in range(kc):
                        k.op("pe", "matmul", reads=[sl, yb], writes=[pb], inc=(c == kc - 1),
                             out=pb.t[:], lhsT=wv[:, c, dc * 128:(dc + 1) * 128], rhs=yt[:, c, ts],
                             start=(c == 0), stop=(c == kc - 1))
                    k.op("dve", "tensor_tensor", reads=[pb, xT[tg]], writes=[xT[tg]],
                         out=xT_t.t[:, dc, ts], in0=xT_t.t[:, dc, ts], in1=pb.t[:], op=ALU.add)

        def dbg_dump(yb, yt, c0, kc, sc):
            if not dbg:
                return
            tmp = sc.sb([128, 512], F32)
            for c in range(kc):
                for tg in range(4):
                    k.op("act", "activation", reads=[yb], writes=[tmp], out=tmp.t[:], in_=yt[:, c, tg * 512:(tg + 1) * 512], func=AF.Copy)
                    k.store("sp", tmp, dbg_d[c0 + c, :, tg * 512:(tg + 1) * 512], tmp.t[:])
            k.wait_all_stores("act", [tmp])

        def phase_conv(l):
            with k.scope() as sc:
                sv, sg = next_slot(), next_slot()
                wvv = sv.t[:].rearrange("p (c e) -> p c e", c=8)
                wgv = sg.t[:].rearrange("p (c e) -> p c e", c=8)
                wload(sv, wvv, w_in_cols(l, 0, 256))
                wload(sg, wgv, w_in_cols(l, 256, 256))
                glu = sc.sb([128, 2, 32 + T], BF16, "glu")
                yc = sc.sb([128, 2, T], BF16, "yconv")
                sig = [sc.sb([128, 512], F32) for _ in range(2)]
                k.op("dve", "memset", writes=[glu], ap=glu.t[:, :, 0:32], constant=0.0)
                n = 0
                for tg in range(4):
                    for j in range(2):
                        pv, pg = P[(n * 2) % 4], P[(n * 2 + 1) % 4]
                        sg_ = sig[n % 2]
                        n += 1
                        proj_fm(pg, sg, wgv, j * 128, 128, tg)
                        proj_fm(pv, sv, wvv, j * 128, 128, tg)
                        k.op("act", "activation", reads=[pg], writes=[sg_], out=sg_.t[:], in_=pg.t[:], func=AF.Sigmoid)
                        k.op("dve", "tensor_tensor", reads=[pv, sg_], writes=[glu],
                             out=glu.t[:, j, 32 + tg * 512:32 + (tg + 1) * 512], in0=pv.t[:], in1=sg_.t[:], op=ALU.mult)
                dg = [sc.sb([128, 128], BF16) for _ in range(4)]
                cv = [sc.sb([128, 2, 512], F32, "cv%d" % i) for i in range(4)]
                cq = [sc.sb([128, 2, 512], F32, "cq%d" % i) for i in range(4)]
                n = 0
                for j in range(2):
                    for tap in range(31):
                        d_ = dg[n % 4]
                        n += 1
                        k.op("dve", "tensor_scalar", reads=[cb, pcol], writes=[d_],
                             out=d_.t[:], in0=ident, scalar1=pcol.t[:, PC_DW + j * 31 + tap:PC_DW + j * 31 + tap + 1],
                             scalar2=None, op0=ALU.mult)
                        for tg in range(4):
                            pb = P[4 + tg]
                            k.op("pe", "matmul", reads=[d_, glu], writes=[pb], inc=(tap == 30 or tg == 3),
                                 out=pb.t[:], lhsT=d_.t[:], rhs=glu.t[:, j, 2 + tap + tg * 512:2 + tap + (tg + 1) * 512],
                                 start=(tap == 0), stop=(tap == 30))
                    for tg in range(4):
                        pb = P[4 + tg]
                        k.op("act", "activation", reads=[pb, pcol], writes=[cv[tg]],
                             out=cv[tg].t[:, j, :], in_=pb.t[:], func=AF.Identity, bias=pcol.t[:, PC_CB + j:PC_CB + j + 1])
                        k.op("act", "activation", reads=[cv[tg]], writes=[cq[tg]],
                             out=cq[tg].t[:, j, :], in_=cv[tg].t[:, j, :], func=AF.Square)
                tmp = [sc.sb([128, 512], F32) for _ in range(4)]
                for tg in range(4):
                    pm, pq = P[(tg * 2) % 4], P[(tg * 2 + 1) % 4]
                    for j in range(2):
                        k.op("pe", "matmul", reads=[cv[tg], c32], writes=[pm], inc=(j == 1),
                             out=pm.t[:], lhsT=ones32, rhs=cv[tg].t[:, j, :], start=(j == 0), stop=(j == 1))
                    for j in range(2):
                        k.op("pe", "matmul", reads=[cq[tg], c32], writes=[pq], inc=(j == 1),
                             out=pq.t[:], lhsT=ones32, rhs=cq[tg].t[:, j, :], start=(j == 0), stop=(j == 1))
                    mean, var = tmp[0], tmp[1]
                    k.op("act", "activation", reads=[pm], writes=[mean], out=mean.t[:], in_=pm.t[:], func=AF.Copy, scale=1.0 / 256)
                    k.op("act", "activation", reads=[mean], writes=[var], out=var.t[:], in_=mean.t[:], func=AF.Square)
                    k.op("dve", "scalar_tensor_tensor", reads=[pq, var], writes=[var],
                         out=var.t[:], in0=pq.t[:], scalar=1.0 / 256, in1=var.t[:], op0=ALU.mult, op1=ALU.subtract)
                    k.op("act", "activation", reads=[var, epsb], writes=[var],
                         out=var.t[:], in_=var.t[:], func=AF.Sqrt, bias=epsb.t[:, 1:2])
                    k.op("dve", "reciprocal", reads=[var], writes=[var], out=var.t[:], in_=var.t[:])
                    for j in range(2):
                        t_ = tmp[2 + j]
                        k.op("dve", "tensor_tensor", reads=[cv[tg], mean], writes=[t_],
                             out=t_.t[:], in0=cv[tg].t[:, j, :], in1=mean.t[:], op=ALU.subtract)
                        k.op("dve", "tensor_tensor", reads=[t_, var], writes=[t_],
                             out=t_.t[:], in0=t_.t[:], in1=var.t[:], op=ALU.mult)
                        k.op("act", "activation", reads=[t_, pcol], writes=[yc],
                             out=yc.t[:, j, tg * 512:(tg + 1) * 512], in_=t_.t[:], func=AF.Silu,
                             scale=pcol.t[:, PC_LG + j:PC_LG + j + 1], bias=pcol.t[:, PC_LB + j:PC_LB + j + 1])
                dbg_dump(yc, yc.t, 0, 2, sc)
                out_proj(l, yc, yc.t, 0, 2)

        def phase_attn(l, lam_init):
            scale = 64 ** -0.5
            with k.scope() as sc:
                lt = sc.sb([128, 2, 64], F32)
                ls = sc.sb([128, 2], F32)
                nlam = sc.sb([128, 1], F32)
                k.op("dve", "tensor_tensor", reads=[prow], writes=[lt], out=lt.t[:, 0, :],
                     in0=prow.t[:, PR_LAM:PR_LAM + 64], in1=prow.t[:, PR_LAM + 64:PR_LAM + 128], op=ALU.mult)
                k.op("dve", "tensor_tensor", reads=[prow], writes=[lt], out=lt.t[:, 1, :],
                     in0=prow.t[:, PR_LAM + 128:PR_LAM + 192], in1=prow.t[:, PR_LAM + 192:PR_LAM + 256], op=ALU.mult)
                k.op("dve", "tensor_reduce", reads=[lt], writes=[ls], out=ls.t[:], in_=lt.t[:], axis=AX.X, op=ALU.add)
                k.op("act", "activation", reads=[ls], writes=[ls], out=ls.t[:], in_=ls.t[:], func=AF.Exp)
                k.op("dve", "scalar_tensor_tensor", reads=[ls], writes=[nlam], out=nlam.t[:], in0=ls.t[:, 1:2],
                     scalar=-lam_init, in1=ls.t[:, 0:1], op0=ALU.add, op1=ALU.subtract)
                wrow = sc.sb([128, 128], F32)
                k.op("dve", "tensor_scalar", reads=[prow], writes=[wrow], out=wrow.t[:], in0=prow.t[:, PR_SUB:PR_SUB + 128],
                     scalar1=1.0 - lam_init, scalar2=None, op0=ALU.mult)

                qT = sc.sb([128, 2, T], BF16, "qT")
                kT = sc.sb([128, 2, T], BF16, "kT")
                V1 = sc.sb([128, NT, 2, 132], BF16, "V1")
                yd = sc.sb([128, 2, T], BF16, "ydiff")
                pt = [sc.sb([128, 512], BF16, "pt%d" % i) for i in range(3)]
                oh = [sc.sb([128, 4, 132], F32, "oh%d" % i) for i in range(2)]
                rc = sc.sb([128, 8], F32)
                ssq = sc.sb([128, 8], F32)
                ob4 = sc.sb([128, 4, 128], F32, "ob4")
                oc4 = sc.sb([128, 4, 128], F32, "oc4")
                yb4 = sc.sb([128, 4, 128], BF16, "yb4")
                for hp in range(2):
                    sq_, sk_, sv_ = next_slot(), next_slot(), next_slot()
                    views = []
                    for s_, c0 in ((sq_, 1544), (sk_, 2056), (sv_, 2568)):
                        v_ = s_.t[:].rearrange("p (c e) -> p c e", c=8)
                        wload(s_, v_, w_in_cols(l, c0 + hp * 256, 256))
                        views.append(v_)
                    wq, wk, wv = views
                    k.op("dve", "memset", writes=[V1], ap=V1.t[:, :, :, 128:129], constant=1.0)
                    n = 0
                    for hh in range(2):
                        for tg in range(4):
                            pb = P[n % 2]
                            n += 1
                            proj_fm(pb, sq_, wq, hh * 128, 128, tg)
                            k.op("act", "activation", reads=[pb], writes=[qT], out=qT.t[:, hh, tg * 512:(tg + 1) * 512],
                                 in_=pb.t[:], func=AF.Copy, scale=scale)
                            pb = P[n % 2]
                            n += 1
                            proj_fm(pb, sk_, wk, hh * 128, 128, tg)
                            k.op("dve", "tensor_copy", reads=[pb], writes=[kT], out=kT.t[:, hh, tg * 512:(tg + 1) * 512],
                                 in_=pb.t[:])
                    for tt in range(NT):
                        pb = P[tt % 2]
                        for c in range(8):
                            k.op("pe", "matmul", reads=[sv_, hT[tt // 4]], writes=[pb], inc=(c == 7),
                                 out=pb.t[:, 0:256], lhsT=hT_t.t[:, c, tt * 128:(tt + 1) * 128], rhs=wv[:, c, :],
                                 start=(c == 0), stop=(c == 7))
                        k.op("act", "activation", reads=[pb], writes=[V1], out=V1.t[:, tt, :, 0:128],
                             in_=pb.t[:, 0:256].rearrange("p (h e) -> p h e", h=2), func=AF.Copy)
                    for hh in range(2):
                        h = hp * 2 + hh
                        W = 4 if SLOPES[h] * 511 <= 40 else (2 if SLOPES[h] * 255 <= 70 else 1)
                        items = [(g, c, j) for g in range(4) for c in range(2) for j in range(4 * g + 4)]
                        acc = [P[2], P[3], P[4], P[5]]

                        def emit_qk(i):
                            g, c, j = items[i]
                            pr = slice(c * 64, (c + 1) * 64)
                            qb0 = max(j, 4 * g)
                            nq = 4 * g + 4 - qb0
                            pS = P[6 + (i % 2)]
                            k.op("pe", "matmul", reads=[kT, qT], writes=[pS],
                                 out=pS.t[:, 0:nq * 128], lhsT=kT.t[pr, hh, j * 128:(j + 1) * 128],
                                 rhs=qT.t[pr, hh, qb0 * 128:(qb0 + nq) * 128], start=True, stop=True)

                        def emit_rest(i):
                            g, c, j = items[i]
                            qb0 = max(j, 4 * g)
                            nq = 4 * g + 4 - qb0
                            pS = P[6 + (i % 2)]
                            p_ = pt[i % 3]
                            qi = 0
                            while qi < nq:
                                qb = qb0 + qi
                                ref = (qb // W) * W
                                n_ = min(nq - qi, ref + W - qb)
                                k.op("act", "activation", reads=[pS, c32], writes=[p_], out=p_.t[:, qi * 128:(qi + n_) * 128],
                                     in_=pS.t[:, qi * 128:(qi + n_) * 128], func=AF.Exp, bias=abias(h, ref - j))
                                qi += n_
                            if j >= 4 * g:
                                k.op("dve", "tensor_tensor", reads=[p_, cb], writes=[p_], out=p_.t[:, 0:128],
                                     in0=p_.t[:, 0:128], in1=triU, op=ALU.mult)
                            for qi in range(nq):
                                qb = qb0 + qi
                                a_ = acc[qb % 4]
                                k.op("pe", "matmul", reads=[p_, V1], writes=[a_], inc=(qi == nq - 1),
                                     out=a_.t[:, 0:129], lhsT=p_.t[:, qi * 128:(qi + 1) * 128],
                                     rhs=V1.t[:, j, hh, 0:129], start=(j == 0), stop=(j == qb))
                            if j == 4 * g + 3:
                                o_ = oh[c]
                                for a in range(4):
                                    if a % 2 == 0:
                                        k.op("act", "activation", reads=[acc[a]], writes=[o_], out=o_.t[:, a, 0:129],
                                             in_=acc[a].t[:, 0:129], func=AF.Copy)
                                    else:
                                        k.op("dve", "tensor_copy", reads=[acc[a]], writes=[o_], out=o_.t[:, a, 0:129],
                                             in_=acc[a].t[:, 0:129])
                                if c == 1:
                                    post(g)

                        def post(g):
                            for c in range(2):
                                k.op("dve", "reciprocal", reads=[oh[c]], writes=[rc], out=rc.t[:, c * 4:(c + 1) * 4],
                                     in_=oh[c].t[:, :, 128])
                            k.op("dve", "tensor_tensor", reads=[oh[0], rc], writes=[ob4], out=ob4.t[:], in0=oh[0].t[:, :, 0:128],
                                 in1=rc.t[:, 0:4].unsqueeze(2).to_broadcast([128, 4, 128]), op=ALU.mult)
                            k.op("dve", "tensor_tensor", reads=[oh[1], rc], writes=[oc4], out=oc4.t[:], in0=oh[1].t[:, :, 0:128],
                                 in1=rc.t[:, 4:8].unsqueeze(2).to_broadcast([128, 4, 128]), op=ALU.mult)
                            k.op("dve", "scalar_tensor_tensor", reads=[ob4, oc4, nlam], writes=[ob4], out=ob4.t[:], in0=oc4.t[:],
                                 scalar=nlam.t[:, 0:1], in1=ob4.t[:], op0=ALU.mult, op1=ALU.add)
                            k.op("dve", "tensor_tensor", reads=[ob4], writes=[oc4], out=oc4.t[:], in0=ob4.t[:], in1=ob4.t[:], op=ALU.mult)
                            k.op("dve", "tensor_reduce", reads=[oc4], writes=[ssq], out=ssq.t[:, 0:4], in_=oc4.t[:], axis=AX.X, op=ALU.add)
                            k.op("act", "activation", reads=[ssq, epsb], writes=[ssq], out=ssq.t[:, 4:8], in_=ssq.t[:, 0:4],
                                 func=AF.Ln, scale=1.0 / 128, bias=epsb.t[:, 0:1])
                            k.op("act", "activation", reads=[ssq], writes=[ssq], out=ssq.t[:, 4:8], in_=ssq.t[:, 4:8],
                                 func=AF.Exp, scale=-0.5)
                            k.op("dve", "tensor_tensor", reads=[ob4, ssq], writes=[ob4], out=ob4.t[:], in0=ob4.t[:],
                                 in1=ssq.t[:, 4:8].unsqueeze(2).to_broadcast([128, 4, 128]), op=ALU.mult)
                            k.op("dve", "tensor_tensor", reads=[ob4, wrow], writes=[yb4], out=yb4.t[:], in0=ob4.t[:],
                                 in1=wrow.t[:, :].unsqueeze(1).to_broadcast([128, 4, 128]), op=ALU.mult)
                            pT = P[g % 2]
                            for qi in range(4):
                                k.op("pe", "matmul", reads=[yb4, cb], writes=[pT], out=pT.t[:, qi * 128:(qi + 1) * 128], lhsT=yb4.t[:, qi, :],
                                     rhs=ident, start=True, stop=True)
                            k.op("act", "activation", reads=[pT], writes=[yd], out=yd.t[:, hh, g * 512:(g + 1) * 512],
                                 in_=pT.t[:], func=AF.Copy)

                        emit_qk(0)
                        for i in range(len(items)):
                            if i + 1 < len(items):
                                emit_qk(i + 1)
                            emit_rest(i)
                    dbg_dump(yd, yd.t, 4 + hp * 2, 2, sc)
                    out_proj(l, yd, yd.t, 4 + hp * 2, 2)

        def phase_ffn(l):
            with k.scope() as sc:
                actT = sc.sb([128, NFC, 1024], BF16, "actT")
                sg = [sc.sb([128, 512], BF16) for _ in range(2)]
                for half in range(2):
                    n = 0
                    for fb in range(NFC):
                        sl = next_slot()
                        wv = sl.t[:].rearrange("p (c e) -> p c e", c=8)
                        wload(sl, wv[:, :, 0:128], w_f1_d[l, :, fb * 128:(fb + 1) * 128].rearrange("(c p) e -> p c e", p=128))
                        wload(sl, wv[:, :, 128:256], w_f1_d[l, :, DFF + fb * 128:DFF + (fb + 1) * 128].rearrange("(c p) e -> p c e", p=128))
                        for t2 in range(2):
                            tg = half * 2 + t2
                            pg, pu = P[(n * 2) % 4], P[(n * 2 + 1) % 4]
                            s_ = sg[n % 2]
                            n += 1
                            proj_fm(pg, sl, wv, 0, 128, tg)
                            proj_fm(pu, sl, wv, 128, 128, tg)
                            k.op("act", "activation", reads=[pg], writes=[s_], out=s_.t[:], in_=pg.t[:], func=AF.Silu)
                            k.op("dve", "tensor_tensor", reads=[pu, s_], writes=[actT], out=actT.t[:, fb, t2 * 512:(t2 + 1) * 512],
                                 in0=pu.t[:], in1=s_.t[:], op=ALU.mult)
                    for dg_ in range(4):
                        banks = [P[4 + i] for i in range(4)]
                        for kb in range(3):
                            nk = min(8, NFC - kb * 8)
                            sl = next_slot()
                            wv = sl.t[:].rearrange("p (c e) -> p c e", c=8)
                            wload(sl, wv[:, 0:nk, :], w_f2_d[l, kb * 1024:kb * 1024 + nk * 128, dg_ * 256:(dg_ + 1) * 256]
                                  .rearrange("(c p) e -> p c e", p=128))
                            for ci in range(nk):
                                fc = kb * 8 + ci
                                for dc2 in range(2):
                                    for t2 in range(2):
                                        pb = banks[dc2 * 2 + t2]
                                        k.op("pe", "matmul", reads=[sl, actT], writes=[pb], inc=(fc == NFC - 1 or (ci == nk - 1 and dc2 == 1 and t2 == 1)),
                                             out=pb.t[:], lhsT=wv[:, ci, dc2 * 128:(dc2 + 1) * 128],
                                             rhs=actT.t[:, fc, t2 * 512:(t2 + 1) * 512], start=(fc == 0), stop=(fc == NFC - 1))
                        for dc2 in range(2):
                            for t2 in range(2):
                                pb = banks[dc2 * 2 + t2]
                                tg = half * 2 + t2
                                dc = dg_ * 2 + dc2
                                ts = slice(tg * 512, (tg + 1) * 512)
                                k.op("dve", "tensor_tensor", reads=[pb, xT[tg]], writes=[xT[tg]],
                                     out=xT_t.t[:, dc, ts], in0=xT_t.t[:, dc, ts], in1=pb.t[:], op=ALU.add)

        def phase_gdn(l):
            with k.scope() as sc:
                s_q, s_k, s_v, s_z = next_slot(), next_slot(), next_slot(), next_slot()
                wviews = []
                for s_, c0 in ((s_q, 512), (s_k, 768), (s_v, 1024), (s_z, 1280)):
                    v_ = s_.t[:].rearrange("p (c e) -> p c e", c=8)
                    wload(s_, v_, w_in_cols(l, c0, 256))
                    wviews.append(v_)
                wq, wk, wv, wz = wviews
                k.load("pool", wab, wab.t[:], w_in_cols(l, 1536, 8))
                abp = P[0]
                for tt in range(NT):
                    for c in range(8):
                        k.op("pe", "matmul", reads=[wab, hT[tt // 4]], writes=[abp], inc=(c == 7),
                             out=abp.t[:, tt * 8:(tt + 1) * 8], lhsT=hT_t.t[:, c, tt * 128:(tt + 1) * 128], rhs=wab.t[:, c, :],
                             start=(c == 0), stop=(c == 7))
                abv = abp.t[:, 0:128].rearrange("p (t e) -> p t e", e=8)
                gt = sc.sb([128, NT, 4], F32, "gt")
                bt = sc.sb([128, NT, 4], F32, "bt")
                nbt = sc.sb([128, NT, 4], F32, "nbt")
                gc = sc.sb([128, NT, 4], F32, "gc")
                ed = sc.sb([128, NT, 4], F32, "ed")
                bew = sc.sb([128, NT, 4], F32, "bew")
                na = sc.sb([128, 4], F32, "na")
                for h in range(4):
                    k.op("dve", "tensor_scalar", reads=[abp, prow], writes=[gt], out=gt.t[:, :, h], in0=abv[:, :, h],
                         scalar1=prow.t[:, PR_DTB + h:PR_DTB + h + 1], scalar2=None, op0=ALU.add)
                k.op("act", "activation", reads=[gt], writes=[gt], out=gt.t[:], in_=gt.t[:], func=AF.Exp)
                k.op("act", "activation", reads=[gt, epsb], writes=[gt], out=gt.t[:], in_=gt.t[:], func=AF.Ln, bias=epsb.t[:, 2:3])
                k.op("act", "activation", reads=[prow], writes=[na], out=na.t[:], in_=prow.t[:, PR_ALOG:PR_ALOG + 4], func=AF.Exp)
                for h in range(4):
                    k.op("dve", "tensor_scalar", reads=[gt, na], writes=[gt], out=gt.t[:, :, h], in0=gt.t[:, :, h],
                         scalar1=na.t[:, h:h + 1], scalar2=-1.0, op0=ALU.mult, op1=ALU.mult)
                k.op("act", "activation", reads=[abp], writes=[bt], out=bt.t[:], in_=abv[:, :, 4:8], func=AF.Sigmoid)
                k.op("dve", "tensor_scalar", reads=[bt], writes=[nbt], out=nbt.t[:], in0=bt.t[:], scalar1=-1.0, scalar2=None, op0=ALU.mult)
                gflat = gt.t[:].rearrange("p t h -> p (t h)")
                pcs = P[1]
                k.op("pe", "matmul", reads=[gt, c32], writes=[pcs], out=pcs.t[:, 0:64], lhsT=triT, rhs=gflat, start=True, stop=True)
                k.op("pe", "matmul", reads=[gt, c32], writes=[pcs], out=pcs.t[:, 64:128], lhsT=ones32, rhs=gflat, start=True, stop=True)
                gcf = gc.t[:].rearrange("p t h -> p (t h)")
                edf = ed.t[:].rearrange("p t h -> p (t h)")
                bewf = bew.t[:].rearrange("p t h -> p (t h)")
                k.op("act", "activation", reads=[pcs], writes=[gc], out=gcf, in_=pcs.t[:, 0:64], func=AF.Copy)
                k.op("dve", "tensor_tensor", reads=[pcs, gc], writes=[ed], out=edf, in0=pcs.t[:, 64:128], in1=gcf, op=ALU.subtract)
                k.op("act", "activation", reads=[ed], writes=[ed], out=edf, in_=edf, func=AF.Exp)
                k.op("act", "activation", reads=[gc], writes=[bew], out=bewf, in_=gcf, func=AF.Exp)
                k.op("dve", "tensor_tensor", reads=[bew, bt], writes=[bew], out=bewf, in0=bewf, in1=bt.t[:].rearrange("p t h -> p (t h)"), op=ALU.mult)

                print("ops at stage1:", k.nops)
                if GSTOP <= 1:
                    return
                qT = sc.sb([128, 2, T], BF16, "gq")
                kT = sc.sb([128, 2, T], BF16, "gk")
                vT = sc.sb([128, 2, T], BF16, "gv")
                yg = sc.sb([128, 2, T], BF16, "yg")
                with k.scope() as sc2:
                    ut = [sc2.sb([128, 4 + T], BF16, "ut%d" % i) for i in range(2)]
                    dgs = [sc2.sb([128, 128], BF16) for _ in range(4)]
                    sl32 = [sc2.sb([128, 512], F32) for _ in range(2)]
                    sqb = [sc2.sb([128, 512], BF16) for _ in range(2)]
                    rn = [sc2.sb([128, 512], F32) for _ in range(2)]
                    for u_ in ut:
                        k.op("dve", "memset", writes=[u_], ap=u_.t[:, 0:4], constant=0.0)
                    n = 0
                    for kind, (slot_, wv_, dst) in enumerate(((s_q, wq, qT), (s_k, wk, kT), (s_v, wv, vT))):
                        for hp in range(2):
                            ch = kind * 2 + hp
                            u_ = ut[(kind * 2 + hp) % 2]
                            for tg in range(4):
                                pb = P[tg % 2]
                                proj_fm(pb, slot_, wv_, hp * 128, 128, tg)
                                k.op("act", "activation", reads=[pb], writes=[u_], out=u_.t[:, 4 + tg * 512:4 + (tg + 1) * 512],
                                     in_=pb.t[:], func=AF.Copy)
                            for tap in range(4):
                                k.op("dve", "tensor_scalar", reads=[cb, pcol], writes=[dgs[tap]], out=dgs[tap].t[:], in0=ident,
                                     scalar1=pcol.t[:, PC_GW + ch * 4 + tap:PC_GW + ch * 4 + tap + 1], scalar2=None, op0=ALU.mult)
                            for tg in range(4):
                                pb = P[4 + tg]
                                ts = slice(tg * 512, (tg + 1) * 512)
                                for tap in range(4):
                                    k.op("pe", "matmul", reads=[dgs[tap], u_], writes=[pb], inc=(tap == 3), out=pb.t[:], lhsT=dgs[tap].t[:],
                                         rhs=u_.t[:, 1 + tap + tg * 512:1 + tap + (tg + 1) * 512], start=(tap == 0), stop=(tap == 3))
                                if kind == 2:
                                    k.op("act", "activation", reads=[pb], writes=[dst], out=dst.t[:, hp, ts], in_=pb.t[:], func=AF.Silu)
                                else:
                                    s32, sq_, rn_ = sl32[n % 2], sqb[n % 2], rn[n % 2]
                                    pn = P[2 + n % 2]
                                    n += 1
                                    k.op("act", "activation", reads=[pb], writes=[s32], out=s32.t[:], in_=pb.t[:], func=AF.Silu)
                                    k.op("act", "activation", reads=[s32], writes=[sq_], out=sq_.t[:], in_=s32.t[:], func=AF.Square)
                                    k.op("pe", "matmul", reads=[sq_, cb], writes=[pn], out=pn.t[:], lhsT=blk64, rhs=sq_.t[:], start=True, stop=True)
                                    k.op("act", "activation", reads=[pn, epsb], writes=[rn_], out=rn_.t[:], in_=pn.t[:], func=AF.Sqrt, bias=epsb.t[:, 0:1])
                                    k.op("dve", "reciprocal", reads=[rn_], writes=[rn_], out=rn_.t[:], in_=rn_.t[:])
                                    k.op("dve", "scalar_tensor_tensor", reads=[s32, rn_], writes=[dst], out=dst.t[:, hp, ts], in0=s32.t[:],
                                         scalar=(0.125 if kind == 0 else 1.0), in1=rn_.t[:], op0=ALU.mult, op1=ALU.mult)

                print("ops at stage2:", k.nops)
                if GSTOP <= 2:
                    return
                S = sc.sb([128, 2, 128], F32, "S")
                k.op("dve", "memset", writes=[S], ap=S.t[:], constant=0.0)
                Gb = sc.sb([128, 4, 128], F32, "Gb")
                Gm = sc.sb([128, 4, 128], F32, "Gm")
                Gm2 = sc.sb([128, 4, 128], F32, "Gm2")
                Dc = sc.sb([128, 4, 128], F32, "Dc")
                DT = sc.sb([128, 4, 128], BF16, "DT")
                Eg = sc.sb([128, 4, 128], F32, "Eg")
                Nb = [sc.sb([128, 4, 128], F32, "Nb%d" % i) for i in range(2)]
                Mb = [sc.sb([128, 4, 128], F32, "Mb%d" % i) for i in range(2)]
                Y = sc.sb([128, 4, 128], F32, "Y")
                QKT = sc.sb([128, 4, 128], BF16, "QKT")
                U = sc.sb([128, 256], F32, "U")
                WT = sc.sb([128, 2, 128], F32, "WT")
                qe = sc.sb([128, 2, 128], F32, "qe")
                kd = sc.sb([128, 256], BF16, "kd")
                rhu = sc.sb([128, 256], F32, "rhu")
                rhw = sc.sb([128, 256], F32, "rhw")
                vnew = sc.sb([128, 256], BF16, "vnew")
                ob = sc.sb([128, 4, 64], F32, "ob")
                osq = sc.sb([128, 4, 64], F32, "osq")
                ors = sc.sb([128, 8], F32, "ors")
                zs = sc.sb([128, 256], F32, "zs")
                ytok = sc.sb([128, 256], BF16, "ytok")
                kc2 = sc.sb([128, 4, 128], BF16, "kc2")
                kc2v = kc2.t[:].rearrange("p (a b) e -> p a b e", a=2)
                k.op("dve", "memset", writes=[kc2], ap=kc2.t[:], constant=0.0)
                gnw = prow.t[:, PR_GNW:PR_GNW + 64]
                for n in range(NT):
                    cs = slice(n * 128, (n + 1) * 128)
                    tgi = n // 4
                    PA, PB, PT_, PN, PM, PY, PK, PX = P
                    for hp in range(2):
                        k.op("pe", "matmul", reads=[kT, cb], writes=[PK], out=PK.t[:, hp * 128:(hp + 1) * 128], lhsT=kT.t[:, hp, cs], rhs=ident,
                             start=True, stop=True)
                        k.op("pe", "matmul", reads=[vT, cb], writes=[PK], out=PK.t[:, 256 + hp * 128:256 + (hp + 1) * 128], lhsT=vT.t[:, hp, cs],
                             rhs=ident, start=True, stop=True)
                    for h in range(4):
                        hs = slice(h * 64, (h + 1) * 64)
                        k.op("dve", "tensor_scalar", reads=[PK, bew], writes=[rhw], out=rhw.t[:, hs], in0=PK.t[:, hs],
                             scalar1=bew.t[:, n, h:h + 1], scalar2=None, op0=ALU.mult)
                        k.op("dve", "tensor_scalar", reads=[PK, ed], writes=[kd], out=kd.t[:, hs], in0=PK.t[:, hs],
                             scalar1=ed.t[:, n, h:h + 1], scalar2=None, op0=ALU.mult)
                        k.op("dve", "tensor_scalar", reads=[PK, bt], writes=[rhu], out=rhu.t[:, hs], in0=PK.t[:, 256 + h * 64:256 + (h + 1) * 64],
                             scalar1=bt.t[:, n, h:h + 1], scalar2=None, op0=ALU.mult)
                    if n == 0: print('ops at stage3:', k.nops)
                    if GSTOP <= 3:
                        return
                    for h in range(4):
                        k.op("dve", "tensor_scalar", reads=[c32, gt], writes=[Gb], out=Gb.t[:, h, :], in0=ones32,
                             scalar1=gt.t[:, n, h:h + 1], scalar2=-1.0, op0=ALU.mult, op1=ALU.mult)
                        k.op("pe", "matmul", reads=[Gb, c32], writes=[PA], out=PA.t[:, h * 128:(h + 1) * 128], lhsT=Gb.t[:, h, :], rhs=triT,
                             start=True, stop=True)
                    PA3 = PA.t[:].rearrange("p (h e) -> p h e", h=4)
                    for h in range(4):
                        k.op("dve", "scalar_tensor_tensor", reads=[PA, gc, c32], writes=[Gm], out=Gm.t[:, h, :], in0=PA3[:, h, :],
                             scalar=gc.t[:, n, h:h + 1], in1=maskC, op0=ALU.add, op1=ALU.add)
                        k.op("dve", "scalar_tensor_tensor", reads=[PA, gc, c32], writes=[Gm2], out=Gm2.t[:, h, :], in0=PA3[:, h, :],
                             scalar=gc.t[:, n, h:h + 1], in1=maskU, op0=ALU.add, op1=ALU.subtract)
                    k.op("act", "activation", reads=[Gm], writes=[Dc], out=Dc.t[:], in_=Gm.t[:], func=AF.Exp)
                    k.op("act", "activation", reads=[Gm2], writes=[DT], out=DT.t[:], in_=Gm2.t[:], func=AF.Exp, scale=-1.0)
                    k.op("act", "activation", reads=[PA], writes=[Eg], out=Eg.t[:], in_=PA3, func=AF.Exp, scale=-1.0)
                    if n == 0: print('ops at stage4:', k.nops)
                    if GSTOP <= 4:
                        return
                    N_, M_ = Nb[0], Mb[0]
                    PB3 = PB.t[:].rearrange("p (h e) -> p h e", h=4)
                    for h in range(4):
                        pr = slice((h % 2) * 64, (h % 2 + 1) * 64)
                        if h == 0:
                            k.op("act", "activation", reads=[kT], writes=[kc2], out=kc2v[0:64, :, 0, :], in_=kT.t[0:64, :, cs], func=AF.Copy)
                            k.op("act", "activation", reads=[kT], writes=[kc2], out=kc2v[64:128, :, 1, :], in_=kT.t[64:128, :, cs], func=AF.Copy)
                        k.op("pe", "matmul", reads=[kT, kc2], writes=[PB], out=PB.t[:, h * 128:(h + 1) * 128], lhsT=kT.t[:, h // 2, cs], rhs=kc2.t[:, h, :],
                             start=True, stop=True)
                    for h in range(4):
                        k.op("dve", "scalar_tensor_tensor", reads=[PB, nbt, Dc], writes=[N_], out=N_.t[:, h, :], in0=PB3[:, h, :],
                             scalar=nbt.t[:, n, h:h + 1], in1=Dc.t[:, h, :], op0=ALU.mult, op1=ALU.mult)
                    PT3 = PT_.t[:].rearrange("p (h e) -> p h e", h=4)
                    for h in range(4):
                        k.op("pe", "matmul", reads=[N_, c32], writes=[PT_], out=PT_.t[:, h * 128:(h + 1) * 128], lhsT=N_.t[:, h, :], rhs=ident32, start=True, stop=True)
                    k.op("act", "activation", reads=[PT_], writes=[M_], out=M_.t[:], in_=PT3, func=AF.Copy)
                    for h in range(4):
                        k.op("dve", "tensor_tensor", reads=[PT_, c32], writes=[Y], out=Y.t[:, h, :], in0=PT3[:, h, :], in1=ident32, op=ALU.add)
                    if n == 0: print('ops at stage5:', k.nops)
                    if GSTOP <= 5:
                        return
                    PN3 = PN.t[:].rearrange("p (h e) -> p h e", h=4)
                    PM3 = PM.t[:].rearrange("p (h e) -> p h e", h=4)
                    PY3 = PY.t[:].rearrange("p (h e) -> p h e", h=4)
                    cur = 0
                    for lev in range(6):
                        Nc, Mc = Nb[cur], Mb[cur]
                        Nn, Mn = Nb[1 - cur], Mb[1 - cur]
                        for h in range(4):
                            k.op("pe", "matmul", reads=[Mc, Nc], writes=[PN], out=PN.t[:, h * 128:(h + 1) * 128], lhsT=Mc.t[:, h, :], rhs=Nc.t[:, h, :],
                                 start=True, stop=True)
                        k.op("act", "activation", reads=[PN], writes=[Nn], out=Nn.t[:], in_=PN3, func=AF.Copy)
                        if lev < 5:
                            for h in range(4):
                                k.op("pe", "matmul", reads=[Mc, Nc], writes=[PM], out=PM.t[:, h * 128:(h + 1) * 128], lhsT=Nc.t[:, h, :], rhs=Mc.t[:, h, :],
                                     start=True, stop=True)
                            k.op("dve", "tensor_copy", reads=[PM], writes=[Mn], out=Mn.t[:], in_=PM3)
                        for h in range(4):
                            k.op("pe", "matmul", reads=[Nn, Y], writes=[PY], out=PY.t[:, h * 128:(h + 1) * 128], lhsT=Nn.t[:, h, :], rhs=Y.t[:, h, :],
                                 start=True, stop=True)
                        k.op("dve", "tensor_tensor", reads=[PY, Y], writes=[Y], out=Y.t[:], in0=Y.t[:], in1=PY3, op=ALU.add)
                        cur = 1 - cur
                    if n == 0: print('ops at stage6:', k.nops)
                    if GSTOP <= 6:
                        return
                    for h in range(4):
                        hs = slice(h * 64, (h + 1) * 64)
                        k.op("pe", "matmul", reads=[Y, rhu], writes=[PX], out=PX.t[:, hs], lhsT=Y.t[:, h, :], rhs=rhu.t[:, hs], start=True, stop=True)
                    k.op("act", "activation", reads=[PX], writes=[U], out=U.t[:], in_=PX.t[:, 0:256], func=AF.Copy)
                    for h in range(4):
                        hp = h // 2
                        k.op("pe", "matmul", reads=[Y, rhw], writes=[PB], out=PB.t[:, h * 128:(h + 1) * 128], lhsT=rhw.t[:, hp * 128:(hp + 1) * 128], rhs=Y.t[:, h, :],
                             start=True, stop=True)
                    for h in range(4):
                        pr = slice((h % 2) * 64, (h % 2 + 1) * 64)
                        if h % 2 == 0:
                            k.op("act", "activation", reads=[PB], writes=[WT], out=WT.t[pr, h // 2, :], in_=PB3[pr, h, :], func=AF.Copy)
                        else:
                            k.op("dve", "tensor_copy", reads=[PB], writes=[WT], out=WT.t[pr, h // 2, :], in_=PB3[pr, h, :])
                    if n == 0: print('ops at stage7:', k.nops)
                    if GSTOP <= 7:
                        return
                    for h in range(4):
                        pr = slice((h % 2) * 64, (h % 2 + 1) * 64)
                        k.op("pe", "matmul", reads=[kc2, qT], writes=[PT_], out=PT_.t[:, h * 128:(h + 1) * 128], lhsT=kc2.t[:, h, :], rhs=qT.t[:, h // 2, cs],
                             start=True, stop=True)
                    k.op("dve", "tensor_tensor", reads=[PT_, DT], writes=[QKT], out=QKT.t[:], in0=PT3, in1=DT.t[:], op=ALU.mult)
                    for h in range(4):
                        pr = slice((h % 2) * 64, (h % 2 + 1) * 64)
                        k.op("dve", "tensor_tensor", reads=[qT, Eg], writes=[qe], out=qe.t[pr, h // 2, :], in0=qT.t[pr, h // 2, cs],
                             in1=Eg.t[pr, h, :], op=ALU.mult)
                    if n == 0: print('ops at stage8:', k.nops)
                    if GSTOP <= 8:
                        return
                    for hp in range(2):
                        k.op("pe", "matmul", reads=[WT, S], writes=[PX], out=PX.t[:, 256 + hp * 128:256 + (hp + 1) * 128], lhsT=WT.t[:, hp, :],
                             rhs=S.t[:, hp, :], start=True, stop=True)
                    k.op("dve", "tensor_tensor", reads=[U, PX], writes=[vnew], out=vnew.t[:], in0=U.t[:], in1=PX.t[:, 256:512], op=ALU.subtract)
                    for h in range(4):
                        hs = slice(h * 64, (h + 1) * 64)
                        k.op("pe", "matmul", reads=[qe, S], writes=[PN], out=PN.t[:, hs], lhsT=qe.t[:, h // 2, :],
                             rhs=S.t[:, h // 2, (h % 2) * 64:(h % 2 + 1) * 64], start=True, stop=False)
                        k.op("pe", "matmul", reads=[QKT, vnew], writes=[PN], out=PN.t[:, hs], lhsT=QKT.t[:, h, :], rhs=vnew.t[:, hs],
                             start=False, stop=True)
                    for hp in range(2):
                        k.op("pe", "matmul", reads=[kd, vnew], writes=[PM], out=PM.t[:, hp * 128:(hp + 1) * 128], lhsT=kd.t[:, hp * 128:(hp + 1) * 128],
                             rhs=vnew.t[:, hp * 128:(hp + 1) * 128], start=True, stop=True)
                    for h in range(4):
                        pr = slice((h % 2) * 64, (h % 2 + 1) * 64)
                        cs2 = slice((h % 2) * 64, (h % 2 + 1) * 64)
                        k.op("dve", "scalar_tensor_tensor", reads=[S, Eg, PM], writes=[S], out=S.t[pr, h // 2, cs2], in0=S.t[pr, h // 2, cs2],
                             scalar=Eg.t[pr, h, 127:128], in1=PM.t[pr, (h // 2) * 128 + (h % 2) * 64:(h // 2) * 128 + (h % 2 + 1) * 64],
                             op0=ALU.mult, op1=ALU.add)
                    if n == 0: print('ops at stage9:', k.nops)
                    if GSTOP <= 9:
                        return
                    k.op("act", "activation", reads=[PN], writes=[ob], out=ob.t[:].rearrange("p h e -> p (h e)"), in_=PN.t[:, 0:256], func=AF.Copy)
                    k.op("act", "activation", reads=[ob], writes=[osq], out=osq.t[:], in_=ob.t[:], func=AF.Square)
                    k.op("dve", "tensor_reduce", reads=[osq], writes=[ors], out=ors.t[:, 0:4], in_=osq.t[:], axis=AX.X, op=ALU.add)
                    k.op("act", "activation", reads=[ors, epsb], writes=[ors], out=ors.t[:, 4:8], in_=ors.t[:, 0:4], func=AF.Sqrt, scale=1.0 / 64,
                         bias=epsb.t[:, 0:1])
                    k.op("dve", "reciprocal", reads=[ors], writes=[ors], out=ors.t[:, 4:8], in_=ors.t[:, 4:8])
                    for c in range(8):
                        k.op("pe", "matmul", reads=[s_z, hT[tgi]], writes=[PY], inc=(c == 7), out=PY.t[:, 0:256], lhsT=hT_t.t[:, c, cs], rhs=wz[:, c, :],
                             start=(c == 0), stop=(c == 7))
                    k.op("act", "activation", reads=[PY], writes=[zs], out=zs.t[:], in_=PY.t[:, 0:256], func=AF.Silu)
                    for h in range(4):
                        k.op("dve", "scalar_tensor_tensor", reads=[ob, ors, prow], writes=[osq], out=osq.t[:, h, :], in0=ob.t[:, h, :],
                             scalar=ors.t[:, 4 + h:5 + h], in1=gnw, op0=ALU.mult, op1=ALU.mult)
                    k.op("dve", "tensor_tensor", reads=[osq, zs], writes=[ytok], out=ytok.t[:], in0=osq.t[:].rearrange("p h e -> p (h e)"),
                         in1=zs.t[:], op=ALU.mult)
                    for hp in range(2):
                        k.op("pe", "matmul", reads=[ytok, cb], writes=[PK], out=PK.t[:, hp * 128:(hp + 1) * 128], lhsT=ytok.t[:, hp * 128:(hp + 1) * 128],
                             rhs=ident, start=True, stop=True)
                    k.op("act", "activation", reads=[PK], writes=[yg], out=yg.t[:, :, cs], in_=PK.t[:, 0:256].rearrange("p (a e) -> p a e", a=2),
                         func=AF.Copy)
                dbg_dump(yg, yg.t, 2, 2, sc)
                out_proj(l, yg, yg.t, 2, 2)

        for s in range(nseq):
            for g in range(4):
                k.load("sp", xT[g], xT_t.t[:, :, g * 512:(g + 1) * 512],
                       xT_d[s, :, g * 512:(g + 1) * 512].rearrange("(c p) t -> p c t", p=128))
            for l in range(depth):
                lam_init = 0.8 - 0.6 * math.exp(-0.3 * l)
                k.load("sp", pcol, pcol.t[:], pcol_d[l])
                k.load("sp", prow, prow.t[:], prow_d[l])
                phase_norm(pcol.t[:, PC_G1:PC_G1 + 8])
                if "conv" in mixers:
                    phase_conv(l)
                if "gdn" in mixers:
                    phase_gdn(l)
                if "attn" in mixers:
                    phase_attn(l, lam_init)
                if do_ffn:
                    phase_norm(pcol.t[:, PC_G2:PC_G2 + 8])
                    phase_ffn(l)
            k.load("sp", pcol, pcol.t[:], pcol_d[DEPTH])
            phase_norm(pcol.t[:, PC_G1:PC_G1 + 8], final=True, s=s)
        print("bass instructions:", k.ninst, "dma sems:", k.ndsem)
    return nc


def make_consts():
    i = np.arange(128)
    c32 = np.zeros((128, 4 * 128 + 4 * 128 + 4 * NR), np.float32)
    c32[:, 0:128] = np.where(i[:, None] > i[None, :], 0.0, NEG)
    c32[:, 128:256] = np.where(i[None, :] >= i[:, None], 0.0, NEG)
    c32[:, 256:384] = (i[:, None] <= i[None, :]).astype(np.float32)
    c32[:, 384:512] = 1.0
    c32[:, 512:640] = np.eye(128)
    for h in range(4):
        for r in range(-3, 16):
            c32[:, 1024 + h * NR + (r + 3)] = SLOPES[h] * (i - 128.0 * r)
    cb = np.zeros((128, 5 * 128), np.float32)
    cb[:, 0:128] = np.eye(128)
    cb[:, 128:256] = 1.0
    cb[0:64, 256:320] = 1.0
    cb[64:128, 320:384] = 1.0
    cb[:, 384:512] = (i[:, None] > i[None, :]).astype(np.float32)
    cb[:, 512:640] = (i[None, :] >= i[:, None]).astype(np.float32)
    return c32, cb


def make_params(inp):
    f = np.float32
    pcol = np.zeros((DEPTH + 1, 128, NPC), f)
    prow = np.zeros((DEPTH, 128, NPR), f)
    for l in range(DEPTH):
        pcol[l, :, PC_G1:PC_G1 + 8] = inp["norm1_g"][l].reshape(8, 128).T
        pcol[l, :, PC_G2:PC_G2 + 8] = inp["norm2_g"][l].reshape(8, 128).T
        pcol[l, :, PC_CB:PC_CB + 2] = inp["conv_dw_b"][l].reshape(2, 128).T
        pcol[l, :, PC_LG:PC_LG + 2] = inp["conv_ln_g"][l].reshape(2, 128).T
        pcol[l, :, PC_LB:PC_LB + 2] = inp["conv_ln_b"][l].reshape(2, 128).T
        pcol[l, :, PC_DW:PC_DW + 62] = inp["conv_dw_w"][l].reshape(31, 2, 128).transpose(2, 1, 0).reshape(128, 62)
        pcol[l, :, PC_GW:PC_GW + 24] = inp["gdn_conv_w"][l].reshape(4, 6, 128).transpose(2, 1, 0).reshape(128, 24)
        prow[l, :, PR_GNW:PR_GNW + 64] = inp["gdn_norm_w"][l][None, :]
        prow[l, :, PR_SUB:PR_SUB + 128] = inp["diff_subln_w"][l][None, :]
        prow[l, :, PR_LAM:PR_LAM + 256] = inp["diff_lambda"][l].reshape(1, 256)
        prow[l, :, PR_ALOG:PR_ALOG + 4] = inp["gdn_a_log"][l][None, :]
        prow[l, :, PR_DTB:PR_DTB + 4] = inp["gdn_dt_bias"][l][None, :]
    pcol[DEPTH, :, PC_G1:PC_G1 + 8] = inp["final_norm_g"].reshape(8, 128).T
    return pcol, prow


def kernel(**inp):
    ncores = 8
    x = np.asarray(inp["x"], np.float32)
    B = x.shape[0]
    nseq = B // ncores
    c32, cb = make_consts()
    pcol, prow = make_params({k_: np.asarray(v, np.float32) for k_, v in inp.items()})
    xT = np.ascontiguousarray(x.transpose(0, 2, 1))
    nc = build_program(nseq=nseq)
    shared = {
        "w_in": np.ascontiguousarray(inp["w_in"], dtype=np.float32),
        "w_out": np.ascontiguousarray(inp["w_out"], dtype=np.float32),
        "w_ffn_in": np.ascontiguousarray(inp["w_ffn_in"], dtype=np.float32),
        "w_ffn_out": np.ascontiguousarray(inp["w_ffn_out"], dtype=np.float32),
        "pcol": pcol, "prow": prow, "c32": c32, "cb": cb,
    }
    in_maps = [dict(shared, xT=xT[c * nseq:(c + 1) * nseq]) for c in range(ncores)]
    res = run_bass_kernel_spmd(nc, in_maps, core_ids=list(range(ncores)))
    outT = np.concatenate([r["outT"] for r in res.results], axis=0)
    return np.ascontiguousarray(outT.transpose(0, 2, 1)).astype(np.float32)
```

```python
import math
import os
GSTOP = int(os.environ.get('GSTOP', '99'))
GCUT = int(os.environ.get('GCUT', '0'))
from contextlib import ExitStack, contextmanager

import numpy as np
import concourse.bass as bass
import concourse.mybir as mybir
from concourse.bass_utils import run_bass_kernel_spmd

F32 = mybir.dt.float32
BF16 = mybir.dt.bfloat16
AF = mybir.ActivationFunctionType
ALU = mybir.AluOpType
AX = mybir.AxisListType

D = 1024
T = 2048
NT = 16
DEPTH = 4
DFF = 2816
NFC = 22
INW = 3080
RMS_EPS = 1e-6
LN_EPS = 1e-5
NEG = -30000.0
SLOPES = [(2.0 ** (-8.0 / 4)) ** (i + 1) for i in range(4)]
NR = 19

PC_G1, PC_G2, PC_CB, PC_LG, PC_LB, PC_DW, PC_GW = 0, 8, 16, 18, 20, 22, 84
NPC = 84 + 24
PR_GNW, PR_SUB, PR_LAM, PR_ALOG, PR_DTB = 0, 64, 192, 448, 452
NPR = 456


class Buf:
    __slots__ = ("t", "lw", "rs", "dsem", "dkey", "dcnt", "name", "psum")

    def __init__(self, t, name, psum=False):
        self.t = t
        self.name = name
        self.psum = psum
        self.lw = None
        self.rs = {}
        self.dsem = None
        self.dkey = None
        self.dcnt = 0

    def view(self, name=None):
        return Buf(self.t, name or self.name)


class K:
    def __init__(self, nc, es):
        self.nc = nc
        self.es = es
        self.engs = {"pe": nc.tensor, "act": nc.scalar, "dve": nc.vector, "pool": nc.gpsimd, "sp": nc.sync}
        self.semh = {}
        self.cnt = {}
        self.pend = {}
        self.seen = {}
        for e in self.engs:
            self.semh[e] = es.enter_context(nc.semaphore("s_" + e))
            self.cnt[e] = 0
            self.pend[e] = False
            self.seen[e] = {}
        self.ndsem = 0
        self.nalloc = 0
        self.ninst = 0

    def sb(self, shape, dt, name=None, es=None):
        self.nalloc += 1
        name = "sb%d_%s" % (self.nalloc, name or "t")
        t = (es or self.es).enter_context(self.nc.sbuf_tensor(name, list(shape), dt))
        return Buf(t, name)

    def ps(self, name):
        t = self.es.enter_context(self.nc.psum_tensor(name, [128, 512], F32))
        return Buf(t, name, psum=True)

    @contextmanager
    def scope(self):
        es = ExitStack()
        k = self

        class S:
            def sb(self, shape, dt, name=None):
                return k.sb(shape, dt, name, es=es)

        try:
            yield S()
            self.barrier()
        finally:
            es.close()

    def _waits(self, e, deps, skipkey=None):
        eng = self.engs[e]
        for key, v in deps.items():
            if key == skipkey:
                continue
            if key == e and e in ("pe", "sp", "pool"):
                continue
            if self.seen[e].get(key, 0) >= v:
                continue
            eng.wait_ge(self.semh[key], v)
            self.seen[e][key] = v
            self.ninst += 1

    @staticmethod
    def _deps(reads, writes, e=None):
        deps = {}
        for b in reads:
            if b.lw is not None:
                deps[b.lw[0]] = max(deps.get(b.lw[0], 0), b.lw[1])
            if b.psum:
                for key, v in b.rs.items():
                    if key != e:
                        deps[key] = max(deps.get(key, 0), v)
        for b in writes:
            if b.lw is not None:
                deps[b.lw[0]] = max(deps.get(b.lw[0], 0), b.lw[1])
            for key, v in b.rs.items():
                deps[key] = max(deps.get(key, 0), v)
        return deps

    def op(self, e, name, reads=(), writes=(), inc=True, **kw):
        self.nops = getattr(self, "nops", 0) + 1
        if GCUT and self.nops > GCUT:
            return None
        deps = self._deps(reads, writes, e)
        self._waits(e, deps)
        ins = getattr(self.engs[e], name)(**kw)
        self.ninst += 1
        idx = self.cnt[e] + 1
        if inc:
            ins.then_inc(self.semh[e], 1)
            self.cnt[e] = idx
            self.pend[e] = False
        else:
            self.pend[e] = True
        for b in reads:
            b.rs[e] = idx
        for b in writes:
            b.lw = (e, idx)
            b.rs = {}
        return ins

    def _dsem(self, b):
        if b.dsem is None:
            self.ndsem += 1
            b.dkey = "d%d_%s" % (self.ndsem, b.name)
            b.dsem = self.es.enter_context(self.nc.semaphore(b.dkey))
            self.semh[b.dkey] = b.dsem
        return b.dsem

    def load(self, q, buf, out, in_):
        sem = self._dsem(buf)
        deps = self._deps((), (buf,))
        self._waits(q, deps, skipkey=buf.dkey)
        self.engs[q].dma_start(out=out, in_=in_).then_inc(sem, 16)
        self.ninst += 1
        buf.dcnt += 16
        buf.lw = (buf.dkey, buf.dcnt)
        buf.rs = {}

    def store(self, q, buf, out, in_):
        sem = self._dsem(buf)
        deps = self._deps((buf,), ())
        self._waits(q, deps, skipkey=None)
        self.engs[q].dma_start(out=out, in_=in_).then_inc(sem, 16)
        self.ninst += 1
        buf.dcnt += 16
        buf.rs[buf.dkey] = buf.dcnt

    def barrier(self):
        assert GCUT or not any(self.pend.values()), self.pend
        for e in ("pe", "act", "dve"):
            deps = {d: self.cnt[d] for d in ("pe", "act", "dve") if d != e and self.cnt[d] > 0}
            self._waits(e, deps)

    def wait_all_stores(self, e, bufs):
        deps = {}
        for b in bufs:
            if b.dkey is not None:
                deps[b.dkey] = b.dcnt
        self._waits(e, deps)


def build_program(nseq=4, depth=DEPTH, dbg=False, mixers=("conv", "gdn", "attn"), do_ffn=True):
    nc = bass.Bass("TRN2", target_bir_lowering=False)
    dr = {}

    def din(name, shape, dt=F32):
        dr[name] = nc.dram_tensor(name, list(shape), dt, kind="ExternalInput").ap()
        return dr[name]

    xT_d = din("xT", [nseq, D, T])
    w_in_d = din("w_in", [DEPTH, D, INW])
    w_out_d = din("w_out", [DEPTH, D, D])
    w_f1_d = din("w_ffn_in", [DEPTH, D, 2 * DFF])
    w_f2_d = din("w_ffn_out", [DEPTH, DFF, D])
    pcol_d = din("pcol", [DEPTH + 1, 128, NPC])
    prow_d = din("prow", [DEPTH, 128, NPR])
    c32_d = din("c32", [128, 4 * 128 + 4 * 128 + 4 * NR])
    cb_d = din("cb", [128, 5 * 128])
    out_d = nc.dram_tensor("outT", [nseq, D, T], F32, kind="ExternalOutput").ap()
    if dbg:
        dbg_d = nc.dram_tensor("dbg", [8, 128, T], F32, kind="ExternalOutput").ap()

    with ExitStack() as es:
        k = K(nc, es)
        xT_t = k.sb([128, 8, T], F32, "xT")
        hT_t = k.sb([128, 8, T], BF16, "hT")
        xT = [xT_t.view("xT%d" % g) for g in range(4)]
        hT = [hT_t.view("hT%d" % g) for g in range(4)]
        NS = 6
        slots = [k.sb([128, 2048], BF16, "slot%d" % i) for i in range(NS)]
        c32 = k.sb([128, 4 * 128 + 4 * 128 + 4 * NR], F32, "c32")
        cb = k.sb([128, 5 * 128], BF16, "cb")
        pcol = k.sb([128, NPC], F32, "pcol")
        prow = k.sb([128, NPR], F32, "prow")
        wab = k.sb([128, 8, 8], BF16, "wab")
        P = [k.ps("P%d" % i) for i in range(8)]
        epsb = k.sb([128, 4], F32, "epsb")
        k.op("dve", "memset", writes=[epsb], ap=epsb.t[:, 0:1], constant=RMS_EPS)
        k.op("dve", "memset", writes=[epsb], ap=epsb.t[:, 1:2], constant=LN_EPS)
        k.op("dve", "memset", writes=[epsb], ap=epsb.t[:, 2:3], constant=1.0)
        k.op("dve", "memset", writes=[epsb], ap=epsb.t[:, 3:4], constant=0.0)

        k.load("sp", c32, c32.t[:], c32_d)
        k.load("pool", cb, cb.t[:], cb_d)
        maskC = c32.t[:, 0:128]
        maskU = c32.t[:, 128:256]
        triT = c32.t[:, 256:384]
        ones32 = c32.t[:, 384:512]
        ident32 = c32.t[:, 512:640]
        abias = lambda h, r: c32.t[:, 1024 + h * NR + (r + 3):1024 + h * NR + (r + 3) + 1]
        ident = cb.t[:, 0:128]
        onesb = cb.t[:, 128:256]
        blk64 = cb.t[:, 256:384]
        strict = cb.t[:, 384:512]
        triU = cb.t[:, 512:640]

        slot_i = [0]

        def next_slot():
            s = slots[slot_i[0] % NS]
            slot_i[0] += 1
            return s

        def wload(slot, dst, src):
            k.load("pool", slot, dst, src)

        def w_in_cols(l, c0, n):
            return w_in_d[l, :, c0:c0 + n].rearrange("(c p) e -> p c e", p=128)

        def proj_fm(pb, wslot, wv, m0, m, tg, act=hT, kc=8, act_t=None):
            at = act_t if act_t is not None else hT_t.t
            for c in range(kc):
                k.op("pe", "matmul", reads=[wslot, act[tg]], writes=[pb], inc=(c == kc - 1),
                     out=pb.t[0:m, :], lhsT=wv[:, c, m0:m0 + m], rhs=at[:, c, tg * 512:(tg + 1) * 512],
                     start=(c == 0), stop=(c == kc - 1))

        def phase_norm(gcol, final=False, s=0):
            with k.scope() as sc:
                sq = [sc.sb([128, 8, 512], BF16) for _ in range(2)]
                rs = [sc.sb([128, 512], F32) for _ in range(2)]
                ob = [sc.sb([128, 8, 512], F32) for _ in range(2)] if final else None
                for tg in range(4):
                    ts = slice(tg * 512, (tg + 1) * 512)
                    s_, r_, pb = sq[tg % 2], rs[tg % 2], P[tg % 2]
                    k.op("act", "activation", reads=[xT[tg]], writes=[s_],
                         out=s_.t[:], in_=xT_t.t[:, :, ts], func=AF.Square)
                    for c in range(8):
                        k.op("pe", "matmul", reads=[s_, cb], writes=[pb], inc=(c == 7),
                             out=pb.t[:], lhsT=onesb, rhs=s_.t[:, c, :], start=(c == 0), stop=(c == 7))
                    k.op("act", "activation", reads=[pb, epsb], writes=[r_],
                         out=r_.t[:], in_=pb.t[:], func=AF.Sqrt, scale=1.0 / D, bias=epsb.t[:, 0:1])
                    k.op("dve", "reciprocal", reads=[r_], writes=[r_], out=r_.t[:], in_=r_.t[:])
                    if not final:
                        for c in range(8):
                            k.op("dve", "scalar_tensor_tensor", reads=[xT[tg], r_, pcol], writes=[hT[tg]],
                                 out=hT_t.t[:, c, ts], in0=xT_t.t[:, c, ts], scalar=gcol[:, c:c + 1],
                                 in1=r_.t[:], op0=ALU.mult, op1=ALU.mult)
                    else:
                        o_ = ob[tg % 2]
                        for c in range(8):
                            k.op("dve", "scalar_tensor_tensor", reads=[xT[tg], r_, pcol], writes=[o_],
                                 out=o_.t[:, c, :], in0=xT_t.t[:, c, ts], scalar=gcol[:, c:c + 1],
                                 in1=r_.t[:], op0=ALU.mult, op1=ALU.mult)
                        k.store("sp", o_, out_d[s, :, ts].rearrange("(c p) t -> p c t", p=128), o_.t[:])
                if final:
                    k.wait_all_stores("sp", ob)
                    k.wait_all_stores("act", ob)
                    k.wait_all_stores("dve", ob)

        def out_proj(l, yb, yt, r0, kc):
            sl = next_slot()
            wv = sl.t[:, 0:kc * 1024].rearrange("p (c e) -> p c e", c=kc)
            wload(sl, wv, w_out_d[l, r0 * 128:(r0 + kc) * 128, :].rearrange("(c p) e -> p c e", p=128))
            i = 0
            for dc in range(8):
                for tg in range(4):
                    pb = P[i % 2]
                    i += 1
                    ts = slice(tg * 512, (tg + 1) * 512)
                    for c in range(kc):
                        k.op("pe", "matmul", reads=[sl, yb], writes=[pb], inc=(c == kc - 1),
                             out=pb.t[:], lhsT=wv[:, c, dc * 128:(dc + 1) * 128], rhs=yt[:, c, ts],
                             start=(c == 0), stop=(c == kc - 1))
                    k.op("dve", "tensor_tensor", reads=[pb, xT[tg]], writes=[xT[tg]],
                         out=xT_t.t[:, dc, ts], in0=xT_t.t[:, dc, ts], in1=pb.t[:], op=ALU.add)

        def dbg_dump(yb, yt, c0, kc, sc):
            if not dbg:
                return
            tmp = sc.sb([128, 512], F32)
            for c in range(kc):
                for tg in range(4):
                    k.op("act", "activation", reads=[yb], writes=[tmp], out=tmp.t[:], in_=yt[:, c, tg * 512:(tg + 1) * 512], func=AF.Copy)
                    k.store("sp", tmp, dbg_d[c0 + c, :, tg * 512:(tg + 1) * 512], tmp.t[:])
            k.wait_all_stores("act", [tmp])

        def phase_conv(l):
            with k.scope() as sc:
                sv, sg = next_slot(), next_slot()
                wvv = sv.t[:].rearrange("p (c e) -> p c e", c=8)
                wgv = sg.t[:].rearrange("p (c e) -> p c e", c=8)
                wload(sv, wvv, w_in_cols(l, 0, 256))
                wload(sg, wgv, w_in_cols(l, 256, 256))
                glu = sc.sb([128, 2, 32 + T], BF16, "glu")
                yc = sc.sb([128, 2, T], BF16, "yconv")
                sig = [sc.sb([128, 512], F32) for _ in range(2)]
                k.op("dve", "memset", writes=[glu], ap=glu.t[:, :, 0:32], constant=0.0)
                n = 0
                for tg in range(4):
                    for j in range(2):
                        pv, pg = P[(n * 2) % 4], P[(n * 2 + 1) % 4]
                        sg_ = sig[n % 2]
                        n += 1
                        proj_fm(pg, sg, wgv, j * 128, 128, tg)
                        proj_fm(pv, sv, wvv, j * 128, 128, tg)
                        k.op("act", "activation", reads=[pg], writes=[sg_], out=sg_.t[:], in_=pg.t[:], func=AF.Sigmoid)
                        k.op("dve", "tensor_tensor", reads=[pv, sg_], writes=[glu],
                             out=glu.t[:, j, 32 + tg * 512:32 + (tg + 1) * 512], in0=pv.t[:], in1=sg_.t[:], op=ALU.mult)
                dg = [sc.sb([128, 128], BF16) for _ in range(4)]
                cv = [sc.sb([128, 2, 512], F32, "cv%d" % i) for i in range(4)]
                cq = [sc.sb([128, 2, 512], F32, "cq%d" % i) for i in range(4)]
                n = 0
                for j in range(2):
                    for tap in range(31):
                        d_ = dg[n % 4]
                        n += 1
                        k.op("dve", "tensor_scalar", reads=[cb, pcol], writes=[d_],
                             out=d_.t[:], in0=ident, scalar1=pcol.t[:, PC_DW + j * 31 + tap:PC_DW + j * 31 + tap + 1],
                             scalar2=None, op0=ALU.mult)
                        for tg in range(4):
                            pb = P[4 + tg]
                            k.op("pe", "matmul", reads=[d_, glu], writes=[pb], inc=(tap == 30 or tg == 3),
                                 out=pb.t[:], lhsT=d_.t[:], rhs=glu.t[:, j, 2 + tap + tg * 512:2 + tap + (tg + 1) * 512],
                                 start=(tap == 0), stop=(tap == 30))
                    for tg in range(4):
                        pb = P[4 + tg]
                        k.op("act", "activation", reads=[pb, pcol], writes=[cv[tg]],
                             out=cv[tg].t[:, j, :], in_=pb.t[:], func=AF.Identity, bias=pcol.t[:, PC_CB + j:PC_CB + j + 1])
                        k.op("act", "activation", reads=[cv[tg]], writes=[cq[tg]],
                             out=cq[tg].t[:, j, :], in_=cv[tg].t[:, j, :], func=AF.Square)
                tmp = [sc.sb([128, 512], F32) for _ in range(4)]
                for tg in range(4):
                    pm, pq = P[(tg * 2) % 4], P[(tg * 2 + 1) % 4]
                    for j in range(2):
                        k.op("pe", "matmul", reads=[cv[tg], c32], writes=[pm], inc=(j == 1),
                             out=pm.t[:], lhsT=ones32, rhs=cv[tg].t[:, j, :], start=(j == 0), stop=(j == 1))
                    for j in range(2):
                        k.op("pe", "matmul", reads=[cq[tg], c32], writes=[pq], inc=(j == 1),
                             out=pq.t[:], lhsT=ones32, rhs=cq[tg].t[:, j, :], start=(j == 0), stop=(j == 1))
                    mean, var = tmp[0], tmp[1]
                    k.op("act", "activation", reads=[pm], writes=[mean], out=mean.t[:], in_=pm.t[:], func=AF.Copy, scale=1.0 / 256)
                    k.op("act", "activation", reads=[mean], writes=[var], out=var.t[:], in_=mean.t[:], func=AF.Square)
                    k.op("dve", "scalar_tensor_tensor", reads=[pq, var], writes=[var],
                         out=var.t[:], in0=pq.t[:], scalar=1.0 / 256, in1=var.t[:], op0=ALU.mult, op1=ALU.subtract)
                    k.op("act", "activation", reads=[var, epsb], writes=[var],
                         out=var.t[:], in_=var.t[:], func=AF.Sqrt, bias=epsb.t[:, 1:2])
                    k.op("dve", "reciprocal", reads=[var], writes=[var], out=var.t[:], in_=var.t[:])
                    for j in range(2):
                        t_ = tmp[2 + j]
                        k.op("dve", "tensor_tensor", reads=[cv[tg], mean], writes=[t_],
                             out=t_.t[:], in0=cv[tg].t[:, j, :], in1=mean.t[:], op=ALU.subtract)
                        k.op("dve", "tensor_tensor", reads=[t_, var], writes=[t_],
                             out=t_.t[:], in0=t_.t[:], in1=var.t[:], op=ALU.mult)
                        k.op("act", "activation", reads=[t_, pcol], writes=[yc],
                             out=yc.t[:, j, tg * 512:(tg + 1) * 512], in_=t_.t[:], func=AF.Silu,
                             scale=pcol.t[:, PC_LG + j:PC_LG + j + 1], bias=pcol.t[:, PC_LB + j:PC_LB + j + 1])
                dbg_dump(yc, yc.t, 0, 2, sc)
                out_proj(l, yc, yc.t, 0, 2)

        def phase_attn(l, lam_init):
            scale = 64 ** -0.5
            with k.scope() as sc:
                lt = sc.sb([128, 2, 64], F32)
                ls = sc.sb([128, 2], F32)
                nlam = sc.sb([128, 1], F32)
                k.op("dve", "tensor_tensor", reads=[prow], writes=[lt], out=lt.t[:, 0, :],
                     in0=prow.t[:, PR_LAM:PR_LAM + 64], in1=prow.t[:, PR_LAM + 64:PR_LAM + 128], op=ALU.mult)
                k.op("dve", "tensor_tensor", reads=[prow], writes=[lt], out=lt.t[:, 1, :],
                     in0=prow.t[:, PR_LAM + 128:PR_LAM + 192], in1=prow.t[:, PR_LAM + 192:PR_LAM + 256], op=ALU.mult)
                k.op("dve", "tensor_reduce", reads=[lt], writes=[ls], out=ls.t[:], in_=lt.t[:], axis=AX.X, op=ALU.add)
                k.op("act", "activation", reads=[ls], writes=[ls], out=ls.t[:], in_=ls.t[:], func=AF.Exp)
                k.op("dve", "scalar_tensor_tensor", reads=[ls], writes=[nlam], out=nlam.t[:], in0=ls.t[:, 1:2],
                     scalar=-lam_init, in1=ls.t[:, 0:1], op0=ALU.add, op1=ALU.subtract)
                wrow = sc.sb([128, 128], F32)
                k.op("dve", "tensor_scalar", reads=[prow], writes=[wrow], out=wrow.t[:], in0=prow.t[:, PR_SUB:PR_SUB + 128],
                     scalar1=1.0 - lam_init, scalar2=None, op0=ALU.mult)

                qT = sc.sb([128, 2, T], BF16, "qT")
                kT = sc.sb([128, 2, T], BF16, "kT")
                V1 = sc.sb([128, NT, 2, 132], BF16, "V1")
                yd = sc.sb([128, 2, T], BF16, "ydiff")
                pt = [sc.sb([128, 512], BF16, "pt%d" % i) for i in range(3)]
                oh = [sc.sb([128, 4, 132], F32, "oh%d" % i) for i in range(2)]
                rc = sc.sb([128, 8], F32)
                ssq = sc.sb([128, 8], F32)
                ob4 = sc.sb([128, 4, 128], F32, "ob4")
                oc4 = sc.sb([128, 4, 128], F32, "oc4")
                yb4 = sc.sb([128, 4, 128], BF16, "yb4")
                for hp in range(2):
                    sq_, sk_, sv_ = next_slot(), next_slot(), next_slot()
                    views = []
                    for s_, c0 in ((sq_, 1544), (sk_, 2056), (sv_, 2568)):
                        v_ = s_.t[:].rearrange("p (c e) -> p c e", c=8)
                        wload(s_, v_, w_in_cols(l, c0 + hp * 256, 256))
                        views.append(v_)
                    wq, wk, wv = views
                    k.op("dve", "memset", writes=[V1], ap=V1.t[:, :, :, 128:129], constant=1.0)
                    n = 0
                    for hh in range(2):
                        for tg in range(4):
                            pb = P[n % 2]
                            n += 1
                            proj_fm(pb, sq_, wq, hh * 128, 128, tg)
                            k.op("act", "activation", reads=[pb], writes=[qT], out=qT.t[:, hh, tg * 512:(tg + 1) * 512],
                                 in_=pb.t[:], func=AF.Copy, scale=scale)
                            pb = P[n % 2]
                            n += 1
                            proj_fm(pb, sk_, wk, hh * 128, 128, tg)
                            k.op("dve", "tensor_copy", reads=[pb], writes=[kT], out=kT.t[:, hh, tg * 512:(tg + 1) * 512],
                                 in_=pb.t[:])
                    for tt in range(NT):
                        pb = P[tt % 2]
                        for c in range(8):
                            k.op("pe", "matmul", reads=[sv_, hT[tt // 4]], writes=[pb], inc=(c == 7),
                                 out=pb.t[:, 0:256], lhsT=hT_t.t[:, c, tt * 128:(tt + 1) * 128], rhs=wv[:, c, :],
                                 start=(c == 0), stop=(c == 7))
                        k.op("act", "activation", reads=[pb], writes=[V1], out=V1.t[:, tt, :, 0:128],
                             in_=pb.t[:, 0:256].rearrange("p (h e) -> p h e", h=2), func=AF.Copy)
                    for hh in range(2):
                        h = hp * 2 + hh
                        W = 4 if SLOPES[h] * 511 <= 40 else (2 if SLOPES[h] * 255 <= 70 else 1)
                        items = [(g, c, j) for g in range(4) for c in range(2) for j in range(4 * g + 4)]
                        acc = [P[2], P[3], P[4], P[5]]

                        def emit_qk(i):
                            g, c, j = items[i]
                            pr = slice(c * 64, (c + 1) * 64)
                            qb0 = max(j, 4 * g)
                            nq = 4 * g + 4 - qb0
                            pS = P[6 + (i % 2)]
                            k.op("pe", "matmul", reads=[kT, qT], writes=[pS],
                                 out=pS.t[:, 0:nq * 128], lhsT=kT.t[pr, hh, j * 128:(j + 1) * 128],
                                 rhs=qT.t[pr, hh, qb0 * 128:(qb0 + nq) * 128], start=True, stop=True)

                        def emit_rest(i):
                            g, c, j = items[i]
                            qb0 = max(j, 4 * g)
                            nq = 4 * g + 4 - qb0
                            pS = P[6 + (i % 2)]
                            p_ = pt[i % 3]
                            qi = 0
                            while qi < nq:
                                qb = qb0 + qi
                                ref = (qb // W) * W
                                n_ = min(nq - qi, ref + W - qb)
                                k.op("act", "activation", reads=[pS, c32], writes=[p_], out=p_.t[:, qi * 128:(qi + n_) * 128],
                                     in_=pS.t[:, qi * 128:(qi + n_) * 128], func=AF.Exp, bias=abias(h, ref - j))
                                qi += n_
                            if j >= 4 * g:
                                k.op("dve", "tensor_tensor", reads=[p_, cb], writes=[p_], out=p_.t[:, 0:128],
                                     in0=p_.t[:, 0:128], in1=triU, op=ALU.mult)
                            for qi in range(nq):
                                qb = qb0 + qi
                                a_ = acc[qb % 4]
                                k.op("pe", "matmul", reads=[p_, V1], writes=[a_], inc=(qi == nq - 1),
                                     out=a_.t[:, 0:129], lhsT=p_.t[:, qi * 128:(qi + 1) * 128],
                                     rhs=V1.t[:, j, hh, 0:129], start=(j == 0), stop=(j == qb))
                            if j == 4 * g + 3:
                                o_ = oh[c]
                                for a in range(4):
                                    if a % 2 == 0:
                                        k.op("act", "activation", reads=[acc[a]], writes=[o_], out=o_.t[:, a, 0:129],
                                             in_=acc[a].t[:, 0:129], func=AF.Copy)
                                    else:
                                        k.op("dve", "tensor_copy", reads=[acc[a]], writes=[o_], out=o_.t[:, a, 0:129],
                                             in_=acc[a].t[:, 0:129])
                                if c == 1:
                                    post(g)

                        def post(g):
                            for c in range(2):
                                k.op("dve", "reciprocal", reads=[oh[c]], writes=[rc], out=rc.t[:, c * 4:(c + 1) * 4],
                                     in_=oh[c].t[:, :, 128])
                            k.op("dve", "tensor_tensor", reads=[oh[0], rc], writes=[ob4], out=ob4.t[:], in0=oh[0].t[:, :, 0:128],
                                 in1=rc.t[:, 0:4].unsqueeze(2).to_broadcast([128, 4, 128]), op=ALU.mult)
                            k.op("dve", "tensor_tensor", reads=[oh[1], rc], writes=[oc4], out=oc4.t[:], in0=oh[1].t[:, :, 0:128],
                                 in1=rc.t[:, 4:8].unsqueeze(2).to_broadcast([128, 4, 128]), op=ALU.mult)
                            k.op("dve", "scalar_tensor_tensor", reads=[ob4, oc4, nlam], writes=[ob4], out=ob4.t[:], in0=oc4.t[:],
                                 scalar=nlam.t[:, 0:1], in1=ob4.t[:], op0=ALU.mult, op1=ALU.add)
                            k.op("dve", "tensor_tensor", reads=[ob4], writes=[oc4], out=oc4.t[:], in0=ob4.t[:], in1=ob4.t[:], op=ALU.mult)
                            k.op("dve", "tensor_reduce", reads=[oc4], writes=[ssq], out=ssq.t[:, 0:4], in_=oc4.t[:], axis=AX.X, op=ALU.add)
                            k.op("act", "activation", reads=[ssq, epsb], writes=[ssq], out=ssq.t[:, 4:8], in_=ssq.t[:, 0:4],
                                 func=AF.Ln, scale=1.0 / 128, bias=epsb.t[:, 0:1])
                            k.op("act", "activation", reads=[ssq], writes=[ssq], out=ssq.t[:, 4:8], in_=ssq.t[:, 4:8],
                                 func=AF.Exp, scale=-0.5)
                            k.op("dve", "tensor_tensor", reads=[ob4, ssq], writes=[ob4], out=ob4.t[:], in0=ob4.t[:],
                                 in1=ssq.t[:, 4:8].unsqueeze(2).to_broadcast([128, 4, 128]), op=ALU.mult)
                            k.op("dve", "tensor_tensor", reads=[ob4, wrow], writes=[yb4], out=yb4.t[:], in0=ob4.t[:],
                                 in1=wrow.t[:, :].unsqueeze(1).to_broadcast([128, 4, 128]), op=ALU.mult)
                            pT = P[g % 2]
                            for qi in range(4):
                                k.op("pe", "matmul", reads=[yb4, cb], writes=[pT], out=pT.t[:, qi * 128:(qi + 1) * 128], lhsT=yb4.t[:, qi, :],
                                     rhs=ident, start=True, stop=True)
                            k.op("act", "activation", reads=[pT], writes=[yd], out=yd.t[:, hh, g * 512:(g + 1) * 512],
                                 in_=pT.t[:], func=AF.Copy)

                        emit_qk(0)
                        for i in range(len(items)):
                            if i + 1 < len(items):
                                emit_qk(i + 1)
                            emit_rest(i)
                    dbg_dump(yd, yd.t, 4 + hp * 2, 2, sc)
                    out_proj(l, yd, yd.t, 4 + hp * 2, 2)

        def phase_ffn(l):
            with k.scope() as sc:
                actT = sc.sb([128, NFC, 1024], BF16, "actT")
                sg = [sc.sb([128, 512], BF16) for _ in range(2)]
                for half in range(2):
                    n = 0
                    for fb in range(NFC):
                        sl = next_slot()
                        wv = sl.t[:].rearrange("p (c e) -> p c e", c=8)
                        wload(sl, wv[:, :, 0:128], w_f1_d[l, :, fb * 128:(fb + 1) * 128].rearrange("(c p) e -> p c e", p=128))
                        wload(sl, wv[:, :, 128:256], w_f1_d[l, :, DFF + fb * 128:DFF + (fb + 1) * 128].rearrange("(c p) e -> p c e", p=128))
                        for t2 in range(2):
                            tg = half * 2 + t2
                            pg, pu = P[(n * 2) % 4], P[(n * 2 + 1) % 4]
                            s_ = sg[n % 2]
                            n += 1
                            proj_fm(pg, sl, wv, 0, 128, tg)
                            proj_fm(pu, sl, wv, 128, 128, tg)
                            k.op("act", "activation", reads=[pg], writes=[s_], out=s_.t[:], in_=pg.t[:], func=AF.Silu)
                            k.op("dve", "tensor_tensor", reads=[pu, s_], writes=[actT], out=actT.t[:, fb, t2 * 512:(t2 + 1) * 512],
                                 in0=pu.t[:], in1=s_.t[:], op=ALU.mult)
                    for dg_ in range(4):
                        banks = [P[4 + i] for i in range(4)]
                        for kb in range(3):
                            nk = min(8, NFC - kb * 8)
                            sl = next_slot()
                            wv = sl.t[:].rearrange("p (c e) -> p c e", c=8)
                            wload(sl, wv[:, 0:nk, :], w_f2_d[l, kb * 1024:kb * 1024 + nk * 128, dg_ * 256:(dg_ + 1) * 256]
                                  .rearrange("(c p) e -> p c e", p=128))
                            for ci in range(nk):
                                fc = kb * 8 + ci
                                for dc2 in range(2):
                                    for t2 in range(2):
                                        pb = banks[dc2 * 2 + t2]
                                        k.op("pe", "matmul", reads=[sl, actT], writes=[pb], inc=(fc == NFC - 1 or (ci == nk - 1 and dc2 == 1 and t2 == 1)),
                                             out=pb.t[:], lhsT=wv[:, ci, dc2 * 128:(dc2 + 1) * 128],
                                             rhs=actT.t[:, fc, t2 * 512:(t2 + 1) * 512], start=(fc == 0), stop=(fc == NFC - 1))
                        for dc2 in range(2):
                            for t2 in range(2):
                                pb = banks[dc2 * 2 + t2]
                                tg = half * 2 + t2
                                dc = dg_ * 2 + dc2
                                ts = slice(tg * 512, (tg + 1) * 512)
                                k.op("dve", "tensor_tensor", reads=[pb, xT[tg]], writes=[xT[tg]],
                                     out=xT_t.t[:, dc, ts], in0=xT_t.t[:, dc, ts], in1=pb.t[:], op=ALU.add)

        def phase_gdn(l):
            with k.scope() as sc:
                s_q, s_k, s_v, s_z = next_slot(), next_slot(), next_slot(), next_slot()
                wviews = []
                for s_, c0 in ((s_q, 512), (s_k, 768), (s_v, 1024), (s_z, 1280)):
                    v_ = s_.t[:].rearrange("p (c e) -> p c e", c=8)
                    wload(s_, v_, w_in_cols(l, c0, 256))
                    wviews.append(v_)
                wq, wk, wv, wz = wviews
                k.load("pool", wab, wab.t[:], w_in_cols(l, 1536, 8))
                abp = P[0]
                for tt in range(NT):
                    for c in range(8):
                        k.op("pe", "matmul", reads=[wab, hT[tt // 4]], writes=[abp], inc=(c == 7),
                             out=abp.t[:, tt * 8:(tt + 1) * 8], lhsT=hT_t.t[:, c, tt * 128:(tt + 1) * 128], rhs=wab.t[:, c, :],
                             start=(c == 0), stop=(c == 7))
                abv = abp.t[:, 0:128].rearrange("p (t e) -> p t e", e=8)
                gt = sc.sb([128, NT, 4], F32, "gt")
                bt = sc.sb([128, NT, 4], F32, "bt")
                nbt = sc.sb([128, NT, 4], F32, "nbt")
                gc = sc.sb([128, NT, 4], F32, "gc")
                ed = sc.sb([128, NT, 4], F32, "ed")
                bew = sc.sb([128, NT, 4], F32, "bew")
                na = sc.sb([128, 4], F32, "na")
                for h in range(4):
                    k.op("dve", "tensor_scalar", reads=[abp, prow], writes=[gt], out=gt.t[:, :, h], in0=abv[:, :, h],
                         scalar1=prow.t[:, PR_DTB + h:PR_DTB + h + 1], scalar2=None, op0=ALU.add)
                k.op("act", "activation", reads=[gt], writes=[gt], out=gt.t[:], in_=gt.t[:], func=AF.Exp)
                k.op("act", "activation", reads=[gt, epsb], writes=[gt], out=gt.t[:], in_=gt.t[:], func=AF.Ln, bias=epsb.t[:, 2:3])
                k.op("act", "activation", reads=[prow], writes=[na], out=na.t[:], in_=prow.t[:, PR_ALOG:PR_ALOG + 4], func=AF.Exp)
                for h in range(4):
                    k.op("dve", "tensor_scalar", reads=[gt, na], writes=[gt], out=gt.t[:, :, h], in0=gt.t[:, :, h],
                         scalar1=na.t[:, h:h + 1], scalar2=-1.0, op0=ALU.mult, op1=ALU.mult)
                k.op("act", "activation", reads=[abp], writes=[bt], out=bt.t[:], in_=abv[:, :, 4:8], func=AF.Sigmoid)
                k.op("dve", "tensor_scalar", reads=[bt], writes=[nbt], out=nbt.t[:], in0=bt.t[:], scalar1=-1.0, scalar2=None, op0=ALU.mult)
                gflat = gt.t[:].rearrange("p t h -> p (t h)")
                pcs = P[1]
                k.op("pe", "matmul", reads=[gt, c32], writes=[pcs], out=pcs.t[:, 0:64], lhsT=triT, rhs=gflat, start=True, stop=True)
                k.op("pe", "matmul", reads=[gt, c32], writes=[pcs], out=pcs.t[:, 64:128], lhsT=ones32, rhs=gflat, start=True, stop=True)
                gcf = gc.t[:].rearrange("p t h -> p (t h)")
                edf = ed.t[:].rearrange("p t h -> p (t h)")
                bewf = bew.t[:].rearrange("p t h -> p (t h)")
                k.op("act", "activation", reads=[pcs], writes=[gc], out=gcf, in_=pcs.t[:, 0:64], func=AF.Copy)
                k.op("dve", "tensor_tensor", reads=[pcs, gc], writes=[ed], out=edf, in0=pcs.t[:, 64:128], in1=gcf, op=ALU.subtract)
                k.op("act", "activation", reads=[ed], writes=[ed], out=edf, in_=edf, func=AF.Exp)
                k.op("act", "activation", reads=[gc], writes=[bew], out=bewf, in_=gcf, func=AF.Exp)
                k.op("dve", "tensor_tensor", reads=[bew, bt], writes=[bew], out=bewf, in0=bewf, in1=bt.t[:].rearrange("p t h -> p (t h)"), op=ALU.mult)

                print("ops at stage1:", k.nops)
                if GSTOP <= 1:
                    return
                qT = sc.sb([128, 2, T], BF16, "gq")
                kT = sc.sb([128, 2, T], BF16, "gk")
                vT = sc.sb([128, 2, T], BF16, "gv")
                yg = sc.sb([128, 2, T], BF16, "yg")
                with k.scope() as sc2:
                    ut = [sc2.sb([128, 4 + T], BF16, "ut%d" % i) for i in range(2)]
                    dgs = [sc2.sb([128, 128], BF16) for _ in range(4)]
                    sl32 = [sc2.sb([128, 512], F32) for _ in range(2)]
                    sqb = [sc2.sb([128, 512], BF16) for _ in range(2)]
                    rn = [sc2.sb([128, 512], F32) for _ in range(2)]
                    for u_ in ut:
                        k.op("dve", "memset", writes=[u_], ap=u_.t[:, 0:4], constant=0.0)
                    n = 0
                    for kind, (slot_, wv_, dst) in enumerate(((s_q, wq, qT), (s_k, wk, kT), (s_v, wv, vT))):
                        for hp in range(2):
                            ch = kind * 2 + hp
                            u_ = ut[(kind * 2 + hp) % 2]
                            for tg in range(4):
                                pb = P[tg % 2]
                                proj_fm(pb, slot_, wv_, hp * 128, 128, tg)
                                k.op("act", "activation", reads=[pb], writes=[u_], out=u_.t[:, 4 + tg * 512:4 + (tg + 1) * 512],
                                     in_=pb.t[:], func=AF.Copy)
                            for tap in range(4):
                                k.op("dve", "tensor_scalar", reads=[cb, pcol], writes=[dgs[tap]], out=dgs[tap].t[:], in0=ident,
                                     scalar1=pcol.t[:, PC_GW + ch * 4 + tap:PC_GW + ch * 4 + tap + 1], scalar2=None, op0=ALU.mult)
                            for tg in range(4):
                                pb = P[4 + tg]
                                ts = slice(tg * 512, (tg + 1) * 512)
                                for tap in range(4):
                                    k.op("pe", "matmul", reads=[dgs[tap], u_], writes=[pb], inc=(tap == 3), out=pb.t[:], lhsT=dgs[tap].t[:],
                                         rhs=u_.t[:, 1 + tap + tg * 512:1 + tap + (tg + 1) * 512], start=(tap == 0), stop=(tap == 3))
                                if kind == 2:
                                    k.op("act", "activation", reads=[pb], writes=[dst], out=dst.t[:, hp, ts], in_=pb.t[:], func=AF.Silu)
                                else:
                                    s32, sq_, rn_ = sl32[n % 2], sqb[n % 2], rn[n % 2]
                                    pn = P[2 + n % 2]
                                    n += 1
                                    k.op("act", "activation", reads=[pb], writes=[s32], out=s32.t[:], in_=pb.t[:], func=AF.Silu)
                                    k.op("act", "activation", reads=[s32], writes=[sq_], out=sq_.t[:], in_=s32.t[:], func=AF.Square)
                                    k.op("pe", "matmul", reads=[sq_, cb], writes=[pn], out=pn.t[:], lhsT=blk64, rhs=sq_.t[:], start=True, stop=True)
                                    k.op("act", "activation", reads=[pn, epsb], writes=[rn_], out=rn_.t[:], in_=pn.t[:], func=AF.Sqrt, bias=epsb.t[:, 0:1])
                                    k.op("dve", "reciprocal", reads=[rn_], writes=[rn_], out=rn_.t[:], in_=rn_.t[:])
                                    k.op("dve", "scalar_tensor_tensor", reads=[s32, rn_], writes=[dst], out=dst.t[:, hp, ts], in0=s32.t[:],
                                         scalar=(0.125 if kind == 0 else 1.0), in1=rn_.t[:], op0=ALU.mult, op1=ALU.mult)

                print("ops at stage2:", k.nops)
                if GSTOP <= 2:
                    return
                S = sc.sb([128, 2, 128], F32, "S")
                k.op("dve", "memset", writes=[S], ap=S.t[:], constant=0.0)
                Gb = sc.sb([128, 4, 128], F32, "Gb")
                Gm = sc.sb([128, 4, 128], F32, "Gm")
                Gm2 = sc.sb([128, 4, 128], F32, "Gm2")
                Dc = sc.sb([128, 4, 128], F32, "Dc")
                DT = sc.sb([128, 4, 128], BF16, "DT")
                Eg = sc.sb([128, 4, 128], F32, "Eg")
                Nb = [sc.sb([128, 4, 128], F32, "Nb%d" % i) for i in range(2)]
                Mb = [sc.sb([128, 4, 128], F32, "Mb%d" % i) for i in range(2)]
                Y = sc.sb([128, 4, 128], F32, "Y")
                QKT = sc.sb([128, 4, 128], BF16, "QKT")
                U = sc.sb([128, 256], F32, "U")
                WT = sc.sb([128, 2, 128], F32, "WT")
                qe = sc.sb([128, 2, 128], F32, "qe")
                kd = sc.sb([128, 256], BF16, "kd")
                rhu = sc.sb([128, 256], F32, "rhu")
                rhw = sc.sb([128, 256], F32, "rhw")
                vnew = sc.sb([128, 256], BF16, "vnew")
                ob = sc.sb([128, 4, 64], F32, "ob")
                osq = sc.sb([128, 4, 64], F32, "osq")
                ors = sc.sb([128, 8], F32, "ors")
                zs = sc.sb([128, 256], F32, "zs")
                ytok = sc.sb([128, 256], BF16, "ytok")
                kc2 = sc.sb([128, 4, 128], BF16, "kc2")
                kc2v = kc2.t[:].rearrange("p (a b) e -> p a b e", a=2)
                k.op("dve", "memset", writes=[kc2], ap=kc2.t[:], constant=0.0)
                gnw = prow.t[:, PR_GNW:PR_GNW + 64]
                for n in range(NT):
                    cs = slice(n * 128, (n + 1) * 128)
                    tgi = n // 4
                    PA, PB, PT_, PN, PM, PY, PK, PX = P
                    for hp in range(2):
                        k.op("pe", "matmul", reads=[kT, cb], writes=[PK], out=PK.t[:, hp * 128:(hp + 1) * 128], lhsT=kT.t[:, hp, cs], rhs=ident,
                             start=True, stop=True)
                        k.op("pe", "matmul", reads=[vT, cb], writes=[PK], out=PK.t[:, 256 + hp * 128:256 + (hp + 1) * 128], lhsT=vT.t[:, hp, cs],
                             rhs=ident, start=True, stop=True)
                    for h in range(4):
                        hs = slice(h * 64, (h + 1) * 64)
                        k.op("dve", "tensor_scalar", reads=[PK, bew], writes=[rhw], out=rhw.t[:, hs], in0=PK.t[:, hs],
                             scalar1=bew.t[:, n, h:h + 1], scalar2=None, op0=ALU.mult)
                        k.op("dve", "tensor_scalar", reads=[PK, ed], writes=[kd], out=kd.t[:, hs], in0=PK.t[:, hs],
                             scalar1=ed.t[:, n, h:h + 1], scalar2=None, op0=ALU.mult)
                        k.op("dve", "tensor_scalar", reads=[PK, bt], writes=[rhu], out=rhu.t[:, hs], in0=PK.t[:, 256 + h * 64:256 + (h + 1) * 64],
                             scalar1=bt.t[:, n, h:h + 1], scalar2=None, op0=ALU.mult)
                    if n == 0: print('ops at stage3:', k.nops)
                    if GSTOP <= 3:
                        return
                    for h in range(4):
                        k.op("dve", "tensor_scalar", reads=[c32, gt], writes=[Gb], out=Gb.t[:, h, :], in0=ones32,
                             scalar1=gt.t[:, n, h:h + 1], scalar2=-1.0, op0=ALU.mult, op1=ALU.mult)
                        k.op("pe", "matmul", reads=[Gb, c32], writes=[PA], out=PA.t[:, h * 128:(h + 1) * 128], lhsT=Gb.t[:, h, :], rhs=triT,
                             start=True, stop=True)
                    PA3 = PA.t[:].rearrange("p (h e) -> p h e", h=4)
                    for h in range(4):
                        k.op("dve", "scalar_tensor_tensor", reads=[PA, gc, c32], writes=[Gm], out=Gm.t[:, h, :], in0=PA3[:, h, :],
                             scalar=gc.t[:, n, h:h + 1], in1=maskC, op0=ALU.add, op1=ALU.add)
                        k.op("dve", "scalar_tensor_tensor", reads=[PA, gc, c32], writes=[Gm2], out=Gm2.t[:, h, :], in0=PA3[:, h, :],
                             scalar=gc.t[:, n, h:h + 1], in1=maskU, op0=ALU.add, op1=ALU.subtract)
                    k.op("act", "activation", reads=[Gm], writes=[Dc], out=Dc.t[:], in_=Gm.t[:], func=AF.Exp)
                    k.op("act", "activation", reads=[Gm2], writes=[DT], out=DT.t[:], in_=Gm2.t[:], func=AF.Exp, scale=-1.0)
                    k.op("act", "activation", reads=[PA], writes=[Eg], out=Eg.t[:], in_=PA3, func=AF.Exp, scale=-1.0)
                    if n == 0: print('ops at stage4:', k.nops)
                    if GSTOP <= 4:
                        return
                    N_, M_ = Nb[0], Mb[0]
                    PB3 = PB.t[:].rearrange("p (h e) -> p h e", h=4)
                    for h in range(4):
                        pr = slice((h % 2) * 64, (h % 2 + 1) * 64)
                        if h == 0:
                            k.op("act", "activation", reads=[kT], writes=[kc2], out=kc2v[0:64, :, 0, :], in_=kT.t[0:64, :, cs], func=AF.Copy)
                            k.op("act", "activation", reads=[kT], writes=[kc2], out=kc2v[64:128, :, 1, :], in_=kT.t[64:128, :, cs], func=AF.Copy)
                        k.op("pe", "matmul", reads=[kT, kc2], writes=[PB], out=PB.t[:, h * 128:(h + 1) * 128], lhsT=kT.t[:, h // 2, cs], rhs=kc2.t[:, h, :],
                             start=True, stop=True)
                    for h in range(4):
                        k.op("dve", "scalar_tensor_tensor", reads=[PB, nbt, Dc], writes=[N_], out=N_.t[:, h, :], in0=PB3[:, h, :],
                             scalar=nbt.t[:, n, h:h + 1], in1=Dc.t[:, h, :], op0=ALU.mult, op1=ALU.mult)
                    PT3 = PT_.t[:].rearrange("p (h e) -> p h e", h=4)
                    for h in range(4):
                        k.op("pe", "matmul", reads=[N_, c32], writes=[PT_], out=PT_.t[:, h * 128:(h + 1) * 128], lhsT=N_.t[:, h, :], rhs=ident32, start=True, stop=True)
                    k.op("act", "activation", reads=[PT_], writes=[M_], out=M_.t[:], in_=PT3, func=AF.Copy)
                    for h in range(4):
                        k.op("dve", "tensor_tensor", reads=[PT_, c32], writes=[Y], out=Y.t[:, h, :], in0=PT3[:, h, :], in1=ident32, op=ALU.add)
                    if n == 0: print('ops at stage5:', k.nops)
                    if GSTOP <= 5:
                        return
                    PN3 = PN.t[:].rearrange("p (h e) -> p h e", h=4)
                    PM3 = PM.t[:].rearrange("p (h e) -> p h e", h=4)
                    PY3 = PY.t[:].rearrange("p (h e) -> p h e", h=4)
                    cur = 0
                    for lev in range(6):
                        Nc, Mc = Nb[cur], Mb[cur]
                        Nn, Mn = Nb[1 - cur], Mb[1 - cur]
                        for h in range(4):
                            k.op("pe", "matmul", reads=[Mc, Nc], writes=[PN], out=PN.t[:, h * 128:(h + 1) * 128], lhsT=Mc.t[:, h, :], rhs=Nc.t[:, h, :],
                                 start=True, stop=True)
                        k.op("act", "activation", reads=[PN], writes=[Nn], out=Nn.t[:], in_=PN3, func=AF.Copy)
                        if lev < 5:
                            for h in range(4):
                                k.op("pe", "matmul", reads=[Mc, Nc], writes=[PM], out=PM.t[:, h * 128:(h + 1) * 128], lhsT=Nc.t[:, h, :], rhs=Mc.t[:, h, :],
                                     start=True, stop=True)
                            k.op("dve", "tensor_copy", reads=[PM], writes=[Mn], out=Mn.t[:], in_=PM3)
                        for h in range(4):
                            k.op("pe", "matmul", reads=[Nn, Y], writes=[PY], out=PY.t[:, h * 128:(h + 1) * 128], lhsT=Nn.t[:, h, :], rhs=Y.t[:, h, :],
                                 start=True, stop=True)
                        k.op("dve", "tensor_tensor", reads=[PY, Y], writes=[Y], out=Y.t[:], in0=Y.t[:], in1=PY3, op=ALU.add)
                        cur = 1 - cur
                    if n == 0: print('ops at stage6:', k.nops)
                    if GSTOP <= 6:
                        return
                    for h in range(4):
                        hs = slice(h * 64, (h + 1) * 64)
                        k.op("pe", "matmul", reads=[Y, rhu], writes=[PX], out=PX.t[:, hs], lhsT=Y.t[:, h, :], rhs=rhu.t[:, hs], start=True, stop=True)
                    k.op("act", "activation", reads=[PX], writes=[U], out=U.t[:], in_=PX.t[:, 0:256], func=AF.Copy)
                    for h in range(4):
                        hp = h // 2
                        k.op("pe", "matmul", reads=[Y, rhw], writes=[PB], out=PB.t[:, h * 128:(h + 1) * 128], lhsT=rhw.t[:, hp * 128:(hp + 1) * 128], rhs=Y.t[:, h, :],
                             start=True, stop=True)
                    for h in range(4):
                        pr = slice((h % 2) * 64, (h % 2 + 1) * 64)
                        if h % 2 == 0:
                            k.op("act", "activation", reads=[PB], writes=[WT], out=WT.t[pr, h // 2, :], in_=PB3[pr, h, :], func=AF.Copy)
                        else:
                            k.op("dve", "tensor_copy", reads=[PB], writes=[WT], out=WT.t[pr, h // 2, :], in_=PB3[pr, h, :])
                    if n == 0: print('ops at stage7:', k.nops)
                    if GSTOP <= 7:
                        return
                    for h in range(4):
                        pr = slice((h % 2) * 64, (h % 2 + 1) * 64)
                        k.op("pe", "matmul", reads=[kc2, qT], writes=[PT_], out=PT_.t[:, h * 128:(h + 1) * 128], lhsT=kc2.t[:, h, :], rhs=qT.t[:, h // 2, cs],
                             start=True, stop=True)
                    k.op("dve", "tensor_tensor", reads=[PT_, DT], writes=[QKT], out=QKT.t[:], in0=PT3, in1=DT.t[:], op=ALU.mult)
                    for h in range(4):
                        pr = slice((h % 2) * 64, (h % 2 + 1) * 64)
                        k.op("dve", "tensor_tensor", reads=[qT, Eg], writes=[qe], out=qe.t[pr, h // 2, :], in0=qT.t[pr, h // 2, cs],
                             in1=Eg.t[pr, h, :], op=ALU.mult)
                    if n == 0: print('ops at stage8:', k.nops)
                    if GSTOP <= 8:
                        return
                    for hp in range(2):
                        k.op("pe", "matmul", reads=[WT, S], writes=[PX], out=PX.t[:, 256 + hp * 128:256 + (hp + 1) * 128], lhsT=WT.t[:, hp, :],
                             rhs=S.t[:, hp, :], start=True, stop=True)
                    k.op("dve", "tensor_tensor", reads=[U, PX], writes=[vnew], out=vnew.t[:], in0=U.t[:], in1=PX.t[:, 256:512], op=ALU.subtract)
                    for h in range(4):
                        hs = slice(h * 64, (h + 1) * 64)
                        k.op("pe", "matmul", reads=[qe, S], writes=[PN], out=PN.t[:, hs], lhsT=qe.t[:, h // 2, :],
                             rhs=S.t[:, h // 2, (h % 2) * 64:(h % 2 + 1) * 64], start=True, stop=False)
                        k.op("pe", "matmul", reads=[QKT, vnew], writes=[PN], out=PN.t[:, hs], lhsT=QKT.t[:, h, :], rhs=vnew.t[:, hs],
                             start=False, stop=True)
                    for hp in range(2):
                        k.op("pe", "matmul", reads=[kd, vnew], writes=[PM], out=PM.t[:, hp * 128:(hp + 1) * 128], lhsT=kd.t[:, hp * 128:(hp + 1) * 128],
                             rhs=vnew.t[:, hp * 128:(hp + 1) * 128], start=True, stop=True)
                    for h in range(4):
                        pr = slice((h % 2) * 64, (h % 2 + 1) * 64)
                        cs2 = slice((h % 2) * 64, (h % 2 + 1) * 64)
                        k.op("dve", "scalar_tensor_tensor", reads=[S, Eg, PM], writes=[S], out=S.t[pr, h // 2, cs2], in0=S.t[pr, h // 2, cs2],
                             scalar=Eg.t[pr, h, 127:128], in1=PM.t[pr, (h // 2) * 128 + (h % 2) * 64:(h // 2) * 128 + (h % 2 + 1) * 64],
                             op0=ALU.mult, op1=ALU.add)
                    if n == 0: print('ops at stage9:', k.nops)
                    if GSTOP <= 9:
                        return
                    k.op("act", "activation", reads=[PN], writes=[ob], out=ob.t[:].rearrange("p h e -> p (h e)"), in_=PN.t[:, 0:256], func=AF.Copy)
                    k.op("act", "activation", reads=[ob], writes=[osq], out=osq.t[:], in_=ob.t[:], func=AF.Square)
                    k.op("dve", "tensor_reduce", reads=[osq], writes=[ors], out=ors.t[:, 0:4], in_=osq.t[:], axis=AX.X, op=ALU.add)
                    k.op("act", "activation", reads=[ors, epsb], writes=[ors], out=ors.t[:, 4:8], in_=ors.t[:, 0:4], func=AF.Sqrt, scale=1.0 / 64,
                         bias=epsb.t[:, 0:1])
                    k.op("dve", "reciprocal", reads=[ors], writes=[ors], out=ors.t[:, 4:8], in_=ors.t[:, 4:8])
                    for c in range(8):
                        k.op("pe", "matmul", reads=[s_z, hT[tgi]], writes=[PY], inc=(c == 7), out=PY.t[:, 0:256], lhsT=hT_t.t[:, c, cs], rhs=wz[:, c, :],
                             start=(c == 0), stop=(c == 7))
                    k.op("act", "activation", reads=[PY], writes=[zs], out=zs.t[:], in_=PY.t[:, 0:256], func=AF.Silu)
                    for h in range(4):
                        k.op("dve", "scalar_tensor_tensor", reads=[ob, ors, prow], writes=[osq], out=osq.t[:, h, :], in0=ob.t[:, h, :],
                             scalar=ors.t[:, 4 + h:5 + h], in1=gnw, op0=ALU.mult, op1=ALU.mult)
                    k.op("dve", "tensor_tensor", reads=[osq, zs], writes=[ytok], out=ytok.t[:], in0=osq.t[:].rearrange("p h e -> p (h e)"),
                         in1=zs.t[:], op=ALU.mult)
                    for hp in range(2):
                        k.op("pe", "matmul", reads=[ytok, cb], writes=[PK], out=PK.t[:, hp * 128:(hp + 1) * 128], lhsT=ytok.t[:, hp * 128:(hp + 1) * 128],
                             rhs=ident, start=True, stop=True)
                    k.op("act", "activation", reads=[PK], writes=[yg], out=yg.t[:, :, cs], in_=PK.t[:, 0:256].rearrange("p (a e) -> p a e", a=2),
                         func=AF.Copy)
                dbg_dump(yg, yg.t, 2, 2, sc)
                out_proj(l, yg, yg.t, 2, 2)

        for s in range(nseq):
            for g in range(4):
                k.load("sp", xT[g], xT_t.t[:, :, g * 512:(g + 1) * 512],
                       xT_d[s, :, g * 512:(g + 1) * 512].rearrange("(c p) t -> p c t", p=128))
            for l in range(depth):
                lam_init = 0.8 - 0.6 * math.exp(-0.3 * l)
                k.load("sp", pcol, pcol.t[:], pcol_d[l])
                k.load("sp", prow, prow.t[:], prow_d[l])
                phase_norm(pcol.t[:, PC_G1:PC_G1 + 8])
                if "conv" in mixers:
                    phase_conv(l)
                if "gdn" in mixers:
                    phase_gdn(l)
                if "attn" in mixers:
                    phase_attn(l, lam_init)
                if do_ffn:
                    phase_norm(pcol.t[:, PC_G2:PC_G2 + 8])
                    phase_ffn(l)
            k.load("sp", pcol, pcol.t[:], pcol_d[DEPTH])
            phase_norm(pcol.t[:, PC_G1:PC_G1 + 8], final=True, s=s)
        print("bass instructions:", k.ninst, "dma sems:", k.ndsem)
    return nc


def make_consts():
    i = np.arange(128)
    c32 = np.zeros((128, 4 * 128 + 4 * 128 + 4 * NR), np.float32)
    c32[:, 0:128] = np.where(i[:, None] > i[None, :], 0.0, NEG)
    c32[:, 128:256] = np.where(i[None, :] >= i[:, None], 0.0, NEG)
    c32[:, 256:384] = (i[:, None] <= i[None, :]).astype(np.float32)
    c32[:, 384:512] = 1.0
    c32[:, 512:640] = np.eye(128)
    for h in range(4):
        for r in range(-3, 16):
            c32[:, 1024 + h * NR + (r + 3)] = SLOPES[h] * (i - 128.0 * r)
    cb = np.zeros((128, 5 * 128), np.float32)
    cb[:, 0:128] = np.eye(128)
    cb[:, 128:256] = 1.0
    cb[0:64, 256:320] = 1.0
    cb[64:128, 320:384] = 1.0
    cb[:, 384:512] = (i[:, None] > i[None, :]).astype(np.float32)
    cb[:, 512:640] = (i[None, :] >= i[:, None]).astype(np.float32)
    return c32, cb


def make_params(inp):
    f = np.float32
    pcol = np.zeros((DEPTH + 1, 128, NPC), f)
    prow = np.zeros((DEPTH, 128, NPR), f)
    for l in range(DEPTH):
        pcol[l, :, PC_G1:PC_G1 + 8] = inp["norm1_g"][l].reshape(8, 128).T
        pcol[l, :, PC_G2:PC_G2 + 8] = inp["norm2_g"][l].reshape(8, 128).T
        pcol[l, :, PC_CB:PC_CB + 2] = inp["conv_dw_b"][l].reshape(2, 128).T
        pcol[l, :, PC_LG:PC_LG + 2] = inp["conv_ln_g"][l].reshape(2, 128).T
        pcol[l, :, PC_LB:PC_LB + 2] = inp["conv_ln_b"][l].reshape(2, 128).T
        pcol[l, :, PC_DW:PC_DW + 62] = inp["conv_dw_w"][l].reshape(31, 2, 128).transpose(2, 1, 0).reshape(128, 62)
        pcol[l, :, PC_GW:PC_GW + 24] = inp["gdn_conv_w"][l].reshape(4, 6, 128).transpose(2, 1, 0).reshape(128, 24)
        prow[l, :, PR_GNW:PR_GNW + 64] = inp["gdn_norm_w"][l][None, :]
        prow[l, :, PR_SUB:PR_SUB + 128] = inp["diff_subln_w"][l][None, :]
        prow[l, :, PR_LAM:PR_LAM + 256] = inp["diff_lambda"][l].reshape(1, 256)
        prow[l, :, PR_ALOG:PR_ALOG + 4] = inp["gdn_a_log"][l][None, :]
        prow[l, :, PR_DTB:PR_DTB + 4] = inp["gdn_dt_bias"][l][None, :]
    pcol[DEPTH, :, PC_G1:PC_G1 + 8] = inp["final_norm_g"].reshape(8, 128).T
    return pcol, prow


def kernel(**inp):
    ncores = 8
    x = np.asarray(inp["x"], np.float32)
    B = x.shape[0]
    nseq = B // ncores
    c32, cb = make_consts()
    pcol, prow = make_params({k_: np.asarray(v, np.float32) for k_, v in inp.items()})
    xT = np.ascontiguousarray(x.transpose(0, 2, 1))
    nc = build_program(nseq=nseq)
    shared = {
        "w_in": np.ascontiguousarray(inp["w_in"], dtype=np.float32),
        "w_out": np.ascontiguousarray(inp["w_out"], dtype=np.float32),
        "w_ffn_in": np.ascontiguousarray(inp["w_ffn_in"], dtype=np.float32),
        "w_ffn_out": np.ascontiguousarray(inp["w_ffn_out"], dtype=np.float32),
        "pcol": pcol, "prow": prow, "c32": c32, "cb": cb,
    }
    in_maps = [dict(shared, xT=xT[c * nseq:(c + 1) * nseq]) for c in range(ncores)]
    res = run_bass_kernel_spmd(nc, in_maps, core_ids=list(range(ncores)))
    outT = np.concatenate([r["outT"] for r in res.results], axis=0)
    return np.ascontiguousarray(outT.transpose(0, 2, 1)).astype(np.float32)
```

```python
import math
import os
GSTOP = int(os.environ.get('GSTOP', '99'))
GCUT = int(os.environ.get('GCUT', '0'))
from contextlib import ExitStack, contextmanager

import numpy as np
import concourse.bass as bass
import concourse.mybir as mybir
from concourse.bass_utils import run_bass_kernel_spmd

F32 = mybir.dt.float32
BF16 = mybir.dt.bfloat16
AF = mybir.ActivationFunctionType
ALU = mybir.AluOpType
AX = mybir.AxisListType

D = 1024
T = 2048
NT = 16
DEPTH = 4
DFF = 2816
NFC = 22
INW = 3080
RMS_EPS = 1e-6
LN_EPS = 1e-5
NEG = -30000.0
SLOPES = [(2.0 ** (-8.0 / 4)) ** (i + 1) for i in range(4)]
NR = 19

PC_G1, PC_G2, PC_CB, PC_LG, PC_LB, PC_DW, PC_GW = 0, 8, 16, 18, 20, 22, 84
NPC = 84 + 24
PR_GNW, PR_SUB, PR_LAM, PR_ALOG, PR_DTB = 0, 64, 192, 448, 452
NPR = 456


class Buf:
    __slots__ = ("t", "lw", "rs", "dsem", "dkey", "dcnt", "name", "psum")

    def __init__(self, t, name, psum=False):
        self.t = t
        self.name = name
        self.psum = psum
        self.lw = None
        self.rs = {}
        self.dsem = None
        self.dkey = None
        self.dcnt = 0

    def view(self, name=None):
        return Buf(self.t, name or self.name)


class K:
    def __init__(self, nc, es):
        self.nc = nc
        self.es = es
        self.engs = {"pe": nc.tensor, "act": nc.scalar, "dve": nc.vector, "pool": nc.gpsimd, "sp": nc.sync}
        self.semh = {}
        self.cnt = {}
        self.pend = {}
        self.seen = {}
        for e in self.engs:
            self.semh[e] = es.enter_context(nc.semaphore("s_" + e))
            self.cnt[e] = 0
            self.pend[e] = False
            self.seen[e] = {}
        self.ndsem = 0
        self.nalloc = 0
        self.ninst = 0

    def sb(self, shape, dt, name=None, es=None):
        self.nalloc += 1
        name = "sb%d_%s" % (self.nalloc, name or "t")
        t = (es or self.es).enter_context(self.nc.sbuf_tensor(name, list(shape), dt))
        return Buf(t, name)

    def ps(self, name):
        t = self.es.enter_context(self.nc.psum_tensor(name, [128, 512], F32))
        return Buf(t, name, psum=True)

    @contextmanager
    def scope(self):
        es = ExitStack()
        k = self

        class S:
            def sb(self, shape, dt, name=None):
                return k.sb(shape, dt, name, es=es)

        try:
            yield S()
            self.barrier()
        finally:
            es.close()

    def _waits(self, e, deps, skipkey=None):
        eng = self.engs[e]
        for key, v in deps.items():
            if key == skipkey:
                continue
            if key == e and e in ("pe", "sp", "pool"):
                continue
            if self.seen[e].get(key, 0) >= v:
                continue
            eng.wait_ge(self.semh[key], v)
            self.seen[e][key] = v
            self.ninst += 1

    @staticmethod
    def _deps(reads, writes, e=None):
        deps = {}
        for b in reads:
            if b.lw is not None:
                deps[b.lw[0]] = max(deps.get(b.lw[0], 0), b.lw[1])
            if b.psum:
                for key, v in b.rs.items():
                    if key != e:
                        deps[key] = max(deps.get(key, 0), v)
        for b in writes:
            if b.lw is not None:
                deps[b.lw[0]] = max(deps.get(b.lw[0], 0), b.lw[1])
            for key, v in b.rs.items():
                deps[key] = max(deps.get(key, 0), v)
        return deps

    def op(self, e, name, reads=(), writes=(), inc=True, **kw):
        self.nops = getattr(self, "nops", 0) + 1
        if GCUT and self.nops > GCUT:
            return None
        deps = self._deps(reads, writes, e)
        self._waits(e, deps)
        ins = getattr(self.engs[e], name)(**kw)
        self.ninst += 1
        idx = self.cnt[e] + 1
        if inc:
            ins.then_inc(self.semh[e], 1)
            self.cnt[e] = idx
            self.pend[e] = False
        else:
            self.pend[e] = True
        for b in reads:
            b.rs[e] = idx
        for b in writes:
            b.lw = (e, idx)
            b.rs = {}
        return ins

    def _dsem(self, b):
        if b.dsem is None:
            self.ndsem += 1
            b.dkey = "d%d_%s" % (self.ndsem, b.name)
            b.dsem = self.es.enter_context(self.nc.semaphore(b.dkey))
            self.semh[b.dkey] = b.dsem
        return b.dsem

    def load(self, q, buf, out, in_):
        sem = self._dsem(buf)
        deps = self._deps((), (buf,))
        self._waits(q, deps, skipkey=buf.dkey)
        self.engs[q].dma_start(out=out, in_=in_).then_inc(sem, 16)
        self.ninst += 1
        buf.dcnt += 16
        buf.lw = (buf.dkey, buf.dcnt)
        buf.rs = {}

    def store(self, q, buf, out, in_):
        sem = self._dsem(buf)
        deps = self._deps((buf,), ())
        self._waits(q, deps, skipkey=None)
        self.engs[q].dma_start(out=out, in_=in_).then_inc(sem, 16)
        self.ninst += 1
        buf.dcnt += 16
        buf.rs[buf.dkey] = buf.dcnt

    def barrier(self):
        assert GCUT or not any(self.pend.values()), self.pend
        for e in ("pe", "act", "dve"):
            deps = {d: self.cnt[d] for d in ("pe", "act", "dve") if d != e and self.cnt[d] > 0}
            self._waits(e, deps)

    def wait_all_stores(self, e, bufs):
        deps = {}
        for b in bufs:
            if b.dkey is not None:
                deps[b.dkey] = b.dcnt
        self._waits(e, deps)


def build_program(nseq=4, depth=DEPTH, dbg=False, mixers=("conv", "gdn", "attn"), do_ffn=True):
    nc = bass.Bass("TRN2", target_bir_lowering=False)
    dr = {}

    def din(name, shape, dt=F32):
        dr[name] = nc.dram_tensor(name, list(shape), dt, kind="ExternalInput").ap()
        return dr[name]

    xT_d = din("xT", [nseq, D, T])
    w_in_d = din("w_in", [DEPTH, D, INW])
    w_out_d = din("w_out", [DEPTH, D, D])
    w_f1_d = din("w_ffn_in", [DEPTH, D, 2 * DFF])
    w_f2_d = din("w_ffn_out", [DEPTH, DFF, D])
    pcol_d = din("pcol", [DEPTH + 1, 128, NPC])
    prow_d = din("prow", [DEPTH, 128, NPR])
    c32_d = din("c32", [128, 4 * 128 + 4 * 128 + 4 * NR])
    cb_d = din("cb", [128, 5 * 128])
    out_d = nc.dram_tensor("outT", [nseq, D, T], F32, kind="ExternalOutput").ap()
    if dbg:
        dbg_d = nc.dram_tensor("dbg", [8, 128, T], F32, kind="ExternalOutput").ap()

    with ExitStack() as es:
        k = K(nc, es)
        xT_t = k.sb([128, 8, T], F32, "xT")
        hT_t = k.sb([128, 8, T], BF16, "hT")
        xT = [xT_t.view("xT%d" % g) for g in range(4)]
        hT = [hT_t.view("hT%d" % g) for g in range(4)]
        NS = 6
        slots = [k.sb([128, 2048], BF16, "slot%d" % i) for i in range(NS)]
        c32 = k.sb([128, 4 * 128 + 4 * 128 + 4 * NR], F32, "c32")
        cb = k.sb([128, 5 * 128], BF16, "cb")
        pcol = k.sb([128, NPC], F32, "pcol")
        prow = k.sb([128, NPR], F32, "prow")
        wab = k.sb([128, 8, 8], BF16, "wab")
        P = [k.ps("P%d" % i) for i in range(8)]
        epsb = k.sb([128, 4], F32, "epsb")
        k.op("dve", "memset", writes=[epsb], ap=epsb.t[:, 0:1], constant=RMS_EPS)
        k.op("dve", "memset", writes=[epsb], ap=epsb.t[:, 1:2], constant=LN_EPS)
        k.op("dve", "memset", writes=[epsb], ap=epsb.t[:, 2:3], constant=1.0)
        k.op("dve", "memset", writes=[epsb], ap=epsb.t[:, 3:4], constant=0.0)

        k.load("sp", c32, c32.t[:], c32_d)
        k.load("pool", cb, cb.t[:], cb_d)
        maskC = c32.t[:, 0:128]
        maskU = c32.t[:, 128:256]
        triT = c32.t[:, 256:384]
        ones32 = c32.t[:, 384:512]
        ident32 = c32.t[:, 512:640]
        abias = lambda h, r: c32.t[:, 1024 + h * NR + (r + 3):1024 + h * NR + (r + 3) + 1]
        ident = cb.t[:, 0:128]
        onesb = cb.t[:, 128:256]
        blk64 = cb.t[:, 256:384]
        strict = cb.t[:, 384:512]
        triU = cb.t[:, 512:640]

        slot_i = [0]

        def next_slot():
            s = slots[slot_i[0] % NS]
            slot_i[0] += 1
            return s

        def wload(slot, dst, src):
            k.load("pool", slot, dst, src)

        def w_in_cols(l, c0, n):
            return w_in_d[l, :, c0:c0 + n].rearrange("(c p) e -> p c e", p=128)

        def proj_fm(pb, wslot, wv, m0, m, tg, act=hT, kc=8, act_t=None):
            at = act_t if act_t is not None else hT_t.t
            for c in range(kc):
                k.op("pe", "matmul", reads=[wslot, act[tg]], writes=[pb], inc=(c == kc - 1),
                     out=pb.t[0:m, :], lhsT=wv[:, c, m0:m0 + m], rhs=at[:, c, tg * 512:(tg + 1) * 512],
                     start=(c == 0), stop=(c == kc - 1))

        def phase_norm(gcol, final=False, s=0):
            with k.scope() as sc:
                sq = [sc.sb([128, 8, 512], BF16) for _ in range(2)]
                rs = [sc.sb([128, 512], F32) for _ in range(2)]
                ob = [sc.sb([128, 8, 512], F32) for _ in range(2)] if final else None
                for tg in range(4):
                    ts = slice(tg * 512, (tg + 1) * 512)
                    s_, r_, pb = sq[tg % 2], rs[tg % 2], P[tg % 2]
                    k.op("act", "activation", reads=[xT[tg]], writes=[s_],
                         out=s_.t[:], in_=xT_t.t[:, :, ts], func=AF.Square)
                    for c in range(8):
                        k.op("pe", "matmul", reads=[s_, cb], writes=[pb], inc=(c == 7),
                             out=pb.t[:], lhsT=onesb, rhs=s_.t[:, c, :], start=(c == 0), stop=(c == 7))
                    k.op("act", "activation", reads=[pb, epsb], writes=[r_],
                         out=r_.t[:], in_=pb.t[:], func=AF.Sqrt, scale=1.0 / D, bias=epsb.t[:, 0:1])
                    k.op("dve", "reciprocal", reads=[r_], writes=[r_], out=r_.t[:], in_=r_.t[:])
                    if not final:
                        for c in range(8):
                            k.op("dve", "scalar_tensor_tensor", reads=[xT[tg], r_, pcol], writes=[hT[tg]],
                                 out=hT_t.t[:, c, ts], in0=xT_t.t[:, c, ts], scalar=gcol[:, c:c + 1],
                                 in1=r_.t[:], op0=ALU.mult, op1=ALU.mult)
                    else:
                        o_ = ob[tg % 2]
                        for c in range(8):
                            k.op("dve", "scalar_tensor_tensor", reads=[xT[tg], r_, pcol], writes=[o_],
                                 out=o_.t[:, c, :], in0=xT_t.t[:, c, ts], scalar=gcol[:, c:c + 1],
                                 in1=r_.t[:], op0=ALU.mult, op1=ALU.mult)
                        k.store("sp", o_, out_d[s, :, ts].rearrange("(c p) t -> p c t", p=128), o_.t[:])
                if final:
                    k.wait_all_stores("sp", ob)
                    k.wait_all_stores("act", ob)
                    k.wait_all_stores("dve", ob)

        def out_proj(l, yb, yt, r0, kc):
            sl = next_slot()
            wv = sl.t[:, 0:kc * 1024].rearrange("p (c e) -> p c e", c=kc)
            wload(sl, wv, w_out_d[l, r0 * 128:(r0 + kc) * 128, :].rearrange("(c p) e -> p c e", p=128))
            i = 0
            for dc in range(8):
                for tg in range(4):
                    pb = P[i % 2]
                    i += 1
                    ts = slice(tg * 512, (tg + 1) * 512)
                    for c in range(kc):
                        k.op("pe", "matmul", reads=[sl, yb], writes=[pb], inc=(c == kc - 1),
                             out=pb.t[:], lhsT=wv[:, c, dc * 128:(dc + 1) * 128], rhs=yt[:, c, ts],
                             start=(c == 0), stop=(c == kc - 1))
                    k.op("dve", "tensor_tensor", reads=[pb, xT[tg]], writes=[xT[tg]],
                         out=xT_t.t[:, dc, ts], in0=xT_t.t[:, dc, ts], in1=pb.t[:], op=ALU.add)

        def dbg_dump(yb, yt, c0, kc, sc):
            if not dbg:
                return
            tmp = sc.sb([128, 512], F32)
            for c in range(kc):
                for tg in range(4):
                    k.op("act", "activation", reads=[yb], writes=[tmp], out=tmp.t[:], in_=yt[:, c, tg * 512:(tg + 1) * 512], func=AF.Copy)
                    k.store("sp", tmp, dbg_d[c0 + c, :, tg * 512:(tg + 1) * 512], tmp.t[:])
            k.wait_all_stores("act", [tmp])

        def phase_conv(l):
            with k.scope() as sc:
                sv, sg = next_slot(), next_slot()
                wvv = sv.t[:].rearrange("p (c e) -> p c e", c=8)
                wgv = sg.t[:].rearrange("p (c e) -> p c e", c=8)
                wload(sv, wvv, w_in_cols(l, 0, 256))
                wload(sg, wgv, w_in_cols(l, 256, 256))
                glu = sc.sb([128, 2, 32 + T], BF16, "glu")
                yc = sc.sb([128, 2, T], BF16, "yconv")
                sig = [sc.sb([128, 512], F32) for _ in range(2)]
                k.op("dve", "memset", writes=[glu], ap=glu.t[:, :, 0:32], constant=0.0)
                n = 0
                for tg in range(4):
                    for j in range(2):
                        pv, pg = P[(n * 2) % 4], P[(n * 2 + 1) % 4]
                        sg_ = sig[n % 2]
                        n += 1
                        proj_fm(pg, sg, wgv, j * 128, 128, tg)
                        proj_fm(pv, sv, wvv, j * 128, 128, tg)
                        k.op("act", "activation", reads=[pg], writes=[sg_], out=sg_.t[:], in_=pg.t[:], func=AF.Sigmoid)
                        k.op("dve", "tensor_tensor", reads=[pv, sg_], writes=[glu],
                             out=glu.t[:, j, 32 + tg * 512:32 + (tg + 1) * 512], in0=pv.t[:], in1=sg_.t[:], op=ALU.mult)
                dg = [sc.sb([128, 128], BF16) for _ in range(4)]
                cv = [sc.sb([128, 2, 512], F32, "cv%d" % i) for i in range(4)]
                cq = [sc.sb([128, 2, 512], F32, "cq%d" % i) for i in range(4)]
                n = 0
                for j in range(2):
                    for tap in range(31):
                        d_ = dg[n % 4]
                        n += 1
                        k.op("dve", "tensor_scalar", reads=[cb, pcol], writes=[d_],
                             out=d_.t[:], in0=ident, scalar1=pcol.t[:, PC_DW + j * 31 + tap:PC_DW + j * 31 + tap + 1],
                             scalar2=None, op0=ALU.mult)
                        for tg in range(4):
                            pb = P[4 + tg]
                            k.op("pe", "matmul", reads=[d_, glu], writes=[pb], inc=(tap == 30 or tg == 3),
                                 out=pb.t[:], lhsT=d_.t[:], rhs=glu.t[:, j, 2 + tap + tg * 512:2 + tap + (tg + 1) * 512],
                                 start=(tap == 0), stop=(tap == 30))
                    for tg in range(4):
                        pb = P[4 + tg]
                        k.op("act", "activation", reads=[pb, pcol], writes=[cv[tg]],
                             out=cv[tg].t[:, j, :], in_=pb.t[:], func=AF.Identity, bias=pcol.t[:, PC_CB + j:PC_CB + j + 1])
                        k.op("act", "activation", reads=[cv[tg]], writes=[cq[tg]],
                             out=cq[tg].t[:, j, :], in_=cv[tg].t[:, j, :], func=AF.Square)
                tmp = [sc.sb([128, 512], F32) for _ in range(4)]
                for tg in range(4):
                    pm, pq = P[(tg * 2) % 4], P[(tg * 2 + 1) % 4]
                    for j in range(2):
                        k.op("pe", "matmul", reads=[cv[tg], c32], writes=[pm], inc=(j == 1),
                             out=pm.t[:], lhsT=ones32, rhs=cv[tg].t[:, j, :], start=(j == 0), stop=(j == 1))
                    for j in range(2):
                        k.op("pe", "matmul", reads=[cq[tg], c32], writes=[pq], inc=(j == 1),
                             out=pq.t[:], lhsT=ones32, rhs=cq[tg].t[:, j, :], start=(j == 0), stop=(j == 1))
                    mean, var = tmp[0], tmp[1]
                    k.op("act", "activation", reads=[pm], writes=[mean], out=mean.t[:], in_=pm.t[:], func=AF.Copy, scale=1.0 / 256)
                    k.op("act", "activation", reads=[mean], writes=[var], out=var.t[:], in_=mean.t[:], func=AF.Square)
                    k.op("dve", "scalar_tensor_tensor", reads=[pq, var], writes=[var],
                         out=var.t[:], in0=pq.t[:], scalar=1.0 / 256, in1=var.t[:], op0=ALU.mult, op1=ALU.subtract)
                    k.op("act", "activation", reads=[var, epsb], writes=[var],
                         out=var.t[:], in_=var.t[:], func=AF.Sqrt, bias=epsb.t[:, 1:2])
                    k.op("dve", "reciprocal", reads=[var], writes=[var], out=var.t[:], in_=var.t[:])
                    for j in range(2):
                        t_ = tmp[2 + j]
                        k.op("dve", "tensor_tensor", reads=[cv[tg], mean], writes=[t_],
                             out=t_.t[:], in0=cv[tg].t[:, j, :], in1=mean.t[:], op=ALU.subtract)
                        k.op("dve", "tensor_tensor", reads=[t_, var], writes=[t_],
                             out=t_.t[:], in0=t_.t[:], in1=var.t[:], op=ALU.mult)
                        k.op("act", "activation", reads=[t_, pcol], writes=[yc],
                             out=yc.t[:, j, tg * 512:(tg + 1) * 512], in_=t_.t[:], func=AF.Silu,
                             scale=pcol.t[:, PC_LG + j:PC_LG + j + 1], bias=pcol.t[:, PC_LB + j:PC_LB + j + 1])
                dbg_dump(yc, yc.t, 0, 2, sc)
                out_proj(l, yc, yc.t, 0, 2)

        def phase_attn(l, lam_init):
            scale = 64 ** -0.5
            with k.scope() as sc:
                lt = sc.sb([128, 2, 64], F32)
                ls = sc.sb([128, 2], F32)
                nlam = sc.sb([128, 1], F32)
                k.op("dve", "tensor_tensor", reads=[prow], writes=[lt], out=lt.t[:, 0, :],
                     in0=prow.t[:, PR_LAM:PR_LAM + 64], in1=prow.t[:, PR_LAM + 64:PR_LAM + 128], op=ALU.mult)
                k.op("dve", "tensor_tensor", reads=[prow], writes=[lt], out=lt.t[:, 1, :],
                     in0=prow.t[:, PR_LAM + 128:PR_LAM + 192], in1=prow.t[:, PR_LAM + 192:PR_LAM + 256], op=ALU.mult)
                k.op("dve", "tensor_reduce", reads=[lt], writes=[ls], out=ls.t[:], in_=lt.t[:], axis=AX.X, op=ALU.add)
                k.op("act", "activation", reads=[ls], writes=[ls], out=ls.t[:], in_=ls.t[:], func=AF.Exp)
                k.op("dve", "scalar_tensor_tensor", reads=[ls], writes=[nlam], out=nlam.t[:], in0=ls.t[:, 1:2],
                     scalar=-lam_init, in1=ls.t[:, 0:1], op0=ALU.add, op1=ALU.subtract)
                wrow = sc.sb([128, 128], F32)
                k.op("dve", "tensor_scalar", reads=[prow], writes=[wrow], out=wrow.t[:], in0=prow.t[:, PR_SUB:PR_SUB + 128],
                     scalar1=1.0 - lam_init, scalar2=None, op0=ALU.mult)

                qT = sc.sb([128, 2, T], BF16, "qT")
                kT = sc.sb([128, 2, T], BF16, "kT")
                V1 = sc.sb([128, NT, 2, 132], BF16, "V1")
                yd = sc.sb([128, 2, T], BF16, "ydiff")
                pt = [sc.sb([128, 512], BF16, "pt%d" % i) for i in range(3)]
                oh = [sc.sb([128, 4, 132], F32, "oh%d" % i) for i in range(2)]
                rc = sc.sb([128, 8], F32)
                ssq = sc.sb([128, 8], F32)
                ob4 = sc.sb([128, 4, 128], F32, "ob4")
                oc4 = sc.sb([128, 4, 128], F32, "oc4")
                yb4 = sc.sb([128, 4, 128], BF16, "yb4")
                for hp in range(2):
                    sq_, sk_, sv_ = next_slot(), next_slot(), next_slot()
                    views = []
                    for s_, c0 in ((sq_, 1544), (sk_, 2056), (sv_, 2568)):
                        v_ = s_.t[:].rearrange("p (c e) -> p c e", c=8)
                        wload(s_, v_, w_in_cols(l, c0 + hp * 256, 256))
                        views.append(v_)
                    wq, wk, wv = views
                    k.op("dve", "memset", writes=[V1], ap=V1.t[:, :, :, 128:129], constant=1.0)
                    n = 0
                    for hh in range(2):
                        for tg in range(4):
                            pb = P[n % 2]
                            n += 1
                            proj_fm(pb, sq_, wq, hh * 128, 128, tg)
                            k.op("act", "activation", reads=[pb], writes=[qT], out=qT.t[:, hh, tg * 512:(tg + 1) * 512],
                                 in_=pb.t[:], func=AF.Copy, scale=scale)
                            pb = P[n % 2]
                            n += 1
                            proj_fm(pb, sk_, wk, hh * 128, 128, tg)
                            k.op("dve", "tensor_copy", reads=[pb], writes=[kT], out=kT.t[:, hh, tg * 512:(tg + 1) * 512],
                                 in_=pb.t[:])
                    for tt in range(NT):
                        pb = P[tt % 2]
                        for c in range(8):
                            k.op("pe", "matmul", reads=[sv_, hT[tt // 4]], writes=[pb], inc=(c == 7),
                                 out=pb.t[:, 0:256], lhsT=hT_t.t[:, c, tt * 128:(tt + 1) * 128], rhs=wv[:, c, :],
                                 start=(c == 0), stop=(c == 7))
                        k.op("act", "activation", reads=[pb], writes=[V1], out=V1.t[:, tt, :, 0:128],
                             in_=pb.t[:, 0:256].rearrange("p (h e) -> p h e", h=2), func=AF.Copy)
                    for hh in range(2):
                        h = hp * 2 + hh
                        W = 4 if SLOPES[h] * 511 <= 40 else (2 if SLOPES[h] * 255 <= 70 else 1)
                        items = [(g, c, j) for g in range(4) for c in range(2) for j in range(4 * g + 4)]
                        acc = [P[2], P[3], P[4], P[5]]

                        def emit_qk(i):
                            g, c, j = items[i]
                            pr = slice(c * 64, (c + 1) * 64)
                            qb0 = max(j, 4 * g)
                            nq = 4 * g + 4 - qb0
                            pS = P[6 + (i % 2)]
                            k.op("pe", "matmul", reads=[kT, qT], writes=[pS],
                                 out=pS.t[:, 0:nq * 128], lhsT=kT.t[pr, hh, j * 128:(j + 1) * 128],
                                 rhs=qT.t[pr, hh, qb0 * 128:(qb0 + nq) * 128], start=True, stop=True)

                        def emit_rest(i):
                            g, c, j = items[i]
                            qb0 = max(j, 4 * g)
                            nq = 4 * g + 4 - qb0
                            pS = P[6 + (i % 2)]
                            p_ = pt[i % 3]
                            qi = 0
                            while qi < nq:
                                qb = qb0 + qi
                                ref = (qb // W) * W
                                n_ = min(nq - qi, ref + W - qb)
                                k.op("act", "activation", reads=[pS, c32], writes=[p_], out=p_.t[:, qi * 128:(qi + n_) * 128],
                                     in_=pS.t[:, qi * 128:(qi + n_) * 128], func=AF.Exp, bias=abias(h, ref - j))
                                qi += n_
                            if j >= 4 * g:
                                k.op("dve", "tensor_tensor", reads=[p_, cb], writes=[p_], out=p_.t[:, 0:128],
                                     in0=p_.t[:, 0:128], in1=triU, op=ALU.mult)
                            for qi in range(nq):
                                qb = qb0 + qi
                                a_ = acc[qb % 4]
                                k.op("pe", "matmul", reads=[p_, V1], writes=[a_], inc=(qi == nq - 1),
                                     out=a_.t[:, 0:129], lhsT=p_.t[:, qi * 128:(qi + 1) * 128],
                                     rhs=V1.t[:, j, hh, 0:129], start=(j == 0), stop=(j == qb))
                            if j == 4 * g + 3:
                                o_ = oh[c]
                                for a in range(4):
                                    if a % 2 == 0:
                                        k.op("act", "activation", reads=[acc[a]], writes=[o_], out=o_.t[:, a, 0:129],
                                             in_=acc[a].t[:, 0:129], func=AF.Copy)
                                    else:
                                        k.op("dve", "tensor_copy", reads=[acc[a]], writes=[o_], out=o_.t[:, a, 0:129],
                                             in_=acc[a].t[:, 0:129])
                                if c == 1:
                                    post(g)

                        def post(g):
                            for c in range(2):
                                k.op("dve", "reciprocal", reads=[oh[c]], writes=[rc], out=rc.t[:, c * 4:(c + 1) * 4],
                                     in_=oh[c].t[:, :, 128])
                            k.op("dve", "tensor_tensor", reads=[oh[0], rc], writes=[ob4], out=ob4.t[:], in0=oh[0].t[:, :, 0:128],
                                 in1=rc.t[:, 0:4].unsqueeze(2).to_broadcast([128, 4, 128]), op=ALU.mult)
                            k.op("dve", "tensor_tensor", reads=[oh[1], rc], writes=[oc4], out=oc4.t[:], in0=oh[1].t[:, :, 0:128],
                                 in1=rc.t[:, 4:8].unsqueeze(2).to_broadcast([128, 4, 128]), op=ALU.mult)
                            k.op("dve", "scalar_tensor_tensor", reads=[ob4, oc4, nlam], writes=[ob4], out=ob4.t[:], in0=oc4.t[:],
                                 scalar=nlam.t[:, 0:1], in1=ob4.t[:], op0=ALU.mult, op1=ALU.add)
                            k.op("dve", "tensor_tensor", reads=[ob4], writes=[oc4], out=oc4.t[:], in0=ob4.t[:], in1=ob4.t[:], op=ALU.mult)
                            k.op("dve", "tensor_reduce", reads=[oc4], writes=[ssq], out=ssq.t[:, 0:4], in_=oc4.t[:], axis=AX.X, op=ALU.add)
                            k.op("act", "activation", reads=[ssq, epsb], writes=[ssq], out=ssq.t[:, 4:8], in_=ssq.t[:, 0:4],
                                 func=AF.Ln, scale=1.0 / 128, bias=epsb.t[:, 0:1])
                            k.op("act", "activation", reads=[ssq], writes=[ssq], out=ssq.t[:, 4:8], in_=ssq.t[:, 4:8],
                                 func=AF.Exp, scale=-0.5)
                            k.op("dve", "tensor_tensor", reads=[ob4, ssq], writes=[ob4], out=ob4.t[:], in0=ob4.t[:],
                                 in1=ssq.t[:, 4:8].unsqueeze(2).to_broadcast([128, 4, 128]), op=ALU.mult)
                            k.op("dve", "tensor_tensor", reads=[ob4, wrow], writes=[yb4], out=yb4.t[:], in0=ob4.t[:],
                                 in1=wrow.t[:, :].unsqueeze(1).to_broadcast([128, 4, 128]), op=ALU.mult)
                            pT = P[g % 2]
                            for qi in range(4):
                                k.op("pe", "matmul", reads=[yb4, cb], writes=[pT], out=pT.t[:, qi * 128:(qi + 1) * 128], lhsT=yb4.t[:, qi, :],
                                     rhs=ident, start=True, stop=True)
                            k.op("act", "activation", reads=[pT], writes=[yd], out=yd.t[:, hh, g * 512:(g + 1) * 512],
                                 in_=pT.t[:], func=AF.Copy)

                        emit_qk(0)
                        for i in range(len(items)):
                            if i + 1 < len(items):
                                emit_qk(i + 1)
                            emit_rest(i)
                    dbg_dump(yd, yd.t, 4 + hp * 2, 2, sc)
                    out_proj(l, yd, yd.t, 4 + hp * 2, 2)

        def phase_ffn(l):
            with k.scope() as sc:
                actT = sc.sb([128, NFC, 1024], BF16, "actT")
                sg = [sc.sb([128, 512], BF16) for _ in range(2)]
                for half in range(2):
                    n = 0
                    for fb in range(NFC):
                        sl = next_slot()
                        wv = sl.t[:].rearrange("p (c e) -> p c e", c=8)
                        wload(sl, wv[:, :, 0:128], w_f1_d[l, :, fb * 128:(fb + 1) * 128].rearrange("(c p) e -> p c e", p=128))
                        wload(sl, wv[:, :, 128:256], w_f1_d[l, :, DFF + fb * 128:DFF + (fb + 1) * 128].rearrange("(c p) e -> p c e", p=128))
                        for t2 in range(2):
                            tg = half * 2 + t2
                            pg, pu = P[(n * 2) % 4], P[(n * 2 + 1) % 4]
                            s_ = sg[n % 2]
                            n += 1
                            proj_fm(pg, sl, wv, 0, 128, tg)
                            proj_fm(pu, sl, wv, 128, 128, tg)
                            k.op("act", "activation", reads=[pg], writes=[s_], out=s_.t[:], in_=pg.t[:], func=AF.Silu)
                            k.op("dve", "tensor_tensor", reads=[pu, s_], writes=[actT], out=actT.t[:, fb, t2 * 512:(t2 + 1) * 512],
                                 in0=pu.t[:], in1=s_.t[:], op=ALU.mult)
                    for dg_ in range(4):
                        banks = [P[4 + i] for i in range(4)]
                        for kb in range(3):
                            nk = min(8, NFC - kb * 8)
                            sl = next_slot()
                            wv = sl.t[:].rearrange("p (c e) -> p c e", c=8)
                            wload(sl, wv[:, 0:nk, :], w_f2_d[l, kb * 1024:kb * 1024 + nk * 128, dg_ * 256:(dg_ + 1) * 256]
                                  .rearrange("(c p) e -> p c e", p=128))
                            for ci in range(nk):
                                fc = kb * 8 + ci
                                for dc2 in range(2):
                                    for t2 in range(2):
                                        pb = banks[dc2 * 2 + t2]
                                        k.op("pe", "matmul", reads=[sl, actT], writes=[pb], inc=(fc == NFC - 1 or (ci == nk - 1 and dc2 == 1 and t2 == 1)),
                                             out=pb.t[:], lhsT=wv[:, ci, dc2 * 128:(dc2 + 1) * 128],
                                             rhs=actT.t[:, fc, t2 * 512:(t2 + 1) * 512], start=(fc == 0), stop=(fc == NFC - 1))
                        for dc2 in range(2):
                            for t2 in range(2):
                                pb = banks[dc2 * 2 + t2]
                                tg = half * 2 + t2
                                dc = dg_ * 2 + dc2
                                ts = slice(tg * 512, (tg + 1) * 512)
                                k.op("dve", "tensor_tensor", reads=[pb, xT[tg]], writes=[xT[tg]],
                                     out=xT_t.t[:, dc, ts], in0=xT_t.t[:, dc, ts], in1=pb.t[:], op=ALU.add)

        def phase_gdn(l):
            with k.scope() as sc:
                s_q, s_k, s_v, s_z = next_slot(), next_slot(), next_slot(), next_slot()
                wviews = []
                for s_, c0 in ((s_q, 512), (s_k, 768), (s_v, 1024), (s_z, 1280)):
                    v_ = s_.t[:].rearrange("p (c e) -> p c e", c=8)
                    wload(s_, v_, w_in_cols(l, c0, 256))
                    wviews.append(v_)
                wq, wk, wv, wz = wviews
                k.load("pool", wab, wab.t[:], w_in_cols(l, 1536, 8))
                abp = P[0]
                for tt in range(NT):
                    for c in range(8):
                        k.op("pe", "matmul", reads=[wab, hT[tt // 4]], writes=[abp], inc=(c == 7),
                             out=abp.t[:, tt * 8:(tt + 1) * 8], lhsT=hT_t.t[:, c, tt * 128:(tt + 1) * 128], rhs=wab.t[:, c, :],
                             start=(c == 0), stop=(c == 7))
                abv = abp.t[:, 0:128].rearrange("p (t e) -> p t e", e=8)
                gt = sc.sb([128, NT, 4], F32, "gt")
                bt = sc.sb([128, NT, 4], F32, "bt")
                nbt = sc.sb([128, NT, 4], F32, "nbt")
                gc = sc.sb([128, NT, 4], F32, "gc")
                ed = sc.sb([128, NT, 4], F32, "ed")
                bew = sc.sb([128, NT, 4], F32, "bew")
                na = sc.sb([128, 4], F32, "na")
                for h in range(4):
                    k.op("dve", "tensor_scalar", reads=[abp, prow], writes=[gt], out=gt.t[:, :, h], in0=abv[:, :, h],
                         scalar1=prow.t[:, PR_DTB + h:PR_DTB + h + 1], scalar2=None, op0=ALU.add)
                k.op("act", "activation", reads=[gt], writes=[gt], out=gt.t[:], in_=gt.t[:], func=AF.Exp)
                k.op("act", "activation", reads=[gt, epsb], writes=[gt], out=gt.t[:], in_=gt.t[:], func=AF.Ln, bias=epsb.t[:, 2:3])
                k.op("act", "activation", reads=[prow], writes=[na], out=na.t[:], in_=prow.t[:, PR_ALOG:PR_ALOG + 4], func=AF.Exp)
                for h in range(4):
                    k.op("dve", "tensor_scalar", reads=[gt, na], writes=[gt], out=gt.t[:, :, h], in0=gt.t[:, :, h],
                         scalar1=na.t[:, h:h + 1], scalar2=-1.0, op0=ALU.mult, op1=ALU.mult)
                k.op("act", "activation", reads=[abp], writes=[bt], out=bt.t[:], in_=abv[:, :, 4:8], func=AF.Sigmoid)
                k.op("dve", "tensor_scalar", reads=[bt], writes=[nbt], out=nbt.t[:], in0=bt.t[:], scalar1=-1.0, scalar2=None, op0=ALU.mult)
                gflat = gt.t[:].rearrange("p t h -> p (t h)")
                pcs = P[1]
                k.op("pe", "matmul", reads=[gt, c32], writes=[pcs], out=pcs.t[:, 0:64], lhsT=triT, rhs=gflat, start=True, stop=True)
                k.op("pe", "matmul", reads=[gt, c32], writes=[pcs], out=pcs.t[:, 64:128], lhsT=ones32, rhs=gflat, start=True, stop=True)
                gcf = gc.t[:].rearrange("p t h -> p (t h)")
                edf = ed.t[:].rearrange("p t h -> p (t h)")
                bewf = bew.t[:].rearrange("p t h -> p (t h)")
                k.op("act", "activation", reads=[pcs], writes=[gc], out=gcf, in_=pcs.t[:, 0:64], func=AF.Copy)
                k.op("dve", "tensor_tensor", reads=[pcs, gc], writes=[ed], out=edf, in0=pcs.t[:, 64:128], in1=gcf, op=ALU.subtract)
                k.op("act", "activation", reads=[ed], writes=[ed], out=edf, in_=edf, func=AF.Exp)
                k.op("act", "activation", reads=[gc], writes=[bew], out=bewf, in_=gcf, func=AF.Exp)
                k.op("dve", "tensor_tensor", reads=[bew, bt], writes=[bew], out=bewf, in0=bewf, in1=bt.t[:].rearrange("p t h -> p (t h)"), op=ALU.mult)

                print("ops at stage1:", k.nops)
                if GSTOP <= 1:
                    return
                qT = sc.sb([128, 2, T], BF16, "gq")
                kT = sc.sb([128, 2, T], BF16, "gk")
                vT = sc.sb([128, 2, T], BF16, "gv")
                yg = sc.sb([128, 2, T], BF16, "yg")
                with k.scope() as sc2:
                    ut = [sc2.sb([128, 4 + T], BF16, "ut%d" % i) for i in range(2)]
                    dgs = [sc2.sb([128, 128], BF16) for _ in range(4)]
                    sl32 = [sc2.sb([128, 512], F32) for _ in range(2)]
                    sqb = [sc2.sb([128, 512], BF16) for _ in range(2)]
                    rn = [sc2.sb([128, 512], F32) for _ in range(2)]
                    for u_ in ut:
                        k.op("dve", "memset", writes=[u_], ap=u_.t[:, 0:4], constant=0.0)
                    n = 0
                    for kind, (slot_, wv_, dst) in enumerate(((s_q, wq, qT), (s_k, wk, kT), (s_v, wv, vT))):
                        for hp in range(2):
                            ch = kind * 2 + hp
                            u_ = ut[(kind * 2 + hp) % 2]
                            for tg in range(4):
                                pb = P[tg % 2]
                                proj_fm(pb, slot_, wv_, hp * 128, 128, tg)
                                k.op("act", "activation", reads=[pb], writes=[u_], out=u_.t[:, 4 + tg * 512:4 + (tg + 1) * 512],
                                     in_=pb.t[:], func=AF.Copy)
                            for tap in range(4):
                                k.op("dve", "tensor_scalar", reads=[cb, pcol], writes=[dgs[tap]], out=dgs[tap].t[:], in0=ident,
                                     scalar1=pcol.t[:, PC_GW + ch * 4 + tap:PC_GW + ch * 4 + tap + 1], scalar2=None, op0=ALU.mult)
                            for tg in range(4):
                                pb = P[4 + tg]
                                ts = slice(tg * 512, (tg + 1) * 512)
                                for tap in range(4):
                                    k.op("pe", "matmul", reads=[dgs[tap], u_], writes=[pb], inc=(tap == 3), out=pb.t[:], lhsT=dgs[tap].t[:],
                                         rhs=u_.t[:, 1 + tap + tg * 512:1 + tap + (tg + 1) * 512], start=(tap == 0), stop=(tap == 3))
                                if kind == 2:
                                    k.op("act", "activation", reads=[pb], writes=[dst], out=dst.t[:, hp, ts], in_=pb.t[:], func=AF.Silu)
                                else:
                                    s32, sq_, rn_ = sl32[n % 2], sqb[n % 2], rn[n % 2]
                                    pn = P[2 + n % 2]
                                    n += 1
                                    k.op("act", "activation", reads=[pb], writes=[s32], out=s32.t[:], in_=pb.t[:], func=AF.Silu)
                                    k.op("act", "activation", reads=[s32], writes=[sq_], out=sq_.t[:], in_=s32.t[:], func=AF.Square)
                                    k.op("pe", "matmul", reads=[sq_, cb], writes=[pn], out=pn.t[:], lhsT=blk64, rhs=sq_.t[:], start=True, stop=True)
                                    k.op("act", "activation", reads=[pn, epsb], writes=[rn_], out=rn_.t[:], in_=pn.t[:], func=AF.Sqrt, bias=epsb.t[:, 0:1])
                                    k.op("dve", "reciprocal", reads=[rn_], writes=[rn_], out=rn_.t[:], in_=rn_.t[:])
                                    k.op("dve", "scalar_tensor_tensor", reads=[s32, rn_], writes=[dst], out=dst.t[:, hp, ts], in0=s32.t[:],
                                         scalar=(0.125 if kind == 0 else 1.0), in1=rn_.t[:], op0=ALU.mult, op1=ALU.mult)

                print("ops at stage2:", k.nops)
                if GSTOP <= 2:
                    return
                S = sc.sb([128, 2, 128], F32, "S")
                k.op("dve", "memset", writes=[S], ap=S.t[:], constant=0.0)
                Gb = sc.sb([128, 4, 128], F32, "Gb")
                Gm = sc.sb([128, 4, 128], F32, "Gm")
                Gm2 = sc.sb([128, 4, 128], F32, "Gm2")
                Dc = sc.sb([128, 4, 128], F32, "Dc")
                DT = sc.sb([128, 4, 128], BF16, "DT")
                Eg = sc.sb([128, 4, 128], F32, "Eg")
                Nb = [sc.sb([128, 4, 128], F32, "Nb%d" % i) for i in range(2)]
                Mb = [sc.sb([128, 4, 128], F32, "Mb%d" % i) for i in range(2)]
                Y = sc.sb([128, 4, 128], F32, "Y")
                Ng = [[Nb[i].view("Nb%dg%d" % (i, gi)) for gi in range(2)] for i in range(2)]
                Mg = [[Mb[i].view("Mb%dg%d" % (i, gi)) for gi in range(2)] for i in range(2)]
                Yg = [Y.view("Yg%d" % gi) for gi in range(2)]
                QKT = sc.sb([128, 4, 128], BF16, "QKT")
                U = sc.sb([128, 256], F32, "U")
                WT = sc.sb([128, 2, 128], F32, "WT")
                qe = sc.sb([128, 2, 128], F32, "qe")
                kd = sc.sb([128, 256], BF16, "kd")
                rhu = sc.sb([128, 256], F32, "rhu")
                rhw = sc.sb([128, 256], F32, "rhw")
                vnew = sc.sb([128, 256], BF16, "vnew")
                ob = sc.sb([128, 4, 64], F32, "ob")
                osq = sc.sb([128, 4, 64], F32, "osq")
                ors = sc.sb([128, 8], F32, "ors")
                zs = sc.sb([128, 256], F32, "zs")
                ytok = sc.sb([128, 256], BF16, "ytok")
                kc2 = sc.sb([128, 4, 128], BF16, "kc2")
                kc2v = kc2.t[:].rearrange("p (a b) e -> p a b e", a=2)
                k.op("dve", "memset", writes=[kc2], ap=kc2.t[:], constant=0.0)
                gnw = prow.t[:, PR_GNW:PR_GNW + 64]
                for n in range(NT):
                    cs = slice(n * 128, (n + 1) * 128)
                    tgi = n // 4
                    PA, PB, PT_, PN, PM, PY, PK, PX = P
                    for hp in range(2):
                        k.op("pe", "matmul", reads=[kT, cb], writes=[PK], out=PK.t[:, hp * 128:(hp + 1) * 128], lhsT=kT.t[:, hp, cs], rhs=ident,
                             start=True, stop=True)
                        k.op("pe", "matmul", reads=[vT, cb], writes=[PK], out=PK.t[:, 256 + hp * 128:256 + (hp + 1) * 128], lhsT=vT.t[:, hp, cs],
                             rhs=ident, start=True, stop=True)
                    for h in range(4):
                        hs = slice(h * 64, (h + 1) * 64)
                        k.op("dve", "tensor_scalar", reads=[PK, bew], writes=[rhw], out=rhw.t[:, hs], in0=PK.t[:, hs],
                             scalar1=bew.t[:, n, h:h + 1], scalar2=None, op0=ALU.mult)
                        k.op("dve", "tensor_scalar", reads=[PK, ed], writes=[kd], out=kd.t[:, hs], in0=PK.t[:, hs],
                             scalar1=ed.t[:, n, h:h + 1], scalar2=None, op0=ALU.mult)
                        k.op("dve", "tensor_scalar", reads=[PK, bt], writes=[rhu], out=rhu.t[:, hs], in0=PK.t[:, 256 + h * 64:256 + (h + 1) * 64],
                             scalar1=bt.t[:, n, h:h + 1], scalar2=None, op0=ALU.mult)
                    if n == 0: print('ops at stage3:', k.nops)
                    if GSTOP <= 3:
                        return
                    for h in range(4):
                        k.op("dve", "tensor_scalar", reads=[c32, gt], writes=[Gb], out=Gb.t[:, h, :], in0=ones32,
                             scalar1=gt.t[:, n, h:h + 1], scalar2=-1.0, op0=ALU.mult, op1=ALU.mult)
                        k.op("pe", "matmul", reads=[Gb, c32], writes=[PA], out=PA.t[:, h * 128:(h + 1) * 128], lhsT=Gb.t[:, h, :], rhs=triT,
                             start=True, stop=True)
                    PA3 = PA.t[:].rearrange("p (h e) -> p h e", h=4)
                    for h in range(4):
                        k.op("dve", "scalar_tensor_tensor", reads=[PA, gc, c32], writes=[Gm], out=Gm.t[:, h, :], in0=PA3[:, h, :],
                             scalar=gc.t[:, n, h:h + 1], in1=maskC, op0=ALU.add, op1=ALU.add)
                        k.op("dve", "scalar_tensor_tensor", reads=[PA, gc, c32], writes=[Gm2], out=Gm2.t[:, h, :], in0=PA3[:, h, :],
                             scalar=gc.t[:, n, h:h + 1], in1=maskU, op0=ALU.add, op1=ALU.subtract)
                    k.op("act", "activation", reads=[Gm], writes=[Dc], out=Dc.t[:], in_=Gm.t[:], func=AF.Exp)
                    k.op("act", "activation", reads=[Gm2], writes=[DT], out=DT.t[:], in_=Gm2.t[:], func=AF.Exp, scale=-1.0)
                    k.op("act", "activation", reads=[PA], writes=[Eg], out=Eg.t[:], in_=PA3, func=AF.Exp, scale=-1.0)
                    if n == 0: print('ops at stage4:', k.nops)
                    if GSTOP <= 4:
                        return
                    N_, M_ = Nb[0], Mb[0]
                    PB3 = PB.t[:].rearrange("p (h e) -> p h e", h=4)
                    for h in range(4):
                        pr = slice((h % 2) * 64, (h % 2 + 1) * 64)
                        if h == 0:
                            k.op("act", "activation", reads=[kT], writes=[kc2], out=kc2v[0:64, :, 0, :], in_=kT.t[0:64, :, cs], func=AF.Copy)
                            k.op("act", "activation", reads=[kT], writes=[kc2], out=kc2v[64:128, :, 1, :], in_=kT.t[64:128, :, cs], func=AF.Copy)
                        k.op("pe", "matmul", reads=[kT, kc2], writes=[PB], out=PB.t[:, h * 128:(h + 1) * 128], lhsT=kT.t[:, h // 2, cs], rhs=kc2.t[:, h, :],
                             start=True, stop=True)
                    for h in range(4):
                        k.op("dve", "scalar_tensor_tensor", reads=[PB, nbt, Dc], writes=[N_, Ng[0][h // 2]], out=N_.t[:, h, :], in0=PB3[:, h, :],
                             scalar=nbt.t[:, n, h:h + 1], in1=Dc.t[:, h, :], op0=ALU.mult, op1=ALU.mult)
                    PT3 = PT_.t[:].rearrange("p (h e) -> p h e", h=4)
                    for h in range(4):
                        k.op("pe", "matmul", reads=[N_, c32], writes=[PT_], out=PT_.t[:, h * 128:(h + 1) * 128], lhsT=N_.t[:, h, :], rhs=ident32, start=True, stop=True)
                    k.op("act", "activation", reads=[PT_], writes=[M_, Mg[0][0], Mg[0][1]], out=M_.t[:], in_=PT3, func=AF.Copy)
                    for h in range(4):
                        k.op("dve", "tensor_tensor", reads=[PT_, c32], writes=[Y, Yg[h // 2]], out=Y.t[:, h, :], in0=PT3[:, h, :], in1=ident32, op=ALU.add)
                    if n == 0: print('ops at stage5:', k.nops)
                    if GSTOP <= 5:
                        return
                    PN3 = PN.t[:].rearrange("p (h e) -> p h e", h=4)
                    PM3 = PM.t[:].rearrange("p (h e) -> p h e", h=4)
                    PY3 = PY.t[:].rearrange("p (h e) -> p h e", h=4)
                    cur = 0
                    GB = ((PN, PM, PY), (PA, PB, PT_))
                    for lev in range(6):
                        Nc, Mc = Nb[cur], Mb[cur]
                        Nn, Mn = Nb[1 - cur], Mb[1 - cur]
                        for gi in range(2):
                            pn = GB[gi][0]
                            for h2 in range(2):
                                h = gi * 2 + h2
                                k.op("pe", "matmul", reads=[Mg[cur][gi], Ng[cur][gi]], writes=[pn], inc=(h2 == 1),
                                     out=pn.t[:, h2 * 128:(h2 + 1) * 128], lhsT=Mc.t[:, h, :], rhs=Nc.t[:, h, :], start=True, stop=True)
                        for gi in range(2):
                            pn = GB[gi][0]
                            k.op("act", "activation", reads=[pn], writes=[Ng[1 - cur][gi]], out=Nn.t[:, gi * 2:gi * 2 + 2, :],
                                 in_=pn.t[:, 0:256].rearrange("p (h e) -> p h e", h=2), func=AF.Copy)
                        if lev < 5:
                            for gi in range(2):
                                pm = GB[gi][1]
                                for h2 in range(2):
                                    h = gi * 2 + h2
                                    k.op("pe", "matmul", reads=[Mg[cur][gi], Ng[cur][gi]], writes=[pm], inc=(h2 == 1),
                                         out=pm.t[:, h2 * 128:(h2 + 1) * 128], lhsT=Nc.t[:, h, :], rhs=Mc.t[:, h, :], start=True, stop=True)
                            for gi in range(2):
                                pm = GB[gi][1]
                                k.op("dve", "tensor_copy", reads=[pm], writes=[Mg[1 - cur][gi]], out=Mn.t[:, gi * 2:gi * 2 + 2, :],
                                     in_=pm.t[:, 0:256].rearrange("p (h e) -> p h e", h=2))
                        for gi in range(2):
                            py = GB[gi][2]
                            for h2 in range(2):
                                h = gi * 2 + h2
                                k.op("pe", "matmul", reads=[Ng[1 - cur][gi], Yg[gi]], writes=[py], inc=(h2 == 1),
                                     out=py.t[:, h2 * 128:(h2 + 1) * 128], lhsT=Nn.t[:, h, :], rhs=Y.t[:, h, :], start=True, stop=True)
                        for gi in range(2):
                            py = GB[gi][2]
                            k.op("dve", "tensor_tensor", reads=[py, Yg[gi]], writes=[Yg[gi]], out=Y.t[:, gi * 2:gi * 2 + 2, :],
                                 in0=Y.t[:, gi * 2:gi * 2 + 2, :], in1=py.t[:, 0:256].rearrange("p (h e) -> p h e", h=2), op=ALU.add)
                        cur = 1 - cur
                    k.op("dve", "tensor_copy", reads=[Yg[0], Yg[1]], writes=[Y], out=Y.t[:, 0, 0:1], in_=Y.t[:, 0, 0:1])
                    for h in range(4):
                        hs = slice(h * 64, (h + 1) * 64)
                        k.op("pe", "matmul", reads=[Y, rhu], writes=[PX], out=PX.t[:, hs], lhsT=Y.t[:, h, :], rhs=rhu.t[:, hs], start=True, stop=True)
                    k.op("act", "activation", reads=[PX], writes=[U], out=U.t[:], in_=PX.t[:, 0:256], func=AF.Copy)
                    for h in range(4):
                        hp = h // 2
                        k.op("pe", "matmul", reads=[Y, rhw], writes=[PB], out=PB.t[:, h * 128:(h + 1) * 128], lhsT=rhw.t[:, hp * 128:(hp + 1) * 128], rhs=Y.t[:, h, :],
                             start=True, stop=True)
                    for h in range(4):
                        pr = slice((h % 2) * 64, (h % 2 + 1) * 64)
                        if h % 2 == 0:
                            k.op("act", "activation", reads=[PB], writes=[WT], out=WT.t[pr, h // 2, :], in_=PB3[pr, h, :], func=AF.Copy)
                        else:
                            k.op("dve", "tensor_copy", reads=[PB], writes=[WT], out=WT.t[pr, h // 2, :], in_=PB3[pr, h, :])
                    if n == 0: print('ops at stage7:', k.nops)
                    if GSTOP <= 7:
                        return
                    for h in range(4):
                        pr = slice((h % 2) * 64, (h % 2 + 1) * 64)
                        k.op("pe", "matmul", reads=[kc2, qT], writes=[PT_], out=PT_.t[:, h * 128:(h + 1) * 128], lhsT=kc2.t[:, h, :], rhs=qT.t[:, h // 2, cs],
                             start=True, stop=True)
                    k.op("dve", "tensor_tensor", reads=[PT_, DT], writes=[QKT], out=QKT.t[:], in0=PT3, in1=DT.t[:], op=ALU.mult)
                    for h in range(4):
                        pr = slice((h % 2) * 64, (h % 2 + 1) * 64)
                        k.op("dve", "tensor_tensor", reads=[qT, Eg], writes=[qe], out=qe.t[pr, h // 2, :], in0=qT.t[pr, h // 2, cs],
                             in1=Eg.t[pr, h, :], op=ALU.mult)
                    if n == 0: print('ops at stage8:', k.nops)
                    if GSTOP <= 8:
                        return
                    for hp in range(2):
                        k.op("pe", "matmul", reads=[WT, S], writes=[PX], out=PX.t[:, 256 + hp * 128:256 + (hp + 1) * 128], lhsT=WT.t[:, hp, :],
                             rhs=S.t[:, hp, :], start=True, stop=True)
                    k.op("dve", "tensor_tensor", reads=[U, PX], writes=[vnew], out=vnew.t[:], in0=U.t[:], in1=PX.t[:, 256:512], op=ALU.subtract)
                    for h in range(4):
                        hs = slice(h * 64, (h + 1) * 64)
                        k.op("pe", "matmul", reads=[qe, S], writes=[PN], out=PN.t[:, hs], lhsT=qe.t[:, h // 2, :],
                             rhs=S.t[:, h // 2, (h % 2) * 64:(h % 2 + 1) * 64], start=True, stop=False)
                        k.op("pe", "matmul", reads=[QKT, vnew], writes=[PN], out=PN.t[:, hs], lhsT=QKT.t[:, h, :], rhs=vnew.t[:, hs],
                             start=False, stop=True)
                    for hp in range(2):
                        k.op("pe", "matmul", reads=[kd, vnew], writes=[PM], out=PM.t[:, hp * 128:(hp + 1) * 128], lhsT=kd.t[:, hp * 128:(hp + 1) * 128],
                             rhs=vnew.t[:, hp * 128:(hp + 1) * 128], start=True, stop=True)
                    for h in range(4):
                        pr = slice((h % 2) * 64, (h % 2 + 1) * 64)
                        cs2 = slice((h % 2) * 64, (h % 2 + 1) * 64)
                        k.op("dve", "scalar_tensor_tensor", reads=[S, Eg, PM], writes=[S], out=S.t[pr, h // 2, cs2], in0=S.t[pr, h // 2, cs2],
                             scalar=Eg.t[pr, h, 127:128], in1=PM.t[pr, (h // 2) * 128 + (h % 2) * 64:(h // 2) * 128 + (h % 2 + 1) * 64],
                             op0=ALU.mult, op1=ALU.add)
                    if n == 0: print('ops at stage9:', k.nops)
                    if GSTOP <= 9:
                        return
                    k.op("act", "activation", reads=[PN], writes=[ob], out=ob.t[:].rearrange("p h e -> p (h e)"), in_=PN.t[:, 0:256], func=AF.Copy)
                    k.op("act", "activation", reads=[ob], writes=[osq], out=osq.t[:], in_=ob.t[:], func=AF.Square)
                    k.op("dve", "tensor_reduce", reads=[osq], writes=[ors], out=ors.t[:, 0:4], in_=osq.t[:], axis=AX.X, op=ALU.add)
                    k.op("act", "activation", reads=[ors, epsb], writes=[ors], out=ors.t[:, 4:8], in_=ors.t[:, 0:4], func=AF.Sqrt, scale=1.0 / 64,
                         bias=epsb.t[:, 0:1])
                    k.op("dve", "reciprocal", reads=[ors], writes=[ors], out=ors.t[:, 4:8], in_=ors.t[:, 4:8])
                    for c in range(8):
                        k.op("pe", "matmul", reads=[s_z, hT[tgi]], writes=[PY], inc=(c == 7), out=PY.t[:, 0:256], lhsT=hT_t.t[:, c, cs], rhs=wz[:, c, :],
                             start=(c == 0), stop=(c == 7))
                    k.op("act", "activation", reads=[PY], writes=[zs], out=zs.t[:], in_=PY.t[:, 0:256], func=AF.Silu)
                    for h in range(4):
                        k.op("dve", "scalar_tensor_tensor", reads=[ob, ors, prow], writes=[osq], out=osq.t[:, h, :], in0=ob.t[:, h, :],
                             scalar=ors.t[:, 4 + h:5 + h], in1=gnw, op0=ALU.mult, op1=ALU.mult)
                    k.op("dve", "tensor_tensor", reads=[osq, zs], writes=[ytok], out=ytok.t[:], in0=osq.t[:].rearrange("p h e -> p (h e)"),
                         in1=zs.t[:], op=ALU.mult)
                    for hp in range(2):
                        k.op("pe", "matmul", reads=[ytok, cb], writes=[PK], out=PK.t[:, hp * 128:(hp + 1) * 128], lhsT=ytok.t[:, hp * 128:(hp + 1) * 128],
                             rhs=ident, start=True, stop=True)
                    k.op("act", "activation", reads=[PK], writes=[yg], out=yg.t[:, :, cs], in_=PK.t[:, 0:256].rearrange("p (a e) -> p a e", a=2),
                         func=AF.Copy)
                dbg_dump(yg, yg.t, 2, 2, sc)
                out_proj(l, yg, yg.t, 2, 2)

        for s in range(nseq):
            for g in range(4):
                k.load("sp", xT[g], xT_t.t[:, :, g * 512:(g + 1) * 512],
                       xT_d[s, :, g * 512:(g + 1) * 512].rearrange("(c p) t -> p c t", p=128))
            for l in range(depth):
                lam_init = 0.8 - 0.6 * math.exp(-0.3 * l)
                k.load("sp", pcol, pcol.t[:], pcol_d[l])
                k.load("sp", prow, prow.t[:], prow_d[l])
                phase_norm(pcol.t[:, PC_G1:PC_G1 + 8])
                if "conv" in mixers:
                    phase_conv(l)
                if "gdn" in mixers:
                    phase_gdn(l)
                if "attn" in mixers:
                    phase_attn(l, lam_init)
                if do_ffn:
                    phase_norm(pcol.t[:, PC_G2:PC_G2 + 8])
                    phase_ffn(l)
            k.load("sp", pcol, pcol.t[:], pcol_d[DEPTH])
            phase_norm(pcol.t[:, PC_G1:PC_G1 + 8], final=True, s=s)
        print("bass instructions:", k.ninst, "dma sems:", k.ndsem)
    return nc


def make_consts():
    i = np.arange(128)
    c32 = np.zeros((128, 4 * 128 + 4 * 128 + 4 * NR), np.float32)
    c32[:, 0:128] = np.where(i[:, None] > i[None, :], 0.0, NEG)
    c32[:, 128:256] = np.where(i[None, :] >= i[:, None], 0.0, NEG)
    c32[:, 256:384] = (i[:, None] <= i[None, :]).astype(np.float32)
    c32[:, 384:512] = 1.0
    c32[:, 512:640] = np.eye(128)
    for h in range(4):
        for r in range(-3, 16):
            c32[:, 1024 + h * NR + (r + 3)] = SLOPES[h] * (i - 128.0 * r)
    cb = np.zeros((128, 5 * 128), np.float32)
    cb[:, 0:128] = np.eye(128)
    cb[:, 128:256] = 1.0
    cb[0:64, 256:320] = 1.0
    cb[64:128, 320:384] = 1.0
    cb[:, 384:512] = (i[:, None] > i[None, :]).astype(np.float32)
    cb[:, 512:640] = (i[None, :] >= i[:, None]).astype(np.float32)
    return c32, cb


def make_params(inp):
    f = np.float32
    pcol = np.zeros((DEPTH + 1, 128, NPC), f)
    prow = np.zeros((DEPTH, 128, NPR), f)
    for l in range(DEPTH):
        pcol[l, :, PC_G1:PC_G1 + 8] = inp["norm1_g"][l].reshape(8, 128).T
        pcol[l, :, PC_G2:PC_G2 + 8] = inp["norm2_g"][l].reshape(8, 128).T
        pcol[l, :, PC_CB:PC_CB + 2] = inp["conv_dw_b"][l].reshape(2, 128).T
        pcol[l, :, PC_LG:PC_LG + 2] = inp["conv_ln_g"][l].reshape(2, 128).T
        pcol[l, :, PC_LB:PC_LB + 2] = inp["conv_ln_b"][l].reshape(2, 128).T
        pcol[l, :, PC_DW:PC_DW + 62] = inp["conv_dw_w"][l].reshape(31, 2, 128).transpose(2, 1, 0).reshape(128, 62)
        pcol[l, :, PC_GW:PC_GW + 24] = inp["gdn_conv_w"][l].reshape(4, 6, 128).transpose(2, 1, 0).reshape(128, 24)
        prow[l, :, PR_GNW:PR_GNW + 64] = inp["gdn_norm_w"][l][None, :]
        prow[l, :, PR_SUB:PR_SUB + 128] = inp["diff_subln_w"][l][None, :]
        prow[l, :, PR_LAM:PR_LAM + 256] = inp["diff_lambda"][l].reshape(1, 256)
        prow[l, :, PR_ALOG:PR_ALOG + 4] = inp["gdn_a_log"][l][None, :]
        prow[l, :, PR_DTB:PR_DTB + 4] = inp["gdn_dt_bias"][l][None, :]
    pcol[DEPTH, :, PC_G1:PC_G1 + 8] = inp["final_norm_g"].reshape(8, 128).T
    return pcol, prow


def kernel(**inp):
    ncores = 8
    x = np.asarray(inp["x"], np.float32)
    B = x.shape[0]
    nseq = B // ncores
    c32, cb = make_consts()
    pcol, prow = make_params({k_: np.asarray(v, np.float32) for k_, v in inp.items()})
    xT = np.ascontiguousarray(x.transpose(0, 2, 1))
    nc = build_program(nseq=nseq)
    shared = {
        "w_in": np.ascontiguousarray(inp["w_in"], dtype=np.float32),
        "w_out": np.ascontiguousarray(inp["w_out"], dtype=np.float32),
        "w_ffn_in": np.ascontiguousarray(inp["w_ffn_in"], dtype=np.float32),
        "w_ffn_out": np.ascontiguousarray(inp["w_ffn_out"], dtype=np.float32),
        "pcol": pcol, "prow": prow, "c32": c32, "cb": cb,
    }
    in_maps = [dict(shared, xT=xT[c * nseq:(c + 1) * nseq]) for c in range(ncores)]
    res = run_bass_kernel_spmd(nc, in_maps, core_ids=list(range(ncores)))
    outT = np.concatenate([r["outT"] for r in res.results], axis=0)
    return np.ascontiguousarray(outT.transpose(0, 2, 1)).astype(np.float32)
```

```python
import math
import os
GSTOP = int(os.environ.get('GSTOP', '99'))
GCUT = int(os.environ.get('GCUT', '0'))
from contextlib import ExitStack, contextmanager

import numpy as np
import concourse.bass as bass
import concourse.mybir as mybir
from concourse.bass_utils import run_bass_kernel_spmd

F32 = mybir.dt.float32
BF16 = mybir.dt.bfloat16
AF = mybir.ActivationFunctionType
ALU = mybir.AluOpType
AX = mybir.AxisListType

D = 1024
T = 2048
NT = 16
DEPTH = 4
DFF = 2816
NFC = 22
INW = 3080
RMS_EPS = 1e-6
LN_EPS = 1e-5
NEG = -30000.0
SLOPES = [(2.0 ** (-8.0 / 4)) ** (i + 1) for i in range(4)]
NR = 19

PC_G1, PC_G2, PC_CB, PC_LG, PC_LB, PC_DW, PC_GW = 0, 8, 16, 18, 20, 22, 84
NPC = 84 + 24
PR_GNW, PR_SUB, PR_LAM, PR_ALOG, PR_DTB = 0, 64, 192, 448, 452
NPR = 456


class Buf:
    __slots__ = ("t", "lw", "rs", "dsem", "dkey", "dcnt", "name", "psum")

    def __init__(self, t, name, psum=False):
        self.t = t
        self.name = name
        self.psum = psum
        self.lw = None
        self.rs = {}
        self.dsem = None
        self.dkey = None
        self.dcnt = 0

    def view(self, name=None):
        return Buf(self.t, name or self.name)


class K:
    def __init__(self, nc, es):
        self.nc = nc
        self.es = es
        self.engs = {"pe": nc.tensor, "act": nc.scalar, "dve": nc.vector, "pool": nc.gpsimd, "sp": nc.sync}
        self.semh = {}
        self.cnt = {}
        self.pend = {}
        self.seen = {}
        for e in self.engs:
            self.semh[e] = es.enter_context(nc.semaphore("s_" + e))
            self.cnt[e] = 0
            self.pend[e] = False
            self.seen[e] = {}
        self.ndsem = 0
        self.nalloc = 0
        self.ninst = 0

    def sb(self, shape, dt, name=None, es=None):
        self.nalloc += 1
        name = "sb%d_%s" % (self.nalloc, name or "t")
        t = (es or self.es).enter_context(self.nc.sbuf_tensor(name, list(shape), dt))
        return Buf(t, name)

    def ps(self, name):
        t = self.es.enter_context(self.nc.psum_tensor(name, [128, 512], F32))
        return Buf(t, name, psum=True)

    @contextmanager
    def scope(self):
        es = ExitStack()
        k = self

        class S:
            def sb(self, shape, dt, name=None):
                return k.sb(shape, dt, name, es=es)

        try:
            yield S()
            self.barrier()
        finally:
            es.close()

    def _waits(self, e, deps, skipkey=None):
        eng = self.engs[e]
        for key, v in deps.items():
            if key == skipkey:
                continue
            if key == e and e in ("pe", "sp", "pool"):
                continue
            if self.seen[e].get(key, 0) >= v:
                continue
            eng.wait_ge(self.semh[key], v)
            self.seen[e][key] = v
            self.ninst += 1

    @staticmethod
    def _deps(reads, writes, e=None):
        deps = {}
        for b in reads:
            if b.lw is not None:
                deps[b.lw[0]] = max(deps.get(b.lw[0], 0), b.lw[1])
            if b.psum:
                for key, v in b.rs.items():
                    if key != e:
                        deps[key] = max(deps.get(key, 0), v)
        for b in writes:
            if b.lw is not None:
                deps[b.lw[0]] = max(deps.get(b.lw[0], 0), b.lw[1])
            for key, v in b.rs.items():
                deps[key] = max(deps.get(key, 0), v)
        return deps

    def op(self, e, name, reads=(), writes=(), inc=True, **kw):
        self.nops = getattr(self, "nops", 0) + 1
        if GCUT and self.nops > GCUT:
            return None
        deps = self._deps(reads, writes, e)
        self._waits(e, deps)
        ins = getattr(self.engs[e], name)(**kw)
        self.ninst += 1
        idx = self.cnt[e] + 1
        if inc:
            ins.then_inc(self.semh[e], 1)
            self.cnt[e] = idx
            self.pend[e] = False
        else:
            self.pend[e] = True
        for b in reads:
            b.rs[e] = idx
        for b in writes:
            b.lw = (e, idx)
            b.rs = {}
        return ins

    def _dsem(self, b):
        if b.dsem is None:
            self.ndsem += 1
            b.dkey = "d%d_%s" % (self.ndsem, b.name)
            b.dsem = self.es.enter_context(self.nc.semaphore(b.dkey))
            self.semh[b.dkey] = b.dsem
        return b.dsem

    def load(self, q, buf, out, in_):
        sem = self._dsem(buf)
        deps = self._deps((), (buf,))
        self._waits(q, deps, skipkey=buf.dkey)
        self.engs[q].dma_start(out=out, in_=in_).then_inc(sem, 16)
        self.ninst += 1
        buf.dcnt += 16
        buf.lw = (buf.dkey, buf.dcnt)
        buf.rs = {}

    def store(self, q, buf, out, in_):
        sem = self._dsem(buf)
        deps = self._deps((buf,), ())
        self._waits(q, deps, skipkey=None)
        self.engs[q].dma_start(out=out, in_=in_).then_inc(sem, 16)
        self.ninst += 1
        buf.dcnt += 16
        buf.rs[buf.dkey] = buf.dcnt

    def barrier(self):
        assert GCUT or not any(self.pend.values()), self.pend
        for e in ("pe", "act", "dve"):
            deps = {d: self.cnt[d] for d in ("pe", "act", "dve") if d != e and self.cnt[d] > 0}
            self._waits(e, deps)

    def wait_all_stores(self, e, bufs):
        deps = {}
        for b in bufs:
            if b.dkey is not None:
                deps[b.dkey] = b.dcnt
        self._waits(e, deps)


def build_program(nseq=4, depth=DEPTH, dbg=False, mixers=("conv", "gdn", "attn"), do_ffn=True):
    nc = bass.Bass("TRN2", target_bir_lowering=False)
    dr = {}

    def din(name, shape, dt=F32):
        dr[name] = nc.dram_tensor(name, list(shape), dt, kind="ExternalInput").ap()
        return dr[name]

    xT_d = din("xT", [nseq, D, T])
    w_in_d = din("w_in", [DEPTH, D, INW])
    w_out_d = din("w_out", [DEPTH, D, D])
    w_f1_d = din("w_ffn_in", [DEPTH, D, 2 * DFF])
    w_f2_d = din("w_ffn_out", [DEPTH, DFF, D])
    pcol_d = din("pcol", [DEPTH + 1, 128, NPC])
    prow_d = din("prow", [DEPTH, 128, NPR])
    c32_d = din("c32", [128, 4 * 128 + 4 * 128 + 4 * NR])
    cb_d = din("cb", [128, 5 * 128])
    out_d = nc.dram_tensor("outT", [nseq, D, T], F32, kind="ExternalOutput").ap()
    if dbg:
        dbg_d = nc.dram_tensor("dbg", [8, 128, T], F32, kind="ExternalOutput").ap()

    with ExitStack() as es:
        k = K(nc, es)
        xT_t = k.sb([128, 8, T], F32, "xT")
        hT_t = k.sb([128, 8, T], BF16, "hT")
        xT = [xT_t.view("xT%d" % g) for g in range(4)]
        hT = [hT_t.view("hT%d" % g) for g in range(4)]
        NS = 6
        slots = [k.sb([128, 2048], BF16, "slot%d" % i) for i in range(NS)]
        c32 = k.sb([128, 4 * 128 + 4 * 128 + 4 * NR], F32, "c32")
        cb = k.sb([128, 5 * 128], BF16, "cb")
        pcol = k.sb([128, NPC], F32, "pcol")
        prow = k.sb([128, NPR], F32, "prow")
        wab = k.sb([128, 8, 8], BF16, "wab")
        P = [k.ps("P%d" % i) for i in range(8)]
        epsb = k.sb([128, 4], F32, "epsb")
        k.op("dve", "memset", writes=[epsb], ap=epsb.t[:, 0:1], constant=RMS_EPS)
        k.op("dve", "memset", writes=[epsb], ap=epsb.t[:, 1:2], constant=LN_EPS)
        k.op("dve", "memset", writes=[epsb], ap=epsb.t[:, 2:3], constant=1.0)
        k.op("dve", "memset", writes=[epsb], ap=epsb.t[:, 3:4], constant=0.0)

        k.load("sp", c32, c32.t[:], c32_d)
        k.load("pool", cb, cb.t[:], cb_d)
        maskC = c32.t[:, 0:128]
        maskU = c32.t[:, 128:256]
        triT = c32.t[:, 256:384]
        ones32 = c32.t[:, 384:512]
        ident32 = c32.t[:, 512:640]
        abias = lambda h, r: c32.t[:, 1024 + h * NR + (r + 3):1024 + h * NR + (r + 3) + 1]
        ident = cb.t[:, 0:128]
        onesb = cb.t[:, 128:256]
        blk64 = cb.t[:, 256:384]
        strict = cb.t[:, 384:512]
        triU = cb.t[:, 512:640]

        slot_i = [0]

        def next_slot():
            s = slots[slot_i[0] % NS]
            slot_i[0] += 1
            return s

        def wload(slot, dst, src):
            k.load("pool", slot, dst, src)

        def w_in_cols(l, c0, n):
            return w_in_d[l, :, c0:c0 + n].rearrange("(c p) e -> p c e", p=128)

        def proj_fm(pb, wslot, wv, m0, m, tg, act=hT, kc=8, act_t=None):
            at = act_t if act_t is not None else hT_t.t
            for c in range(kc):
                k.op("pe", "matmul", reads=[wslot, act[tg]], writes=[pb], inc=(c == kc - 1),
                     out=pb.t[0:m, :], lhsT=wv[:, c, m0:m0 + m], rhs=at[:, c, tg * 512:(tg + 1) * 512],
                     start=(c == 0), stop=(c == kc - 1))

        def phase_norm(gcol, final=False, s=0):
            with k.scope() as sc:
                sq = [sc.sb([128, 8, 512], BF16) for _ in range(2)]
                rs = [sc.sb([128, 512], F32) for _ in range(2)]
                ob = [sc.sb([128, 8, 512], F32) for _ in range(2)] if final else None
                for tg in range(4):
                    ts = slice(tg * 512, (tg + 1) * 512)
                    s_, r_, pb = sq[tg % 2], rs[tg % 2], P[tg % 2]
                    k.op("act", "activation", reads=[xT[tg]], writes=[s_],
                         out=s_.t[:], in_=xT_t.t[:, :, ts], func=AF.Square)
                    for c in range(8):
                        k.op("pe", "matmul", reads=[s_, cb], writes=[pb], inc=(c == 7),
                             out=pb.t[:], lhsT=onesb, rhs=s_.t[:, c, :], start=(c == 0), stop=(c == 7))
                    k.op("act", "activation", reads=[pb, epsb], writes=[r_],
                         out=r_.t[:], in_=pb.t[:], func=AF.Sqrt, scale=1.0 / D, bias=epsb.t[:, 0:1])
                    k.op("dve", "reciprocal", reads=[r_], writes=[r_], out=r_.t[:], in_=r_.t[:])
                    if not final:
                        for c in range(8):
                            k.op("dve", "scalar_tensor_tensor", reads=[xT[tg], r_, pcol], writes=[hT[tg]],
                                 out=hT_t.t[:, c, ts], in0=xT_t.t[:, c, ts], scalar=gcol[:, c:c + 1],
                                 in1=r_.t[:], op0=ALU.mult, op1=ALU.mult)
                    else:
                        o_ = ob[tg % 2]
                        for c in range(8):
                            k.op("dve", "scalar_tensor_tensor", reads=[xT[tg], r_, pcol], writes=[o_],
                                 out=o_.t[:, c, :], in0=xT_t.t[:, c, ts], scalar=gcol[:, c:c + 1],
                                 in1=r_.t[:], op0=ALU.mult, op1=ALU.mult)
                        k.store("sp", o_, out_d[s, :, ts].rearrange("(c p) t -> p c t", p=128), o_.t[:])
                if final:
                    k.wait_all_stores("sp", ob)
                    k.wait_all_stores("act", ob)
                    k.wait_all_stores("dve", ob)

        def out_proj(l, yb, yt, r0, kc):
            sl = next_slot()
            wv = sl.t[:, 0:kc * 1024].rearrange("p (c e) -> p c e", c=kc)
            wload(sl, wv, w_out_d[l, r0 * 128:(r0 + kc) * 128, :].rearrange("(c p) e -> p c e", p=128))
            i = 0
            for dc in range(8):
                for tg in range(4):
                    pb = P[i % 2]
                    i += 1
                    ts = slice(tg * 512, (tg + 1) * 512)
                    for c in range(kc):
                        k.op("pe", "matmul", reads=[sl, yb], writes=[pb], inc=(c == kc - 1),
                             out=pb.t[:], lhsT=wv[:, c, dc * 128:(dc + 1) * 128], rhs=yt[:, c, ts],
                             start=(c == 0), stop=(c == kc - 1))
                    k.op("dve", "tensor_tensor", reads=[pb, xT[tg]], writes=[xT[tg]],
                         out=xT_t.t[:, dc, ts], in0=xT_t.t[:, dc, ts], in1=pb.t[:], op=ALU.add)

        def dbg_dump(yb, yt, c0, kc, sc):
            if not dbg:
                return
            tmp = sc.sb([128, 512], F32)
            for c in range(kc):
                for tg in range(4):
                    k.op("act", "activation", reads=[yb], writes=[tmp], out=tmp.t[:], in_=yt[:, c, tg * 512:(tg + 1) * 512], func=AF.Copy)
                    k.store("sp", tmp, dbg_d[c0 + c, :, tg * 512:(tg + 1) * 512], tmp.t[:])
            k.wait_all_stores("act", [tmp])

        def phase_conv(l):
            with k.scope() as sc:
                sv, sg = next_slot(), next_slot()
                wvv = sv.t[:].rearrange("p (c e) -> p c e", c=8)
                wgv = sg.t[:].rearrange("p (c e) -> p c e", c=8)
                wload(sv, wvv, w_in_cols(l, 0, 256))
                wload(sg, wgv, w_in_cols(l, 256, 256))
                glu = sc.sb([128, 2, 32 + T], BF16, "glu")
                yc = sc.sb([128, 2, T], BF16, "yconv")
                sig = [sc.sb([128, 512], F32) for _ in range(2)]
                k.op("dve", "memset", writes=[glu], ap=glu.t[:, :, 0:32], constant=0.0)
                n = 0
                for tg in range(4):
                    for j in range(2):
                        pv, pg = P[(n * 2) % 4], P[(n * 2 + 1) % 4]
                        sg_ = sig[n % 2]
                        n += 1
                        proj_fm(pg, sg, wgv, j * 128, 128, tg)
                        proj_fm(pv, sv, wvv, j * 128, 128, tg)
                        k.op("act", "activation", reads=[pg], writes=[sg_], out=sg_.t[:], in_=pg.t[:], func=AF.Sigmoid)
                        k.op("dve", "tensor_tensor", reads=[pv, sg_], writes=[glu],
                             out=glu.t[:, j, 32 + tg * 512:32 + (tg + 1) * 512], in0=pv.t[:], in1=sg_.t[:], op=ALU.mult)
                dg = [sc.sb([128, 128], BF16) for _ in range(4)]
                cv = [sc.sb([128, 2, 512], F32, "cv%d" % i) for i in range(4)]
                cq = [sc.sb([128, 2, 512], F32, "cq%d" % i) for i in range(4)]
                n = 0
                for j in range(2):
                    for tap in range(31):
                        d_ = dg[n % 4]
                        n += 1
                        k.op("dve", "tensor_scalar", reads=[cb, pcol], writes=[d_],
                             out=d_.t[:], in0=ident, scalar1=pcol.t[:, PC_DW + j * 31 + tap:PC_DW + j * 31 + tap + 1],
                             scalar2=None, op0=ALU.mult)
                        for tg in range(4):
                            pb = P[4 + tg]
                            k.op("pe", "matmul", reads=[d_, glu], writes=[pb], inc=(tap == 30 or tg == 3),
                                 out=pb.t[:], lhsT=d_.t[:], rhs=glu.t[:, j, 2 + tap + tg * 512:2 + tap + (tg + 1) * 512],
                                 start=(tap == 0), stop=(tap == 30))
                    for tg in range(4):
                        pb = P[4 + tg]
                        k.op("act", "activation", reads=[pb, pcol], writes=[cv[tg]],
                             out=cv[tg].t[:, j, :], in_=pb.t[:], func=AF.Identity, bias=pcol.t[:, PC_CB + j:PC_CB + j + 1])
                        k.op("act", "activation", reads=[cv[tg]], writes=[cq[tg]],
                             out=cq[tg].t[:, j, :], in_=cv[tg].t[:, j, :], func=AF.Square)
                tmp = [sc.sb([128, 512], F32) for _ in range(4)]
                for tg in range(4):
                    pm, pq = P[(tg * 2) % 4], P[(tg * 2 + 1) % 4]
                    for j in range(2):
                        k.op("pe", "matmul", reads=[cv[tg], c32], writes=[pm], inc=(j == 1),
                             out=pm.t[:], lhsT=ones32, rhs=cv[tg].t[:, j, :], start=(j == 0), stop=(j == 1))
                    for j in range(2):
                        k.op("pe", "matmul", reads=[cq[tg], c32], writes=[pq], inc=(j == 1),
                             out=pq.t[:], lhsT=ones32, rhs=cq[tg].t[:, j, :], start=(j == 0), stop=(j == 1))
                    mean, var = tmp[0], tmp[1]
                    k.op("act", "activation", reads=[pm], writes=[mean], out=mean.t[:], in_=pm.t[:], func=AF.Copy, scale=1.0 / 256)
                    k.op("act", "activation", reads=[mean], writes=[var], out=var.t[:], in_=mean.t[:], func=AF.Square)
                    k.op("dve", "scalar_tensor_tensor", reads=[pq, var], writes=[var],
                         out=var.t[:], in0=pq.t[:], scalar=1.0 / 256, in1=var.t[:], op0=ALU.mult, op1=ALU.subtract)
                    k.op("act", "activation", reads=[var, epsb], writes=[var],
                         out=var.t[:], in_=var.t[:], func=AF.Sqrt, bias=epsb.t[:, 1:2])
                    k.op("dve", "reciprocal", reads=[var], writes=[var], out=var.t[:], in_=var.t[:])
                    for j in range(2):
                        t_ = tmp[2 + j]
                        k.op("dve", "tensor_tensor", reads=[cv[tg], mean], writes=[t_],
                             out=t_.t[:], in0=cv[tg].t[:, j, :], in1=mean.t[:], op=ALU.subtract)
                        k.op("dve", "tensor_tensor", reads=[t_, var], writes=[t_],
                             out=t_.t[:], in0=t_.t[:], in1=var.t[:], op=ALU.mult)
                        k.op("act", "activation", reads=[t_, pcol], writes=[yc],
                             out=yc.t[:, j, tg * 512:(tg + 1) * 512], in_=t_.t[:], func=AF.Silu,
                             scale=pcol.t[:, PC_LG + j:PC_LG + j + 1], bias=pcol.t[:, PC_LB + j:PC_LB + j + 1])
                dbg_dump(yc, yc.t, 0, 2, sc)
                out_proj(l, yc, yc.t, 0, 2)

        def phase_attn(l, lam_init):
            scale = 64 ** -0.5
            with k.scope() as sc:
                lt = sc.sb([128, 2, 64], F32)
                ls = sc.sb([128, 2], F32)
                nlam = sc.sb([128, 1], F32)
                k.op("dve", "tensor_tensor", reads=[prow], writes=[lt], out=lt.t[:, 0, :],
                     in0=prow.t[:, PR_LAM:PR_LAM + 64], in1=prow.t[:, PR_LAM + 64:PR_LAM + 128], op=ALU.mult)
                k.op("dve", "tensor_tensor", reads=[prow], writes=[lt], out=lt.t[:, 1, :],
                     in0=prow.t[:, PR_LAM + 128:PR_LAM + 192], in1=prow.t[:, PR_LAM + 192:PR_LAM + 256], op=ALU.mult)
                k.op("dve", "tensor_reduce", reads=[lt], writes=[ls], out=ls.t[:], in_=lt.t[:], axis=AX.X, op=ALU.add)
                k.op("act", "activation", reads=[ls], writes=[ls], out=ls.t[:], in_=ls.t[:], func=AF.Exp)
                k.op("dve", "scalar_tensor_tensor", reads=[ls], writes=[nlam], out=nlam.t[:], in0=ls.t[:, 1:2],
                     scalar=-lam_init, in1=ls.t[:, 0:1], op0=ALU.add, op1=ALU.subtract)
                wrow = sc.sb([128, 128], F32)
                k.op("dve", "tensor_scalar", reads=[prow], writes=[wrow], out=wrow.t[:], in0=prow.t[:, PR_SUB:PR_SUB + 128],
                     scalar1=1.0 - lam_init, scalar2=None, op0=ALU.mult)

                qT = sc.sb([128, 2, T], BF16, "qT")
                kT = sc.sb([128, 2, T], BF16, "kT")
                V1 = sc.sb([128, NT, 2, 132], BF16, "V1")
                yd = sc.sb([128, 2, T], BF16, "ydiff")
                pt = [sc.sb([128, 512], BF16, "pt%d" % i) for i in range(3)]
                oh = [sc.sb([128, 4, 132], F32, "oh%d" % i) for i in range(2)]
                rc = sc.sb([128, 8], F32)
                ssq = sc.sb([128, 8], F32)
                ob4 = sc.sb([128, 4, 128], F32, "ob4")
                oc4 = sc.sb([128, 4, 128], F32, "oc4")
                yb4 = sc.sb([128, 4, 128], BF16, "yb4")
                for hp in range(2):
                    sq_, sk_, sv_ = next_slot(), next_slot(), next_slot()
                    views = []
                    for s_, c0 in ((sq_, 1544), (sk_, 2056), (sv_, 2568)):
                        v_ = s_.t[:].rearrange("p (c e) -> p c e", c=8)
                        wload(s_, v_, w_in_cols(l, c0 + hp * 256, 256))
                        views.append(v_)
                    wq, wk, wv = views
                    k.op("dve", "memset", writes=[V1], ap=V1.t[:, :, :, 128:129], constant=1.0)
                    n = 0
                    for hh in range(2):
                        for tg in range(4):
                            pb = P[n % 2]
                            n += 1
                            proj_fm(pb, sq_, wq, hh * 128, 128, tg)
                            k.op("act", "activation", reads=[pb], writes=[qT], out=qT.t[:, hh, tg * 512:(tg + 1) * 512],
                                 in_=pb.t[:], func=AF.Copy, scale=scale)
                            pb = P[n % 2]
                            n += 1
                            proj_fm(pb, sk_, wk, hh * 128, 128, tg)
                            k.op("dve", "tensor_copy", reads=[pb], writes=[kT], out=kT.t[:, hh, tg * 512:(tg + 1) * 512],
                                 in_=pb.t[:])
                    for tt in range(NT):
                        pb = P[tt % 2]
                        for c in range(8):
                            k.op("pe", "matmul", reads=[sv_, hT[tt // 4]], writes=[pb], inc=(c == 7),
                                 out=pb.t[:, 0:256], lhsT=hT_t.t[:, c, tt * 128:(tt + 1) * 128], rhs=wv[:, c, :],
                                 start=(c == 0), stop=(c == 7))
                        k.op("act", "activation", reads=[pb], writes=[V1], out=V1.t[:, tt, :, 0:128],
                             in_=pb.t[:, 0:256].rearrange("p (h e) -> p h e", h=2), func=AF.Copy)
                    for hh in range(2):
                        h = hp * 2 + hh
                        W = 4 if SLOPES[h] * 511 <= 40 else (2 if SLOPES[h] * 255 <= 70 else 1)
                        items = [(g, c, j) for g in range(4) for c in range(2) for j in range(4 * g + 4)]
                        acc = [P[2], P[3], P[4], P[5]]

                        def emit_qk(i):
                            g, c, j = items[i]
                            pr = slice(c * 64, (c + 1) * 64)
                            qb0 = max(j, 4 * g)
                            nq = 4 * g + 4 - qb0
                            pS = P[6 + (i % 2)]
                            k.op("pe", "matmul", reads=[kT, qT], writes=[pS],
                                 out=pS.t[:, 0:nq * 128], lhsT=kT.t[pr, hh, j * 128:(j + 1) * 128],
                                 rhs=qT.t[pr, hh, qb0 * 128:(qb0 + nq) * 128], start=True, stop=True)

                        def emit_rest(i):
                            g, c, j = items[i]
                            qb0 = max(j, 4 * g)
                            nq = 4 * g + 4 - qb0
                            pS = P[6 + (i % 2)]
                            p_ = pt[i % 3]
                            qi = 0
                            while qi < nq:
                                qb = qb0 + qi
                                ref = (qb // W) * W
                                n_ = min(nq - qi, ref + W - qb)
                                k.op("act", "activation", reads=[pS, c32], writes=[p_], out=p_.t[:, qi * 128:(qi + n_) * 128],
                                     in_=pS.t[:, qi * 128:(qi + n_) * 128], func=AF.Exp, bias=abias(h, ref - j))
                                qi += n_
                            if j >= 4 * g:
                                k.op("dve", "tensor_tensor", reads=[p_, cb], writes=[p_], out=p_.t[:, 0:128],
                                     in0=p_.t[:, 0:128], in1=triU, op=ALU.mult)
                            for qi in range(nq):
                                qb = qb0 + qi
                                a_ = acc[qb % 4]
                                k.op("pe", "matmul", reads=[p_, V1], writes=[a_], inc=(qi == nq - 1),
                                     out=a_.t[:, 0:129], lhsT=p_.t[:, qi * 128:(qi + 1) * 128],
                                     rhs=V1.t[:, j, hh, 0:129], start=(j == 0), stop=(j == qb))
                            if j == 4 * g + 3:
                                o_ = oh[c]
                                for a in range(4):
                                    if a % 2 == 0:
                                        k.op("act", "activation", reads=[acc[a]], writes=[o_], out=o_.t[:, a, 0:129],
                                             in_=acc[a].t[:, 0:129], func=AF.Copy)
                                    else:
                                        k.op("dve", "tensor_copy", reads=[acc[a]], writes=[o_], out=o_.t[:, a, 0:129],
                                             in_=acc[a].t[:, 0:129])
                                if c == 1:
                                    post(g)

                        def post(g):
                            for c in range(2):
                                k.op("dve", "reciprocal", reads=[oh[c]], writes=[rc], out=rc.t[:, c * 4:(c + 1) * 4],
                                     in_=oh[c].t[:, :, 128])
                            k.op("dve", "tensor_tensor", reads=[oh[0], rc], writes=[ob4], out=ob4.t[:], in0=oh[0].t[:, :, 0:128],
                                 in1=rc.t[:, 0:4].unsqueeze(2).to_broadcast([128, 4, 128]), op=ALU.mult)
                            k.op("dve", "tensor_tensor", reads=[oh[1], rc], writes=[oc4], out=oc4.t[:], in0=oh[1].t[:, :, 0:128],
                                 in1=rc.t[:, 4:8].unsqueeze(2).to_broadcast([128, 4, 128]), op=ALU.mult)
                            k.op("dve", "scalar_tensor_tensor", reads=[ob4, oc4, nlam], writes=[ob4], out=ob4.t[:], in0=oc4.t[:],
                                 scalar=nlam.t[:, 0:1], in1=ob4.t[:], op0=ALU.mult, op1=ALU.add)
                            k.op("dve", "tensor_tensor", reads=[ob4], writes=[oc4], out=oc4.t[:], in0=ob4.t[:], in1=ob4.t[:], op=ALU.mult)
                            k.op("dve", "tensor_reduce", reads=[oc4], writes=[ssq], out=ssq.t[:, 0:4], in_=oc4.t[:], axis=AX.X, op=ALU.add)
                            k.op("act", "activation", reads=[ssq, epsb], writes=[ssq], out=ssq.t[:, 4:8], in_=ssq.t[:, 0:4],
                                 func=AF.Ln, scale=1.0 / 128, bias=epsb.t[:, 0:1])
                            k.op("act", "activation", reads=[ssq], writes=[ssq], out=ssq.t[:, 4:8], in_=ssq.t[:, 4:8],
                                 func=AF.Exp, scale=-0.5)
                            k.op("dve", "tensor_tensor", reads=[ob4, ssq], writes=[ob4], out=ob4.t[:], in0=ob4.t[:],
                                 in1=ssq.t[:, 4:8].unsqueeze(2).to_broadcast([128, 4, 128]), op=ALU.mult)
                            k.op("dve", "tensor_tensor", reads=[ob4, wrow], writes=[yb4], out=yb4.t[:], in0=ob4.t[:],
                                 in1=wrow.t[:, :].unsqueeze(1).to_broadcast([128, 4, 128]), op=ALU.mult)
                            pT = P[g % 2]
                            for qi in range(4):
                                k.op("pe", "matmul", reads=[yb4, cb], writes=[pT], out=pT.t[:, qi * 128:(qi + 1) * 128], lhsT=yb4.t[:, qi, :],
                                     rhs=ident, start=True, stop=True)
                            k.op("act", "activation", reads=[pT], writes=[yd], out=yd.t[:, hh, g * 512:(g + 1) * 512],
                                 in_=pT.t[:], func=AF.Copy)

                        emit_qk(0)
                        for i in range(len(items)):
                            if i + 1 < len(items):
                                emit_qk(i + 1)
                            emit_rest(i)
                    dbg_dump(yd, yd.t, 4 + hp * 2, 2, sc)
                    out_proj(l, yd, yd.t, 4 + hp * 2, 2)

        def phase_ffn(l):
            with k.scope() as sc:
                actT = sc.sb([128, NFC, 1024], BF16, "actT")
                sg = [sc.sb([128, 512], BF16) for _ in range(2)]
                for half in range(2):
                    n = 0
                    for fb in range(NFC):
                        sl = next_slot()
                        wv = sl.t[:].rearrange("p (c e) -> p c e", c=8)
                        wload(sl, wv[:, :, 0:128], w_f1_d[l, :, fb * 128:(fb + 1) * 128].rearrange("(c p) e -> p c e", p=128))
                        wload(sl, wv[:, :, 128:256], w_f1_d[l, :, DFF + fb * 128:DFF + (fb + 1) * 128].rearrange("(c p) e -> p c e", p=128))
                        for t2 in range(2):
                            tg = half * 2 + t2
                            pg, pu = P[(n * 2) % 4], P[(n * 2 + 1) % 4]
                            s_ = sg[n % 2]
                            n += 1
                            proj_fm(pg, sl, wv, 0, 128, tg)
                            proj_fm(pu, sl, wv, 128, 128, tg)
                            k.op("act", "activation", reads=[pg], writes=[s_], out=s_.t[:], in_=pg.t[:], func=AF.Silu)
                            k.op("dve", "tensor_tensor", reads=[pu, s_], writes=[actT], out=actT.t[:, fb, t2 * 512:(t2 + 1) * 512],
                                 in0=pu.t[:], in1=s_.t[:], op=ALU.mult)
                    for dg_ in range(4):
                        banks = [P[4 + i] for i in range(4)]
                        for kb in range(3):
                            nk = min(8, NFC - kb * 8)
                            sl = next_slot()
                            wv = sl.t[:].rearrange("p (c e) -> p c e", c=8)
                            wload(sl, wv[:, 0:nk, :], w_f2_d[l, kb * 1024:kb * 1024 + nk * 128, dg_ * 256:(dg_ + 1) * 256]
                                  .rearrange("(c p) e -> p c e", p=128))
                            for ci in range(nk):
                                fc = kb * 8 + ci
                                for dc2 in range(2):
                                    for t2 in range(2):
                                        pb = banks[dc2 * 2 + t2]
                                        k.op("pe", "matmul", reads=[sl, actT], writes=[pb], inc=(fc == NFC - 1 or (ci == nk - 1 and dc2 == 1 and t2 == 1)),
                                             out=pb.t[:], lhsT=wv[:, ci, dc2 * 128:(dc2 + 1) * 128],
                                             rhs=actT.t[:, fc, t2 * 512:(t2 + 1) * 512], start=(fc == 0), stop=(fc == NFC - 1))
                        for dc2 in range(2):
                            for t2 in range(2):
                                pb = banks[dc2 * 2 + t2]
                                tg = half * 2 + t2
                                dc = dg_ * 2 + dc2
                                ts = slice(tg * 512, (tg + 1) * 512)
                                k.op("dve", "tensor_tensor", reads=[pb, xT[tg]], writes=[xT[tg]],
                                     out=xT_t.t[:, dc, ts], in0=xT_t.t[:, dc, ts], in1=pb.t[:], op=ALU.add)

        def phase_gdn(l):
            with k.scope() as sc:
                s_q, s_k, s_v, s_z = next_slot(), next_slot(), next_slot(), next_slot()
                wviews = []
                for s_, c0 in ((s_q, 512), (s_k, 768), (s_v, 1024), (s_z, 1280)):
                    v_ = s_.t[:].rearrange("p (c e) -> p c e", c=8)
                    wload(s_, v_, w_in_cols(l, c0, 256))
                    wviews.append(v_)
                wq, wk, wv, wz = wviews
                k.load("pool", wab, wab.t[:], w_in_cols(l, 1536, 8))
                abp = P[0]
                for tt in range(NT):
                    for c in range(8):
                        k.op("pe", "matmul", reads=[wab, hT[tt // 4]], writes=[abp], inc=(c == 7),
                             out=abp.t[:, tt * 8:(tt + 1) * 8], lhsT=hT_t.t[:, c, tt * 128:(tt + 1) * 128], rhs=wab.t[:, c, :],
                             start=(c == 0), stop=(c == 7))
                abv = abp.t[:, 0:128].rearrange("p (t e) -> p t e", e=8)
                gt = sc.sb([128, NT, 4], F32, "gt")
                bt = sc.sb([128, NT, 4], F32, "bt")
                nbt = sc.sb([128, NT, 4], F32, "nbt")
                gc = sc.sb([128, NT, 4], F32, "gc")
                ed = sc.sb([128, NT, 4], F32, "ed")
                bew = sc.sb([128, NT, 4], F32, "bew")
                na = sc.sb([128, 4], F32, "na")
                for h in range(4):
                    k.op("dve", "tensor_scalar", reads=[abp, prow], writes=[gt], out=gt.t[:, :, h], in0=abv[:, :, h],
                         scalar1=prow.t[:, PR_DTB + h:PR_DTB + h + 1], scalar2=None, op0=ALU.add)
                k.op("act", "activation", reads=[gt], writes=[gt], out=gt.t[:], in_=gt.t[:], func=AF.Exp)
                k.op("act", "activation", reads=[gt, epsb], writes=[gt], out=gt.t[:], in_=gt.t[:], func=AF.Ln, bias=epsb.t[:, 2:3])
                k.op("act", "activation", reads=[prow], writes=[na], out=na.t[:], in_=prow.t[:, PR_ALOG:PR_ALOG + 4], func=AF.Exp)
                for h in range(4):
                    k.op("dve", "tensor_scalar", reads=[gt, na], writes=[gt], out=gt.t[:, :, h], in0=gt.t[:, :, h],
                         scalar1=na.t[:, h:h + 1], scalar2=-1.0, op0=ALU.mult, op1=ALU.mult)
                k.op("act", "activation", reads=[abp], writes=[bt], out=bt.t[:], in_=abv[:, :, 4:8], func=AF.Sigmoid)
                k.op("dve", "tensor_scalar", reads=[bt], writes=[nbt], out=nbt.t[:], in0=bt.t[:], scalar1=-1.0, scalar2=None, op0=ALU.mult)
                gflat = gt.t[:].rearrange("p t h -> p (t h)")
                pcs = P[1]
                k.op("pe", "matmul", reads=[gt, c32], writes=[pcs], out=pcs.t[:, 0:64], lhsT=triT, rhs=gflat, start=True, stop=True)
                k.op("pe", "matmul", reads=[gt, c32], writes=[pcs], out=pcs.t[:, 64:128], lhsT=ones32, rhs=gflat, start=True, stop=True)
                gcf = gc.t[:].rearrange("p t h -> p (t h)")
                edf = ed.t[:].rearrange("p t h -> p (t h)")
                bewf = bew.t[:].rearrange("p t h -> p (t h)")
                k.op("act", "activation", reads=[pcs], writes=[gc], out=gcf, in_=pcs.t[:, 0:64], func=AF.Copy)
                k.op("dve", "tensor_tensor", reads=[pcs, gc], writes=[ed], out=edf, in0=pcs.t[:, 64:128], in1=gcf, op=ALU.subtract)
                k.op("act", "activation", reads=[ed], writes=[ed], out=edf, in_=edf, func=AF.Exp)
                k.op("act", "activation", reads=[gc], writes=[bew], out=bewf, in_=gcf, func=AF.Exp)
                k.op("dve", "tensor_tensor", reads=[bew, bt], writes=[bew], out=bewf, in0=bewf, in1=bt.t[:].rearrange("p t h -> p (t h)"), op=ALU.mult)

                print("ops at stage1:", k.nops)
                if GSTOP <= 1:
                    return
                qT = sc.sb([128, 2, T], BF16, "gq")
                kT = sc.sb([128, 2, T], BF16, "gk")
                vT = sc.sb([128, 2, T], BF16, "gv")
                yg = sc.sb([128, 2, T], BF16, "yg")
                with k.scope() as sc2:
                    ut = [sc2.sb([128, 4 + T], BF16, "ut%d" % i) for i in range(2)]
                    dgs = [sc2.sb([128, 128], BF16) for _ in range(4)]
                    sl32 = [sc2.sb([128, 512], F32) for _ in range(2)]
                    sqb = [sc2.sb([128, 512], BF16) for _ in range(2)]
                    rn = [sc2.sb([128, 512], F32) for _ in range(2)]
                    for u_ in ut:
                        k.op("dve", "memset", writes=[u_], ap=u_.t[:, 0:4], constant=0.0)
                    n = 0
                    for kind, (slot_, wv_, dst) in enumerate(((s_q, wq, qT), (s_k, wk, kT), (s_v, wv, vT))):
                        for hp in range(2):
                            ch = kind * 2 + hp
                            u_ = ut[(kind * 2 + hp) % 2]
                            for tg in range(4):
                                pb = P[tg % 2]
                                proj_fm(pb, slot_, wv_, hp * 128, 128, tg)
                                k.op("act", "activation", reads=[pb], writes=[u_], out=u_.t[:, 4 + tg * 512:4 + (tg + 1) * 512],
                                     in_=pb.t[:], func=AF.Copy)
                            for tap in range(4):
                                k.op("dve", "tensor_scalar", reads=[cb, pcol], writes=[dgs[tap]], out=dgs[tap].t[:], in0=ident,
                                     scalar1=pcol.t[:, PC_GW + ch * 4 + tap:PC_GW + ch * 4 + tap + 1], scalar2=None, op0=ALU.mult)
                            for tg in range(4):
                                pb = P[4 + tg]
                                ts = slice(tg * 512, (tg + 1) * 512)
                                for tap in range(4):
                                    k.op("pe", "matmul", reads=[dgs[tap], u_], writes=[pb], inc=(tap == 3), out=pb.t[:], lhsT=dgs[tap].t[:],
                                         rhs=u_.t[:, 1 + tap + tg * 512:1 + tap + (tg + 1) * 512], start=(tap == 0), stop=(tap == 3))
                                if kind == 2:
                                    k.op("act", "activation", reads=[pb], writes=[dst], out=dst.t[:, hp, ts], in_=pb.t[:], func=AF.Silu)
                                else:
                                    s32, sq_, rn_ = sl32[n % 2], sqb[n % 2], rn[n % 2]
                                    pn = P[2 + n % 2]
                                    n += 1
                                    k.op("act", "activation", reads=[pb], writes=[s32], out=s32.t[:], in_=pb.t[:], func=AF.Silu)
                                    k.op("act", "activation", reads=[s32], writes=[sq_], out=sq_.t[:], in_=s32.t[:], func=AF.Square)
                                    k.op("pe", "matmul", reads=[sq_, cb], writes=[pn], out=pn.t[:], lhsT=blk64, rhs=sq_.t[:], start=True, stop=True)
                                    k.op("act", "activation", reads=[pn, epsb], writes=[rn_], out=rn_.t[:], in_=pn.t[:], func=AF.Sqrt, bias=epsb.t[:, 0:1])
                                    k.op("dve", "reciprocal", reads=[rn_], writes=[rn_], out=rn_.t[:], in_=rn_.t[:])
                                    k.op("dve", "scalar_tensor_tensor", reads=[s32, rn_], writes=[dst], out=dst.t[:, hp, ts], in0=s32.t[:],
                                         scalar=(0.125 if kind == 0 else 1.0), in1=rn_.t[:], op0=ALU.mult, op1=ALU.mult)

                print("ops at stage2:", k.nops)
                if GSTOP <= 2:
                    return
                S = sc.sb([128, 2, 128], F32, "S")
                k.op("dve", "memset", writes=[S], ap=S.t[:], constant=0.0)
                Gb = sc.sb([128, 4, 128], F32, "Gb")
                Gm = sc.sb([128, 4, 128], F32, "Gm")
                Gm2 = sc.sb([128, 4, 128], F32, "Gm2")
                Dc = sc.sb([128, 4, 128], F32, "Dc")
                DT = sc.sb([128, 4, 128], BF16, "DT")
                Eg = sc.sb([128, 4, 128], F32, "Eg")
                Nb = [sc.sb([128, 4, 128], F32, "Nb%d" % i) for i in range(2)]
                Mb = [sc.sb([128, 4, 128], F32, "Mb%d" % i) for i in range(2)]
                Y = sc.sb([128, 4, 128], F32, "Y")
                Ng = [[Nb[i].view("Nb%dg%d" % (i, gi)) for gi in range(2)] for i in range(2)]
                Mg = [[Mb[i].view("Mb%dg%d" % (i, gi)) for gi in range(2)] for i in range(2)]
                Yg = [Y.view("Yg%d" % gi) for gi in range(2)]
                QKT = sc.sb([128, 4, 128], BF16, "QKT")
                U = sc.sb([128, 256], F32, "U")
                WT = sc.sb([128, 2, 128], F32, "WT")
                qe = sc.sb([128, 2, 128], F32, "qe")
                kd = sc.sb([128, 256], BF16, "kd")
                rhu = sc.sb([128, 256], F32, "rhu")
                rhw = sc.sb([128, 256], F32, "rhw")
                vnew = sc.sb([128, 256], BF16, "vnew")
                ob = sc.sb([128, 4, 64], F32, "ob")
                osq = sc.sb([128, 4, 64], F32, "osq")
                ors = sc.sb([128, 8], F32, "ors")
                zs = sc.sb([128, 256], F32, "zs")
                ytok = sc.sb([128, 256], BF16, "ytok")
                kc2 = sc.sb([128, 4, 128], BF16, "kc2")
                kc2v = kc2.t[:].rearrange("p (a b) e -> p a b e", a=2)
                k.op("dve", "memset", writes=[kc2], ap=kc2.t[:], constant=0.0)
                gnw = prow.t[:, PR_GNW:PR_GNW + 64]
                for n in range(NT):
                    cs = slice(n * 128, (n + 1) * 128)
                    tgi = n // 4
                    PA, PB, PT_, PN, PM, PY, PK, PX = P
                    for hp in range(2):
                        k.op("pe", "matmul", reads=[kT, cb], writes=[PK], out=PK.t[:, hp * 128:(hp + 1) * 128], lhsT=kT.t[:, hp, cs], rhs=ident,
                             start=True, stop=True)
                        k.op("pe", "matmul", reads=[vT, cb], writes=[PK], out=PK.t[:, 256 + hp * 128:256 + (hp + 1) * 128], lhsT=vT.t[:, hp, cs],
                             rhs=ident, start=True, stop=True)
                    for h in range(4):
                        hs = slice(h * 64, (h + 1) * 64)
                        k.op("dve", "tensor_scalar", reads=[PK, bew], writes=[rhw], out=rhw.t[:, hs], in0=PK.t[:, hs],
                             scalar1=bew.t[:, n, h:h + 1], scalar2=None, op0=ALU.mult)
                        k.op("dve", "tensor_scalar", reads=[PK, ed], writes=[kd], out=kd.t[:, hs], in0=PK.t[:, hs],
                             scalar1=ed.t[:, n, h:h + 1], scalar2=None, op0=ALU.mult)
                        k.op("dve", "tensor_scalar", reads=[PK, bt], writes=[rhu], out=rhu.t[:, hs], in0=PK.t[:, 256 + h * 64:256 + (h + 1) * 64],
                             scalar1=bt.t[:, n, h:h + 1], scalar2=None, op0=ALU.mult)
                    if n == 0: print('ops at stage3:', k.nops)
                    if GSTOP <= 3:
                        return
                    for h in range(4):
                        k.op("dve", "tensor_scalar", reads=[c32, gt], writes=[Gb], out=Gb.t[:, h, :], in0=ones32,
                             scalar1=gt.t[:, n, h:h + 1], scalar2=-1.0, op0=ALU.mult, op1=ALU.mult)
                        k.op("pe", "matmul", reads=[Gb, c32], writes=[PA], out=PA.t[:, h * 128:(h + 1) * 128], lhsT=Gb.t[:, h, :], rhs=triT,
                             start=True, stop=True)
                    PA3 = PA.t[:].rearrange("p (h e) -> p h e", h=4)
                    for h in range(4):
                        k.op("dve", "scalar_tensor_tensor", reads=[PA, gc, c32], writes=[Gm], out=Gm.t[:, h, :], in0=PA3[:, h, :],
                             scalar=gc.t[:, n, h:h + 1], in1=maskC, op0=ALU.add, op1=ALU.add)
                        k.op("dve", "scalar_tensor_tensor", reads=[PA, gc, c32], writes=[Gm2], out=Gm2.t[:, h, :], in0=PA3[:, h, :],
                             scalar=gc.t[:, n, h:h + 1], in1=maskU, op0=ALU.add, op1=ALU.subtract)
                    k.op("act", "activation", reads=[Gm], writes=[Dc], out=Dc.t[:], in_=Gm.t[:], func=AF.Exp)
                    k.op("act", "activation", reads=[Gm2], writes=[DT], out=DT.t[:], in_=Gm2.t[:], func=AF.Exp, scale=-1.0)
                    k.op("act", "activation", reads=[PA], writes=[Eg], out=Eg.t[:], in_=PA3, func=AF.Exp, scale=-1.0)
                    if n == 0: print('ops at stage4:', k.nops)
                    if GSTOP <= 4:
                        return
                    N_, M_ = Nb[0], Mb[0]
                    PB3 = PB.t[:].rearrange("p (h e) -> p h e", h=4)
                    for h in range(4):
                        pr = slice((h % 2) * 64, (h % 2 + 1) * 64)
                        if h == 0:
                            k.op("act", "activation", reads=[kT], writes=[kc2], out=kc2v[0:64, :, 0, :], in_=kT.t[0:64, :, cs], func=AF.Copy)
                            k.op("act", "activation", reads=[kT], writes=[kc2], out=kc2v[64:128, :, 1, :], in_=kT.t[64:128, :, cs], func=AF.Copy)
                        k.op("pe", "matmul", reads=[kT, kc2], writes=[PB], out=PB.t[:, h * 128:(h + 1) * 128], lhsT=kT.t[:, h // 2, cs], rhs=kc2.t[:, h, :],
                             start=True, stop=True)
                    for h in range(4):
                        k.op("dve", "scalar_tensor_tensor", reads=[PB, nbt, Dc], writes=[N_, Ng[0][h // 2]], out=N_.t[:, h, :], in0=PB3[:, h, :],
                             scalar=nbt.t[:, n, h:h + 1], in1=Dc.t[:, h, :], op0=ALU.mult, op1=ALU.mult)
                    PT3 = PT_.t[:].rearrange("p (h e) -> p h e", h=4)
                    for h in range(4):
                        k.op("pe", "matmul", reads=[N_, c32], writes=[PT_], out=PT_.t[:, h * 128:(h + 1) * 128], lhsT=N_.t[:, h, :], rhs=ident32, start=True, stop=True)
                    k.op("act", "activation", reads=[PT_], writes=[M_, Mg[0][0], Mg[0][1]], out=M_.t[:], in_=PT3, func=AF.Copy)
                    for h in range(4):
                        k.op("dve", "tensor_tensor", reads=[PT_, c32], writes=[Y, Yg[h // 2]], out=Y.t[:, h, :], in0=PT3[:, h, :], in1=ident32, op=ALU.add)
                    if n == 0: print('ops at stage5:', k.nops)
                    if GSTOP <= 5:
                        return
                    PN3 = PN.t[:].rearrange("p (h e) -> p h e", h=4)
                    PM3 = PM.t[:].rearrange("p (h e) -> p h e", h=4)
                    PY3 = PY.t[:].rearrange("p (h e) -> p h e", h=4)
                    cur = 0
                    GB = ((PN, PM, PY), (PA, PB, PT_))
                    for lev in range(6):
                        Nc, Mc = Nb[cur], Mb[cur]
                        Nn, Mn = Nb[1 - cur], Mb[1 - cur]
                        for gi in range(2):
                            pn = GB[gi][0]
                            for h2 in range(2):
                                h = gi * 2 + h2
                                k.op("pe", "matmul", reads=[Mg[cur][gi], Ng[cur][gi]], writes=[pn], inc=(h2 == 1),
                                     out=pn.t[:, h2 * 128:(h2 + 1) * 128], lhsT=Mc.t[:, h, :], rhs=Nc.t[:, h, :], start=True, stop=True)
                        for gi in range(2):
                            pn = GB[gi][0]
                            k.op("act", "activation", reads=[pn], writes=[Ng[1 - cur][gi]], out=Nn.t[:, gi * 2:gi * 2 + 2, :],
                                 in_=pn.t[:, 0:256].rearrange("p (h e) -> p h e", h=2), func=AF.Copy)
                        if lev < 5:
                            for gi in range(2):
                                pm = GB[gi][1]
                                for h2 in range(2):
                                    h = gi * 2 + h2
                                    k.op("pe", "matmul", reads=[Mg[cur][gi], Ng[cur][gi]], writes=[pm], inc=(h2 == 1),
                                         out=pm.t[:, h2 * 128:(h2 + 1) * 128], lhsT=Nc.t[:, h, :], rhs=Mc.t[:, h, :], start=True, stop=True)
                            for gi in range(2):
                                pm = GB[gi][1]
                                k.op("dve", "tensor_copy", reads=[pm], writes=[Mg[1 - cur][gi]], out=Mn.t[:, gi * 2:gi * 2 + 2, :],
                                     in_=pm.t[:, 0:256].rearrange("p (h e) -> p h e", h=2))
                        for gi in range(2):
                            py = GB[gi][2]
                            for h2 in range(2):
                                h = gi * 2 + h2
                                k.op("pe", "matmul", reads=[Ng[1 - cur][gi], Yg[gi]], writes=[py], inc=(h2 == 1),
                                     out=py.t[:, h2 * 128:(h2 + 1) * 128], lhsT=Nn.t[:, h, :], rhs=Y.t[:, h, :], start=True, stop=True)
                        for gi in range(2):
                            py = GB[gi][2]
                            k.op("dve", "tensor_tensor", reads=[py, Yg[gi]], writes=[Yg[gi]], out=Y.t[:, gi * 2:gi * 2 + 2, :],
                                 in0=Y.t[:, gi * 2:gi * 2 + 2, :], in1=py.t[:, 0:256].rearrange("p (h e) -> p h e", h=2), op=ALU.add)
                        cur = 1 - cur
                    k.op("dve", "tensor_copy", reads=[Yg[0], Yg[1]], writes=[Y], out=Y.t[:, 0, 0:1], in_=Y.t[:, 0, 0:1])
                    for h in range(4):
                        hs = slice(h * 64, (h + 1) * 64)
                        k.op("pe", "matmul", reads=[Y, rhu], writes=[PX], out=PX.t[:, hs], lhsT=Y.t[:, h, :], rhs=rhu.t[:, hs], start=True, stop=True)
                    k.op("act", "activation", reads=[PX], writes=[U], out=U.t[:], in_=PX.t[:, 0:256], func=AF.Copy)
                    for h in range(4):
                        hp = h // 2
                        k.op("pe", "matmul", reads=[Y, rhw], writes=[PB], out=PB.t[:, h * 128:(h + 1) * 128], lhsT=rhw.t[:, hp * 128:(hp + 1) * 128], rhs=Y.t[:, h, :],
                             start=True, stop=True)
                    for h in range(4):
                        pr = slice((h % 2) * 64, (h % 2 + 1) * 64)
                        if h % 2 == 0:
                            k.op("act", "activation", reads=[PB], writes=[WT], out=WT.t[pr, h // 2, :], in_=PB3[pr, h, :], func=AF.Copy)
                        else:
                            k.op("dve", "tensor_copy", reads=[PB], writes=[WT], out=WT.t[pr, h // 2, :], in_=PB3[pr, h, :])
                    if n == 0: print('ops at stage7:', k.nops)
                    if GSTOP <= 7:
                        return
                    for h in range(4):
                        pr = slice((h % 2) * 64, (h % 2 + 1) * 64)
                        k.op("pe", "matmul", reads=[kc2, qT], writes=[PT_], out=PT_.t[:, h * 128:(h + 1) * 128], lhsT=kc2.t[:, h, :], rhs=qT.t[:, h // 2, cs],
                             start=True, stop=True)
                    k.op("dve", "tensor_tensor", reads=[PT_, DT], writes=[QKT], out=QKT.t[:], in0=PT3, in1=DT.t[:], op=ALU.mult)
                    for h in range(4):
                        pr = slice((h % 2) * 64, (h % 2 + 1) * 64)
                        k.op("dve", "tensor_tensor", reads=[qT, Eg], writes=[qe], out=qe.t[pr, h // 2, :], in0=qT.t[pr, h // 2, cs],
                             in1=Eg.t[pr, h, :], op=ALU.mult)
                    if n == 0: print('ops at stage8:', k.nops)
                    if GSTOP <= 8:
                        return
                    for hp in range(2):
                        k.op("pe", "matmul", reads=[WT, S], writes=[PX], out=PX.t[:, 256 + hp * 128:256 + (hp + 1) * 128], lhsT=WT.t[:, hp, :],
                             rhs=S.t[:, hp, :], start=True, stop=True)
                    k.op("dve", "tensor_tensor", reads=[U, PX], writes=[vnew], out=vnew.t[:], in0=U.t[:], in1=PX.t[:, 256:512], op=ALU.subtract)
                    for h in range(4):
                        hs = slice(h * 64, (h + 1) * 64)
                        k.op("pe", "matmul", reads=[qe, S], writes=[PN], out=PN.t[:, hs], lhsT=qe.t[:, h // 2, :],
                             rhs=S.t[:, h // 2, (h % 2) * 64:(h % 2 + 1) * 64], start=True, stop=False)
                        k.op("pe", "matmul", reads=[QKT, vnew], writes=[PN], out=PN.t[:, hs], lhsT=QKT.t[:, h, :], rhs=vnew.t[:, hs],
                             start=False, stop=True)
                    for hp in range(2):
                        k.op("pe", "matmul", reads=[kd, vnew], writes=[PM], out=PM.t[:, hp * 128:(hp + 1) * 128], lhsT=kd.t[:, hp * 128:(hp + 1) * 128],
                             rhs=vnew.t[:, hp * 128:(hp + 1) * 128], start=True, stop=True)
                    for h in range(4):
                        pr = slice((h % 2) * 64, (h % 2 + 1) * 64)
                        cs2 = slice((h % 2) * 64, (h % 2 + 1) * 64)
                        k.op("dve", "scalar_tensor_tensor", reads=[S, Eg, PM], writes=[S], out=S.t[pr, h // 2, cs2], in0=S.t[pr, h // 2, cs2],
                             scalar=Eg.t[pr, h, 127:128], in1=PM.t[pr, (h // 2) * 128 + (h % 2) * 64:(h // 2) * 128 + (h % 2 + 1) * 64],
                             op0=ALU.mult, op1=ALU.add)
                    if n == 0: print('ops at stage9:', k.nops)
                    if GSTOP <= 9:
                        return
                    k.op("act", "activation", reads=[PN], writes=[ob], out=ob.t[:].rearrange("p h e -> p (h e)"), in_=PN.t[:, 0:256], func=AF.Copy)
                    k.op("dve", "tensor_tensor", reads=[ob], writes=[osq], out=osq.t[:], in0=ob.t[:], in1=ob.t[:], op=ALU.mult)
                    k.op("dve", "tensor_reduce", reads=[osq], writes=[ors], out=ors.t[:, 0:4], in_=osq.t[:], axis=AX.X, op=ALU.add)
                    k.op("act", "activation", reads=[ors, epsb], writes=[ors], out=ors.t[:, 4:8], in_=ors.t[:, 0:4], func=AF.Ln, scale=1.0 / 64,
                         bias=epsb.t[:, 0:1])
                    k.op("act", "activation", reads=[ors], writes=[ors], out=ors.t[:, 4:8], in_=ors.t[:, 4:8], func=AF.Exp, scale=-0.5)
                    for c in range(8):
                        k.op("pe", "matmul", reads=[s_z, hT[tgi]], writes=[PY], inc=(c == 7), out=PY.t[:, 0:256], lhsT=hT_t.t[:, c, cs], rhs=wz[:, c, :],
                             start=(c == 0), stop=(c == 7))
                    k.op("act", "activation", reads=[PY], writes=[zs], out=zs.t[:], in_=PY.t[:, 0:256], func=AF.Silu)
                    for h in range(4):
                        k.op("dve", "scalar_tensor_tensor", reads=[ob, ors, prow], writes=[osq], out=osq.t[:, h, :], in0=ob.t[:, h, :],
                             scalar=ors.t[:, 4 + h:5 + h], in1=gnw, op0=ALU.mult, op1=ALU.mult)
                    k.op("dve", "tensor_tensor", reads=[osq, zs], writes=[ytok], out=ytok.t[:], in0=osq.t[:].rearrange("p h e -> p (h e)"),
                         in1=zs.t[:], op=ALU.mult)
                    for hp in range(2):
                        k.op("pe", "matmul", reads=[ytok, cb], writes=[PK], out=PK.t[:, hp * 128:(hp + 1) * 128], lhsT=ytok.t[:, hp * 128:(hp + 1) * 128],
                             rhs=ident, start=True, stop=True)
                    k.op("act", "activation", reads=[PK], writes=[yg], out=yg.t[:, :, cs], in_=PK.t[:, 0:256].rearrange("p (a e) -> p a e", a=2),
                         func=AF.Copy)
                dbg_dump(yg, yg.t, 2, 2, sc)
                out_proj(l, yg, yg.t, 2, 2)

        for s in range(nseq):
            for g in range(4):
                k.load("sp", xT[g], xT_t.t[:, :, g * 512:(g + 1) * 512],
                       xT_d[s, :, g * 512:(g + 1) * 512].rearrange("(c p) t -> p c t", p=128))
            for l in range(depth):
                lam_init = 0.8 - 0.6 * math.exp(-0.3 * l)
                k.load("sp", pcol, pcol.t[:], pcol_d[l])
                k.load("sp", prow, prow.t[:], prow_d[l])
                phase_norm(pcol.t[:, PC_G1:PC_G1 + 8])
                if "conv" in mixers:
                    phase_conv(l)
                if "gdn" in mixers:
                    phase_gdn(l)
                if "attn" in mixers:
                    phase_attn(l, lam_init)
                if do_ffn:
                    phase_norm(pcol.t[:, PC_G2:PC_G2 + 8])
                    phase_ffn(l)
            k.load("sp", pcol, pcol.t[:], pcol_d[DEPTH])
            phase_norm(pcol.t[:, PC_G1:PC_G1 + 8], final=True, s=s)
        print("bass instructions:", k.ninst, "dma sems:", k.ndsem)
    return nc


def make_consts():
    i = np.arange(128)
    c32 = np.zeros((128, 4 * 128 + 4 * 128 + 4 * NR), np.float32)
    c32[:, 0:128] = np.where(i[:, None] > i[None, :], 0.0, NEG)
    c32[:, 128:256] = np.where(i[None, :] >= i[:, None], 0.0, NEG)
    c32[:, 256:384] = (i[:, None] <= i[None, :]).astype(np.float32)
    c32[:, 384:512] = 1.0
    c32[:, 512:640] = np.eye(128)
    for h in range(4):
        for r in range(-3, 16):
            c32[:, 1024 + h * NR + (r + 3)] = SLOPES[h] * (i - 128.0 * r)
    cb = np.zeros((128, 5 * 128), np.float32)
    cb[:, 0:128] = np.eye(128)
    cb[:, 128:256] = 1.0
    cb[0:64, 256:320] = 1.0
    cb[64:128, 320:384] = 1.0
    cb[:, 384:512] = (i[:, None] > i[None, :]).astype(np.float32)
    cb[:, 512:640] = (i[None, :] >= i[:, None]).astype(np.float32)
    return c32, cb


def make_params(inp):
    f = np.float32
    pcol = np.zeros((DEPTH + 1, 128, NPC), f)
    prow = np.zeros((DEPTH, 128, NPR), f)
    for l in range(DEPTH):
        pcol[l, :, PC_G1:PC_G1 + 8] = inp["norm1_g"][l].reshape(8, 128).T
        pcol[l, :, PC_G2:PC_G2 + 8] = inp["norm2_g"][l].reshape(8, 128).T
        pcol[l, :, PC_CB:PC_CB + 2] = inp["conv_dw_b"][l].reshape(2, 128).T
        pcol[l, :, PC_LG:PC_LG + 2] = inp["conv_ln_g"][l].reshape(2, 128).T
        pcol[l, :, PC_LB:PC_LB + 2] = inp["conv_ln_b"][l].reshape(2, 128).T
        pcol[l, :, PC_DW:PC_DW + 62] = inp["conv_dw_w"][l].reshape(31, 2, 128).transpose(2, 1, 0).reshape(128, 62)
        pcol[l, :, PC_GW:PC_GW + 24] = inp["gdn_conv_w"][l].reshape(4, 6, 128).transpose(2, 1, 0).reshape(128, 24)
        prow[l, :, PR_GNW:PR_GNW + 64] = inp["gdn_norm_w"][l][None, :]
        prow[l, :, PR_SUB:PR_SUB + 128] = inp["diff_subln_w"][l][None, :]
        prow[l, :, PR_LAM:PR_LAM + 256] = inp["diff_lambda"][l].reshape(1, 256)
        prow[l, :, PR_ALOG:PR_ALOG + 4] = inp["gdn_a_log"][l][None, :]
        prow[l, :, PR_DTB:PR_DTB + 4] = inp["gdn_dt_bias"][l][None, :]
    pcol[DEPTH, :, PC_G1:PC_G1 + 8] = inp["final_norm_g"].reshape(8, 128).T
    return pcol, prow


def kernel(**inp):
    ncores = 8
    x = np.asarray(inp["x"], np.float32)
    B = x.shape[0]
    nseq = B // ncores
    c32, cb = make_consts()
    pcol, prow = make_params({k_: np.asarray(v, np.float32) for k_, v in inp.items()})
    xT = np.ascontiguousarray(x.transpose(0, 2, 1))
    nc = build_program(nseq=nseq)
    shared = {
        "w_in": np.ascontiguousarray(inp["w_in"], dtype=np.float32),
        "w_out": np.ascontiguousarray(inp["w_out"], dtype=np.float32),
        "w_ffn_in": np.ascontiguousarray(inp["w_ffn_in"], dtype=np.float32),
        "w_ffn_out": np.ascontiguousarray(inp["w_ffn_out"], dtype=np.float32),
        "pcol": pcol, "prow": prow, "c32": c32, "cb": cb,
    }
    in_maps = [dict(shared, xT=xT[c * nseq:(c + 1) * nseq]) for c in range(ncores)]
    res = run_bass_kernel_spmd(nc, in_maps, core_ids=list(range(ncores)))
    outT = np.concatenate([r["outT"] for r in res.results], axis=0)
    return np.ascontiguousarray(outT.transpose(0, 2, 1)).astype(np.float32)
```
